# Optimizing a Trainium2 kernel written in Bass

```python
import jax, jax.numpy as jnp
from jax import lax
import numpy as np

D_MODEL = 1024
BATCH = 16
SEQ = 2048
DEPTH = 2

N_META = 16
HEAD_DIM = 64
N_HEADS = (D_MODEL // 2) // HEAD_DIM
N_KV_HEADS = N_HEADS // 4
Q_GROUP = N_HEADS // N_KV_HEADS
ATTN_WIDTH = N_HEADS * HEAD_DIM
KV_WIDTH = N_KV_HEADS * HEAD_DIM
FOURIER_WIDTH = D_MODEL - ATTN_WIDTH
N_FOURIER_GROUPS = 4
FOURIER_GROUP = FOURIER_WIDTH // N_FOURIER_GROUPS
N_BRANCHES = 2
IN_WIDTH = ATTN_WIDTH + 2 * KV_WIDTH + FOURIER_WIDTH + N_BRANCHES * D_MODEL
WINDOW = 128
BLOCK = 128
ROPE_THETA = 10000.0
D_FF = -(-8 * D_MODEL // (3 * 256)) * 256
EPS = 1e-6

kernel_name = "hybrid_fourier_swa_gated_encoder"


def _rmsnorm(x, gain):
    x32 = x.astype(jnp.float32)
    y = x32 * lax.rsqrt(jnp.mean(x32 * x32, axis=-1, keepdims=True) + EPS)
    return (y * gain.astype(jnp.float32)).astype(x.dtype)


def _rope_tables(length):
    inv_freq = ROPE_THETA ** (-jnp.arange(0, HEAD_DIM, 2, dtype=jnp.float32) / HEAD_DIM)
    ang = jnp.arange(length, dtype=jnp.float32)[:, None] * inv_freq[None, :]
    return jnp.cos(ang)[:, None, :], jnp.sin(ang)[:, None, :]


def _rope(x, cos, sin):
    x32 = x.astype(jnp.float32)
    x1, x2 = jnp.split(x32, 2, axis=-1)
    out = jnp.concatenate([x1 * cos - x2 * sin, x2 * cos + x1 * sin], axis=-1)
    return out.astype(x.dtype)


def _sink_attend(q, k, v, mask, sink):
    scale = HEAD_DIM ** -0.5
    s = jnp.einsum('bnqkgd,bnskd->bnkgqs', q, k).astype(jnp.float32) * scale
    s = jnp.where(mask[None, :, None, None], s, -jnp.inf)
    sk = sink.astype(jnp.float32)[None, None, :, :, None, None]
    m = jnp.maximum(jnp.max(s, axis=-1, keepdims=True), sk)
    p = jnp.exp(s - m)
    p = p / (jnp.sum(p, axis=-1, keepdims=True) + jnp.exp(sk - m))
    return jnp.einsum('bnkgqs,bnskd->bnqkgd', p.astype(v.dtype), v)


def _windowed_gqa(q, k, v, sink):
    B, L = q.shape[0], q.shape[1]
    S = L - N_META
    nb = S // BLOCK
    sink = sink.reshape(N_KV_HEADS, Q_GROUP)
    qm, qr = q[:, :N_META], q[:, N_META:]
    km, kr = k[:, :N_META], k[:, N_META:]
    vm, vr = v[:, :N_META], v[:, N_META:]

    qb = qr.reshape(B, nb, BLOCK, N_KV_HEADS, Q_GROUP, HEAD_DIM)
    pad = ((0, 0), (BLOCK, BLOCK), (0, 0), (0, 0))

    def band(t):
        tp = jnp.pad(t, pad).reshape(B, nb + 2, BLOCK, N_KV_HEADS, HEAD_DIM)
        return jnp.concatenate([tp[:, :-2], tp[:, 1:-1], tp[:, 2:]], axis=2)

    def with_meta(meta, win):
        meta_b = jnp.broadcast_to(meta[:, None], (B, nb, N_META, N_KV_HEADS, HEAD_DIM))
        return jnp.concatenate([meta_b, win], axis=2)

    k_all = with_meta(km, band(kr))
    v_all = with_meta(vm, band(vr))
    s_idx = jnp.arange(BLOCK)[:, None]
    t_idx = jnp.arange(3 * BLOCK)[None, :]
    rel = t_idx - BLOCK - s_idx
    key_pos = jnp.arange(nb)[:, None, None] * BLOCK - BLOCK + t_idx[None]
    wmask = (jnp.abs(rel) <= WINDOW)[None] & (key_pos >= 0) & (key_pos < S)
    mask = jnp.concatenate([jnp.ones((nb, BLOCK, N_META), bool), wmask], axis=-1)
    out_r = _sink_attend(qb, k_all, v_all, mask, sink).reshape(B, S, ATTN_WIDTH)

    qmb = qm.reshape(B, 1, N_META, N_KV_HEADS, Q_GROUP, HEAD_DIM)
    km_q = jnp.concatenate([km, kr[:, :BLOCK]], axis=1)[:, None]
    vm_q = jnp.concatenate([vm, vr[:, :BLOCK]], axis=1)[:, None]
    p_idx = jnp.arange(N_META)[:, None]
    j_idx = jnp.arange(BLOCK)[None, :]
    mmask = jnp.concatenate([jnp.ones((N_META, N_META), bool),
                             (N_META + j_idx - p_idx) <= WINDOW], axis=-1)[None]
    out_m = _sink_attend(qmb, km_q, vm_q, mmask, sink).reshape(B, N_META, ATTN_WIDTH)
    return jnp.concatenate([out_m, out_r], axis=1)


def _fourier_mix(f):
    B, L = f.shape[0], f.shape[1]
    fg = f.astype(jnp.float32).reshape(B, L, N_FOURIER_GROUPS, FOURIER_GROUP)
    out = jnp.fft.fft2(fg, axes=(1, 3), norm='ortho').real
    return out.reshape(B, L, FOURIER_WIDTH).astype(f.dtype)


def setup_inputs(seed: int = 0) -> dict:
    key = jax.random.key(seed)
    ks = jax.random.split(key, 16)
    f32 = jnp.float32

    def w(k, shape, fan_in):
        return jax.random.normal(k, shape, f32) * (fan_in ** -0.5)

    def gain(k):
        return 1.0 + 0.02 * jax.random.normal(k, (DEPTH, D_MODEL), f32)

    return {
        "x": jax.random.normal(ks[0], (BATCH, SEQ, D_MODEL), f32),
        "meta_tokens": jax.random.normal(ks[1], (N_META, D_MODEL), f32),
        "w_in": w(ks[2], (DEPTH, D_MODEL, IN_WIDTH), D_MODEL),
        "w_fourier_out": w(ks[3], (DEPTH, FOURIER_WIDTH, D_MODEL), FOURIER_WIDTH),
        "w_attn_out": w(ks[4], (DEPTH, ATTN_WIDTH, D_MODEL), ATTN_WIDTH),
        "w_o": w(ks[5], (DEPTH, D_MODEL, D_MODEL), D_MODEL),
        "sink_logits": jax.random.normal(ks[6], (DEPTH, N_HEADS), f32),
        "norm_mix_pre": gain(ks[7]),
        "norm_mix_post": gain(ks[8]),
        "norm_ffn_pre": gain(ks[9]),
        "norm_ffn_post": gain(ks[10]),
        "w_ffn_gate": w(ks[11], (DEPTH, D_MODEL, D_FF), D_MODEL),
        "w_ffn_up": w(ks[12], (DEPTH, D_MODEL, D_FF), D_MODEL),
        "w_ffn_down": w(ks[13], (DEPTH, D_FF, D_MODEL), D_FF),
    }


def reference(x, meta_tokens, w_in, w_fourier_out, w_attn_out, w_o, sink_logits,
              norm_mix_pre, norm_mix_post, norm_ffn_pre, norm_ffn_post,
              w_ffn_gate, w_ffn_up, w_ffn_down):
    B = x.shape[0]
    meta = jnp.broadcast_to(meta_tokens[None].astype(x.dtype), (B, N_META, D_MODEL))
    h = jnp.concatenate([meta, x], axis=1)
    L = h.shape[1]
    cos, sin = _rope_tables(L)
    splits = np.cumsum([ATTN_WIDTH, KV_WIDTH, KV_WIDTH, FOURIER_WIDTH]).tolist()

    for l in range(DEPTH):
        u = _rmsnorm(h, norm_mix_pre[l])
        proj = u @ w_in[l]
        q, k, v, f, g = jnp.split(proj, splits, axis=-1)
        q = _rope(q.reshape(B, L, N_HEADS, HEAD_DIM), cos, sin)
        k = _rope(k.reshape(B, L, N_KV_HEADS, HEAD_DIM), cos, sin)
        v = v.reshape(B, L, N_KV_HEADS, HEAD_DIM)
        y_attn = _windowed_gqa(q, k, v, sink_logits[l]) @ w_attn_out[l]
        y_four = _fourier_mix(f) @ w_fourier_out[l]
        g_four, g_attn = jnp.split(jax.nn.sigmoid(g), N_BRANCHES, axis=-1)
        mixed = (g_four * y_four + g_attn * y_attn) @ w_o[l]
        h = h + _rmsnorm(mixed, norm_mix_post[l])

        u = _rmsnorm(h, norm_ffn_pre[l])
        ff = (jax.nn.silu(u @ w_ffn_gate[l]) * (u @ w_ffn_up[l])) @ w_ffn_down[l]
        h = h + _rmsnorm(ff, norm_ffn_post[l])

    return h[:, N_META:]
```

```python
import contextlib
import numpy as np
import ml_dtypes
import concourse.bass as bass
import concourse.mybir as mybir
from concourse.bass_utils import run_bass_kernel_spmd

F32 = mybir.dt.float32
BF16 = mybir.dt.bfloat16
AF = mybir.ActivationFunctionType
ALU = mybir.AluOpType

D_MODEL = 1024
SEQ = 2048
DEPTH = 2
N_META = 16
LTOK = N_META + SEQ
NTILE = 17
D_FF = 2816
NFC = D_FF // 128
EPS = 1e-6
NKC = 12
KCW = LTOK // NKC

TILES = [(0, 16)] + [(16 + 128 * i, 128) for i in range(16)]
GROUPS = [(0, 16, [0])] + [(16 + 512 * g, 512, [1 + 4 * g + i for i in range(4)]) for g in range(4)]
TILE_GROUP = {}
for _gi, (_g0, _n, _tl) in enumerate(GROUPS):
    for _j in _tl:
        TILE_GROUP[_j] = _gi


class _Op:
    __slots__ = ("eng", "fn", "deps", "flag", "semval", "sem", "is_dma")


class Prog:
    ENGS = ("sync", "tensor", "vector", "scalar", "gpsimd")

    def __init__(self, nc, n_dma_sems=32):
        self.nc = nc
        self.ops = {e: [] for e in self.ENGS}
        self.last_writer = {}
        self.readers = {}
        self.n_dma_sems = n_dma_sems
        self.dma_count = 0
        self.dma_last = [None] * n_dma_sems
        self.dma_val = [0] * n_dma_sems
        self.region_of = {}

    def _expand(self, keys):
        out = []
        regs = set()
        for k in keys:
            out.append(k)
            r = self.region_of.get(k[0])
            if r is not None:
                regs.add(("reg", r))
        return out, regs

    def add(self, eng, fn, reads=(), writes=(), dma=False, regs=()):
        op = _Op()
        op.eng = eng
        op.fn = fn
        op.flag = False
        op.semval = 0
        op.sem = None
        op.is_dma = dma
        reads, r1 = self._expand(reads)
        writes, r2 = self._expand(writes)
        rset = r1 | r2 | {("reg", r) for r in regs}
        reads = list(reads) + [r for r in rset if r not in writes]
        deps = set()
        for r in reads:
            w = self.last_writer.get(r)
            if w is not None:
                deps.add(w)
        for r in writes:
            w = self.last_writer.get(r)
            if w is not None:
                deps.add(w)
            for rd in self.readers.get(r, ()):
                deps.add(rd)
        for r in reads:
            self.readers.setdefault(r, []).append(op)
        for r in writes:
            self.last_writer[r] = op
            self.readers[r] = []
        if dma:
            k = self.dma_count % self.n_dma_sems
            self.dma_count += 1
            prev = self.dma_last[k]
            if prev is not None:
                deps.add(prev)
            self.dma_last[k] = op
            self.dma_val[k] += 16
            op.sem = k
            op.semval = self.dma_val[k]
        deps.discard(op)
        if eng == "tensor":
            deps = {d for d in deps if not (d.eng == "tensor" and not d.is_dma)}
        op.deps = deps
        self.ops[eng].append(op)
        return op

    def emit(self, final_wait_eng="sync"):
        nc = self.nc
        for e in self.ENGS:
            for op in self.ops[e]:
                for d in op.deps:
                    if not d.is_dma:
                        d.flag = True
        for e in self.ENGS:
            c = 0
            for op in self.ops[e]:
                if op.flag and not op.is_dma:
                    c += 1
                    op.semval = c
        with contextlib.ExitStack() as st:
            engsem = {e: st.enter_context(nc.semaphore("s_" + e)) for e in self.ENGS}
            dmasem = [st.enter_context(nc.semaphore("d%d" % i)) for i in range(self.n_dma_sems)]
            block = st.enter_context(nc.Block())

            def semof(d):
                if d.is_dma:
                    return ("d", d.sem), dmasem[d.sem], d.semval
                return ("e", d.eng), engsem[d.eng], d.semval

            def make_body(ename):
                def body(e):
                    waited = {}
                    for op in self.ops[ename]:
                        need = {}
                        for d in op.deps:
                            key, sem, val = semof(d)
                            if waited.get(key, 0) < val and need.get(key, (None, 0))[1] < val:
                                need[key] = (sem, val)
                        for key, (sem, val) in need.items():
                            e.wait_ge(sem, val)
                            waited[key] = val
                        ins = op.fn(e)
                        if op.is_dma:
                            ins.then_inc(dmasem[op.sem], 16)
                        elif op.flag:
                            ins.then_inc(engsem[ename], 1)
                    if ename == final_wait_eng:
                        for k in range(self.n_dma_sems):
                            if self.dma_val[k] > waited.get(("d", k), 0):
                                e.wait_ge(dmasem[k], self.dma_val[k])
                return body

            for ename in self.ENGS:
                if self.ops[ename] or ename == final_wait_eng:
                    getattr(block, ename)(make_body(ename))


class Region:
    def __init__(self, nc, st, name, nbytes):
        self.name = name
        self.nbytes = nbytes
        self.t = st.enter_context(nc.sbuf_tensor(name, [128, nbytes // 2], BF16))

    def view(self, off, free_shape, dtype):
        esz = 2 if dtype == BF16 else 4
        n = 1
        for d in free_shape:
            n *= d
        assert off % 4 == 0 and off + n * esz <= self.nbytes, (self.name, off, free_shape)
        a = self.t[:, off // 2: off // 2 + n * esz // 2]
        if dtype != BF16:
            a = a.bitcast(dtype)
        if len(free_shape) == 2:
            a = a.rearrange("p (a b) -> p a b", a=free_shape[0])
        elif len(free_shape) == 3:
            a = a.rearrange("p (a b c) -> p a b c", a=free_shape[0], b=free_shape[1])
        return a


def build_program(nseq=2, nlayers=DEPTH, stop=None, dbg=False):
    nc = bass.Bass("TRN2", target_bir_lowering=False)

    def din(name, shape, dt):
        return nc.dram_tensor(name, shape, dt, kind="ExternalInput").ap()

    x_d = din("x", [2, SEQ, D_MODEL], F32)
    meta_d = din("meta", [N_META, D_MODEL], F32)
    wa_d = din("wa", [DEPTH, 128, 8, 1280], F32)
    was_d = din("was", [DEPTH, 128, 8, 640], F32)
    wdm_d = din("wdm", [DEPTH, 8, 128, 24, 128], F32)
    wo_d = din("wo", [DEPTH, 128, 8, 1024], F32)
    wgu_d = din("wgu", [DEPTH, NFC, 128, 16, 128], F32)
    wdn_d = din("wdn", [DEPTH, 128, NFC, 1024], F32)
    gains_d = din("gains", [DEPTH, 4, D_MODEL], F32)
    sink_d = din("sink", [DEPTH, 128, 4], F32)
    cos_d = din("rcos", [128, LTOK], F32)
    sin_d = din("rsin", [128, LTOK], F32)
    cs128_d = din("cs128", [128, 256], BF16)
    csl_d = din("csl", [NKC, 128, NTILE, 2, KCW], BF16)
    masks_d = din("masks", [128, 3, 128], BF16)
    ident_d = din("ident", [128, 128], BF16)
    ones_d = din("onesp", [128, 2, 128], BF16)
    out_d = nc.dram_tensor("out", [2, SEQ, D_MODEL], F32, kind="ExternalOutput").ap()
    hs_d = nc.dram_tensor("hs", [2, LTOK, D_MODEL], F32, kind="Internal").ap()
    dbg_outs = {}

    with contextlib.ExitStack() as st:
        X1 = Region(nc, st, "X1", 64256)
        X2 = Region(nc, st, "X2", 33024)
        X3 = Region(nc, st, "X3", 45056)
        X4 = Region(nc, st, "X4", 45056)

        def sb(name, shape, dt):
            return st.enter_context(nc.sbuf_tensor("sb_" + name, shape, dt))

        ident = sb("ident", [128, 128], BF16)
        cs128 = sb("cs128", [128, 256], BF16)
        masks = sb("masks", [128, 3, 128], BF16)
        onesp = sb("onesp", [128, 2, 128], BF16)
        epsb = sb("epsb", [128, 1], F32)
        sinkraw = sb("sinkraw", [128, DEPTH, 4], F32)
        sinkexp = sb("sinkexp", [128, DEPTH, 4], F32)
        NSTAT = nseq * nlayers * NTILE * 4 * 3
        stats = sb("stats", [128, NSTAT], F32)
        swd = sb("swd", [128, 8], F32)
        PQ = [st.enter_context(nc.psum_tensor("pq%d" % i, [128, 1024], F32)) for i in range(4)]

        P = Prog(nc)
        for nm in ("qT", "kT", "VP", "AB", "mixT", "Dst", "Wo", "actT", "Fst"):
            P.region_of[nm] = "X1"
        P.region_of["uT"] = "X2"
        for nm in ("attnT", "YT", "WAm", "rope", "Wd"):
            P.region_of[nm] = "X3"
        for nm in ("WAs", "t1", "t2", "fT", "gain", "hbuf", "ubuf", "PT", "densb", "rec", "CSL", "sg", "mm",
                   "tmp", "hnew"):
            P.region_of[nm] = "X4"

        uT = X2.view(0, [8, LTOK], BF16)

        def switch(reg):
            P.add("vector", lambda e: e.memset(swd[:, 0:1], 0.0), writes=[("reg", reg)])

        def psk(i, h=None):
            if h is None:
                return [("ps", i, 0), ("ps", i, 1)]
            return [("ps", i, h)]

        stat_ctr = [0]

        def new_stat():
            c = stat_ctr[0]
            stat_ctr[0] += 3
            assert c + 3 <= NSTAT
            return c

        def mm_chain(out_ap, pairs, reads, writes):
            def fn(e):
                n = len(pairs)
                ins = None
                for i, (l_, r_) in enumerate(pairs):
                    ins = e.matmul(out_ap, lhsT=l_, rhs=r_, start=(i == 0), stop=(i == n - 1))
                return ins
            P.add("tensor", fn, reads=reads, writes=writes)

        P.add("sync", lambda e: e.dma_start(out=ident[:], in_=ident_d), writes=[("ident",)], dma=True)
        P.add("sync", lambda e: e.dma_start(out=cs128[:], in_=cs128_d), writes=[("cs128",)], dma=True)
        P.add("sync", lambda e: e.dma_start(out=masks[:], in_=masks_d), writes=[("masks",)], dma=True)
        P.add("sync", lambda e: e.dma_start(out=onesp[:], in_=ones_d), writes=[("onesp",)], dma=True)
        P.add("sync", lambda e: e.dma_start(out=sinkraw[:], in_=sink_d.rearrange("l p c -> p l c")),
              writes=[("sinkraw",)], dma=True)
        P.add("vector", lambda e: e.memset(epsb[:], EPS), writes=[("eps",)])
        P.add("vector", lambda e: e.memset(stats[:], 0.0), writes=[("stats",)])
        P.add("scalar", lambda e: e.activation(out=sinkexp[:], in_=sinkraw[:], func=AF.Exp),
              reads=[("sinkraw",)], writes=[("sinkexp",)])

        hb_ctr = [0]

        def next_hb():
            n = hb_ctr[0] % 4
            hb_ctr[0] += 1
            return n // 2, n % 2

        def rms_stats(src_ap, nt, src_keys, junk_ap):
            c = new_stat()
            key = ("st", c)
            P.add("scalar", lambda e: e.activation(out=junk_ap, in_=src_ap, func=AF.Square,
                                                   accum_out=stats[0:nt, c:c + 1]),
                  reads=list(src_keys) + [("stats",)], writes=[key], regs=["X4"])
            P.add("scalar", lambda e: e.activation(out=stats[0:nt, c + 1:c + 2], in_=stats[0:nt, c:c + 1],
                                                   func=AF.Sqrt, scale=1.0 / D_MODEL, bias=epsb[0:nt, 0:1]),
                  reads=[key, ("eps",)], writes=[key])
            P.add("vector", lambda e: e.reciprocal(out=stats[0:nt, c + 2:c + 3], in_=stats[0:nt, c + 1:c + 2]),
                  reads=[key], writes=[key])
            return stats[0:nt, c + 2:c + 3], key

        def transposes_to_uT(ub_ap, ub_key, j, b):
            t0, nt = TILES[j]
            pT = PQ[3][:, b * 512:(b + 1) * 512].bitcast(BF16).rearrange("p (k t) -> p k t", k=8)

            def fn(e):
                ins = None
                for k in range(8):
                    ins = e.transpose(pT[:, k, 0:nt], ub_ap[0:nt, k * 128:(k + 1) * 128], ident[0:nt, 0:nt])
                return ins
            P.add("tensor", fn, reads=[ub_key, ("ident",)], writes=psk(3, b))
            P.add("scalar", lambda e: e.activation(out=uT[:, :, t0:t0 + nt], in_=pT[:, :, 0:nt], func=AF.Copy),
                  reads=psk(3, b), writes=[("uT", j)])

        def h_src(s, l, j):
            t0, nt = TILES[j]
            if l == 0:
                if j == 0:
                    return meta_d, []
                return x_d[s, 128 * (j - 1):128 * j, :], []
            return hs_d[s, t0:t0 + nt, :], [("hs", s, j)]

        def phase_A(s, l):
            switch("X1")
            switch("X3")
            switch("X4")
            WAm = X3.view(0, [8, 1280], BF16)
            cosT = X3.view(20480, [LTOK], F32)
            sinT = X3.view(20480 + 8256, [LTOK], F32)
            WAs = X4.view(0, [8, 640], BF16)
            t1 = [X4.view(10240 + 2048 * i, [512], F32) for i in range(2)]
            t2 = [X4.view(14336 + 2048 * i, [512], F32) for i in range(2)]
            fT = [X4.view(18432 + 4096 * i, [4, 512], BF16) for i in range(2)]
            gain = X4.view(26624, [1024], F32)
            hbuf = [X4.view(30720 + 4096 * i, [1024], F32) for i in range(2)]
            ubuf = [X4.view(38912 + 2048 * i, [1024], BF16) for i in range(2)]
            junk = X4.view(43008, [1024], BF16)
            qT = X1.view(0, [4, LTOK], BF16)
            kT = X1.view(16512, [LTOK], BF16)
            VP = X1.view(20640, [NTILE, 2, 128], BF16)
            AB = X1.view(29344, [NTILE, 1024], BF16)

            P.add("gpsimd", lambda e: e.dma_start(out=WAm, in_=wa_d[l]), writes=[("WAm",)], dma=True)
            P.add("gpsimd", lambda e: e.dma_start(out=WAs, in_=was_d[l]), writes=[("WAs",)], dma=True)
            P.add("sync", lambda e: e.dma_start(out=cosT, in_=cos_d), writes=[("rope", 0)], dma=True)
            P.add("sync", lambda e: e.dma_start(out=sinT, in_=sin_d), writes=[("rope", 1)], dma=True)
            P.add("sync", lambda e: e.dma_start(out=gain, in_=gains_d[l, 0].partition_broadcast(128)),
                  writes=[("gain", 0)], dma=True)
            VPflat = X1.view(20640, [NTILE * 2 * 128], BF16)
            P.add("vector", lambda e: e.memset(VPflat, 0.0), writes=[("VP", j) for j in range(NTILE)])

            def tile_stage(j):
                t0, nt = TILES[j]
                b = j % 2
                src, skeys = h_src(s, l, j)
                P.add("sync", lambda e: e.dma_start(out=hbuf[b][0:nt], in_=src), reads=skeys,
                      writes=[("hbuf", b)], dma=True)
                rstd, skey = rms_stats(hbuf[b][0:nt], nt, [("hbuf", b)], junk[0:nt])
                P.add("vector", lambda e: e.scalar_tensor_tensor(out=ubuf[b][0:nt], in0=hbuf[b][0:nt], scalar=rstd,
                                                                 in1=gain[0:nt], op0=ALU.mult, op1=ALU.mult),
                      reads=[("hbuf", b), skey, ("gain", 0)], writes=[("ubuf", b)])
                transposes_to_uT(ubuf[b], ("ubuf", b), j, b)

            def group_stage(gi):
                g0, N, tl = GROUPS[gi]
                ukeys = [("uT", j) for j in tl]
                fb = gi % 2
                for c in range(5):
                    col = c * 128
                    a_i, a_h = next_hb()
                    b_i, b_h = next_hb()
                    pa = PQ[a_i][:, a_h * 512:a_h * 512 + N]
                    pb = PQ[b_i][:, b_h * 512:b_h * 512 + N]
                    mm_chain(pa, [(WAm[:, k, col:col + 128], uT[:, k, g0:g0 + N]) for k in range(8)],
                             reads=ukeys + [("WAm",)], writes=psk(a_i, a_h))
                    mm_chain(pb, [(WAs[:, k, col:col + 128], uT[:, k, g0:g0 + N]) for k in range(8)],
                             reads=ukeys + [("WAs",)], writes=psk(b_i, b_h))
                    tb = c % 2
                    P.add("vector", lambda e, pa=pa, tb=tb: e.tensor_tensor(out=t1[tb][:, 0:N], in0=pa,
                                                                          in1=cosT[:, g0:g0 + N], op=ALU.mult),
                          reads=psk(a_i, a_h) + [("rope", 0)], writes=[("t1", tb)])
                    P.add("vector", lambda e, pb=pb, tb=tb: e.tensor_tensor(out=t2[tb][:, 0:N], in0=pb,
                                                                          in1=sinT[:, g0:g0 + N], op=ALU.mult),
                          reads=psk(b_i, b_h) + [("rope", 1)], writes=[("t2", tb)])
                    if c < 4:
                        dst = qT[:, c, g0:g0 + N]
                        dkey = ("qT", c, gi)
                    else:
                        dst = kT[:, g0:g0 + N]
                        dkey = ("kT", gi)
                    P.add("vector", lambda e, dst=dst, tb=tb: e.tensor_tensor(out=dst, in0=t1[tb][:, 0:N],
                                                                            in1=t2[tb][:, 0:N], op=ALU.add),
                          reads=[("t1", tb), ("t2", tb)], writes=[dkey])
                for c in range(4):
                    col = 640 + c * 128
                    a_i, a_h = next_hb()
                    pa = PQ[a_i][:, a_h * 512:a_h * 512 + N]
                    mm_chain(pa, [(WAm[:, k, col:col + 128], uT[:, k, g0:g0 + N]) for k in range(8)],
                             reads=ukeys + [("WAm",)], writes=psk(a_i, a_h))
                    P.add("scalar", lambda e, pa=pa, c=c: e.activation(out=fT[fb][:, c, 0:N], in_=pa, func=AF.Copy),
                          reads=psk(a_i, a_h), writes=[("fT", fb, c)])
                for j in tl:
                    t0, nt = TILES[j]
                    a_i, a_h = next_hb()
                    pv = PQ[a_i][0:nt, a_h * 512:a_h * 512 + 128]
                    mm_chain(pv, [(uT[:, k, t0:t0 + nt], WAm[:, k, 1152:1280]) for k in range(8)],
                             reads=[("uT", j), ("WAm",)], writes=psk(a_i, a_h))
                    P.add("scalar", lambda e, pv=pv, j=j, nt=nt: e.activation(out=VP[0:nt, j, 0, 0:64],
                                                                            in_=pv[:, 0:64], func=AF.Copy),
                          reads=psk(a_i, a_h), writes=[("VP", j)])
                    P.add("vector", lambda e, pv=pv, j=j, nt=nt: e.tensor_copy(out=VP[0:nt, j, 1, 64:128],
                                                                             in_=pv[:, 64:128]),
                          reads=psk(a_i, a_h), writes=[("VP", j)])
                    lo = t0 - g0
                    for hh in range(2):
                        def fn(e, hh=hh, lo=lo, nt=nt):
                            ins = None
                            for gg in (2 * hh, 2 * hh + 1):
                                ins = e.matmul(PQ[2][0:nt, gg * 256:(gg + 1) * 256], lhsT=fT[fb][:, gg, lo:lo + nt],
                                               rhs=cs128[:, :], start=True, stop=True)
                            return ins
                        P.add("tensor", fn, reads=[("fT", fb, 2 * hh), ("fT", fb, 2 * hh + 1), ("cs128",)],
                              writes=psk(2, hh))
                        eng = "scalar" if hh == 0 else "vector"
                        if hh == 0:
                            P.add("scalar", lambda e, j=j, nt=nt: e.activation(out=AB[0:nt, j, 0:512],
                                                                             in_=PQ[2][0:nt, 0:512], func=AF.Copy),
                                  reads=psk(2, 0), writes=[("AB", j, 0)])
                        else:
                            P.add("vector", lambda e, j=j, nt=nt: e.tensor_copy(out=AB[0:nt, j, 512:1024],
                                                                              in_=PQ[2][0:nt, 512:1024]),
                                  reads=psk(2, 1), writes=[("AB", j, 1)])

            for j in GROUPS[0][2]:
                tile_stage(j)
            for gi in range(len(GROUPS)):
                if gi + 1 < len(GROUPS):
                    for j in GROUPS[gi + 1][2]:
                        tile_stage(j)
                group_stage(gi)
            return dict(qT=qT, kT=kT, VP=VP, AB=AB)

        def phase_B(s, l, bufs):
            switch("X4")
            switch("X3")
            qT, kT, VP = bufs["qT"], bufs["kT"], bufs["VP"]
            attnT = X3.view(0, [4, LTOK], BF16)
            NPT = 16
            PT = [X4.view(1024 * i, [512], BF16) for i in range(NPT)]
            densb = [X4.view(16384 + 2048 * i, [512], F32) for i in range(2)]
            rec = [X4.view(20480 + 2048 * i, [512], F32) for i in range(2)]
            pt_ctr = [0]
            blocks = list(range(-1, 16))

            def s_stage(bi):
                if bi < 0:
                    q0, nq = 0, 16
                    ktiles = [(0, None), (1, 2)]
                else:
                    q0, nq = 16 + 128 * bi, 128
                    ktiles = [(0, None)]
                    if bi >= 1:
                        ktiles.append((bi, 0))
                    ktiles.append((bi + 1, None))
                    if bi <= 14:
                        ktiles.append((bi + 2, 1))
                jq = bi + 1
                gq = TILE_GROUP[jq]
                N = 4 * nq
                plist = []
                for kv in range(2):
                    r0 = 64 * kv
                    for (jt, mk) in ktiles:
                        kt0, nk = TILES[jt]
                        a_i, a_h = next_hb()
                        ps = PQ[a_i][0:nk, a_h * 512:a_h * 512 + N]
                        ps3 = ps.rearrange("p (c q) -> p c q", c=4)
                        mm_chain(ps3, [(kT[r0:r0 + 64, kt0:kt0 + nk], qT[r0:r0 + 64, :, q0:q0 + nq])],
                                 reads=[("kT", TILE_GROUP[jt])] + [("qT", c, gq) for c in range(4)],
                                 writes=psk(a_i, a_h))
                        pi = pt_ctr[0] % NPT
                        pt_ctr[0] += 1
                        pt = PT[pi][0:nk, 0:N]
                        P.add("scalar", lambda e, pt=pt, ps=ps: e.activation(out=pt, in_=ps, func=AF.Exp, scale=0.125),
                              reads=psk(a_i, a_h), writes=[("PT", pi)])
                        if mk is not None:
                            pt3 = pt.rearrange("p (c q) -> p c q", c=4)
                            mk_ap = masks[0:nk, mk, 0:nq].unsqueeze(1).to_broadcast([nk, 4, nq])
                            P.add("vector", lambda e, pt3=pt3, mk_ap=mk_ap: e.tensor_tensor(out=pt3, in0=pt3, in1=mk_ap,
                                                                                          op=ALU.mult),
                                  reads=[("PT", pi), ("masks",)], writes=[("PT", pi)])
                        plist.append((kv, jt, nk, pi, pt))
                return (bi, q0, nq, N, plist)

            def pv_stage(info, n):
                bi, q0, nq, N, plist = info
                h = n % 2
                po = PQ[2][:, h * 512:h * 512 + N]
                pd = PQ[3][:, h * 512:h * 512 + N]
                rk = [("PT", pi) for (_, _, _, pi, _) in plist]
                mm_chain(po, [(VP[0:nk, jt, kv, :], pt) for (kv, jt, nk, pi, pt) in plist],
                         reads=rk + [("VP", jt) for (_, jt, _, _, _) in plist], writes=psk(2, h))
                mm_chain(pd, [(onesp[0:nk, kv, :], pt) for (kv, jt, nk, pi, pt) in plist],
                         reads=rk + [("onesp",)], writes=psk(3, h))
                d3 = densb[h][:, 0:N].rearrange("p (c q) -> p c q", c=4)
                r3 = rec[h][:, 0:N].rearrange("p (c q) -> p c q", c=4)
                P.add("vector", lambda e: e.tensor_tensor(out=d3, in0=pd.rearrange("p (c q) -> p c q", c=4),
                                                          in1=sinkexp[:, l, :].unsqueeze(2).to_broadcast([128, 4, nq]),
                                                          op=ALU.add),
                      reads=psk(3, h) + [("sinkexp",)], writes=[("densb", h)])
                P.add("vector", lambda e: e.reciprocal(out=rec[h][:, 0:N], in_=densb[h][:, 0:N]),
                      reads=[("densb", h)], writes=[("rec", h)])
                P.add("vector", lambda e: e.tensor_tensor(out=attnT[:, :, q0:q0 + nq],
                                                          in0=po.rearrange("p (c q) -> p c q", c=4), in1=r3,
                                                          op=ALU.mult),
                      reads=psk(2, h) + [("rec", h)], writes=[("attnT", bi + 1)])

            prev = s_stage(blocks[0])
            for n, bi in enumerate(blocks):
                nxt = s_stage(blocks[n + 1]) if n + 1 < len(blocks) else None
                pv_stage(prev, n)
                prev = nxt
            return attnT

        def phase_C(s, l, bufs):
            switch("X4")
            AB = bufs["AB"]
            YT = X3.view(16512, [4, LTOK], BF16)
            CSL = [X4.view(11696 * i, [NTILE, 2, KCW], BF16) for i in range(2)]
            ctr = [0]

            def load(kc):
                cb = kc % 2
                P.add("sync", lambda e: e.dma_start(out=CSL[cb], in_=csl_d[kc]), writes=[("CSL", cb)], dma=True)

            load(0)
            for kc in range(NKC):
                if kc + 1 < NKC:
                    load(kc + 1)
                cb = kc % 2
                for g in range(4):
                    n = ctr[0]
                    ctr[0] += 1
                    a_i, a_h = (n % 8) // 2, n % 2
                    ps = PQ[a_i][:, a_h * 512:a_h * 512 + KCW]
                    pairs = []
                    for j in range(NTILE):
                        t0, nt = TILES[j]
                        pairs.append((AB[0:nt, j, g * 256:g * 256 + 128], CSL[cb][0:nt, j, 0, :]))
                        pairs.append((AB[0:nt, j, g * 256 + 128:g * 256 + 256], CSL[cb][0:nt, j, 1, :]))
                    mm_chain(ps, pairs, reads=[("AB", j, g // 2) for j in range(NTILE)] + [("CSL", cb)],
                             writes=psk(a_i, a_h))
                    dst = YT[:, g, kc * KCW:(kc + 1) * KCW]
                    if n % 2 == 0:
                        P.add("scalar", lambda e, dst=dst, ps=ps: e.activation(out=dst, in_=ps, func=AF.Copy),
                              reads=psk(a_i, a_h), writes=[("YT", g, kc)])
                    else:
                        P.add("vector", lambda e, dst=dst, ps=ps: e.tensor_copy(out=dst, in_=ps),
                              reads=psk(a_i, a_h), writes=[("YT", g, kc)])
            return YT

        def phase_D(s, l, attnT, YT):
            switch("X1")
            switch("X4")
            mixT = X1.view(0, [8, LTOK], BF16)
            Dst = [X1.view(33024 + 6144 * i, [24, 128], BF16) for i in range(2)]
            Wo = X1.view(45312, [8, 1024], BF16)
            sg = [X4.view(4096 * i, [2, 512], F32) for i in range(2)]
            mm = [X4.view(8192 + 4096 * i, [2, 512], F32) for i in range(2)]

            def load(c):
                P.add("gpsimd", lambda e: e.dma_start(out=Dst[c % 2], in_=wdm_d[l, c]), writes=[("Dst", c % 2)],
                      dma=True)

            load(0)
            P.add("gpsimd", lambda e: e.dma_start(out=Wo, in_=wo_d[l]), writes=[("Wo",)], dma=True)
            n = 0
            for c in range(8):
                if c + 1 < 8:
                    load(c + 1)
                W = Dst[c % 2]
                for gi, (g0, N, tl) in enumerate(GROUPS):
                    pa_i = (n % 2) * 2
                    pb_i = pa_i + 1
                    b = n % 2
                    n += 1
                    ukeys = [("uT", j) for j in tl]
                    ykeys = [("YT", g, kc) for g in range(4) for kc in range(g0 // KCW, (g0 + N - 1) // KCW + 1)]
                    akeys = [("attnT", j) for j in tl]
                    mm_chain(PQ[pa_i][:, 0:N], [(W[:, k, :], uT[:, k, g0:g0 + N]) for k in range(8)],
                             reads=ukeys + [("Dst", c % 2)], writes=psk(pa_i, 0))
                    mm_chain(PQ[pa_i][:, 512:512 + N], [(W[:, 8 + k, :], uT[:, k, g0:g0 + N]) for k in range(8)],
                             reads=ukeys + [("Dst", c % 2)], writes=psk(pa_i, 1))
                    mm_chain(PQ[pb_i][:, 0:N], [(W[:, 16 + k, :], YT[:, k, g0:g0 + N]) for k in range(4)],
                             reads=ykeys + [("Dst", c % 2)], writes=psk(pb_i, 0))
                    mm_chain(PQ[pb_i][:, 512:512 + N], [(W[:, 20 + k, :], attnT[:, k, g0:g0 + N]) for k in range(4)],
                             reads=akeys + [("Dst", c % 2)], writes=psk(pb_i, 1))
                    pa3 = PQ[pa_i][:, :].rearrange("p (a b) -> p a b", a=2)[:, :, 0:N]
                    pb3 = PQ[pb_i][:, :].rearrange("p (a b) -> p a b", a=2)[:, :, 0:N]
                    P.add("scalar", lambda e, b=b, pa3=pa3, N=N: e.activation(out=sg[b][:, :, 0:N], in_=pa3,
                                                                            func=AF.Sigmoid),
                          reads=psk(pa_i), writes=[("sg", b)])
                    P.add("vector", lambda e, b=b, pb3=pb3, N=N: e.tensor_tensor(out=mm[b][:, :, 0:N],
                                                                               in0=sg[b][:, :, 0:N], in1=pb3,
                                                                               op=ALU.mult),
                          reads=psk(pb_i) + [("sg", b)], writes=[("mm", b)])
                    P.add("vector", lambda e, b=b, c=c, g0=g0, N=N: e.tensor_tensor(out=mixT[:, c, g0:g0 + N],
                                                                                  in0=mm[b][:, 0, 0:N],
                                                                                  in1=mm[b][:, 1, 0:N], op=ALU.add),
                          reads=[("mm", b)], writes=[("mixT", c, gi)])
            return mixT, Wo

        def phase_E(s, l, mixT, Wo):
            switch("X4")
            hbuf = [X4.view(4096 * i, [1024], F32) for i in range(2)]
            tmp = [X4.view(8192 + 4096 * i, [1024], F32) for i in range(2)]
            hnew = [X4.view(16384 + 4096 * i, [1024], F32) for i in range(2)]
            ubuf = [X4.view(24576 + 2048 * i, [1024], BF16) for i in range(2)]
            junk = X4.view(28672, [1024], BF16)
            gain = [X4.view(30720 + 4096 * i, [1024], F32) for i in range(2)]
            for gi_, idx in ((0, 1), (1, 2)):
                P.add("sync", lambda e, gi_=gi_, idx=idx: e.dma_start(out=gain[gi_],
                                                                      in_=gains_d[l, idx].partition_broadcast(128)),
                      writes=[("gain", gi_)], dma=True)

            def load(j):
                t0, nt = TILES[j]
                src, skeys = h_src(s, l, j)
                P.add("sync", lambda e: e.dma_start(out=hbuf[j % 2][0:nt], in_=src), reads=skeys,
                      writes=[("hbuf", j % 2)], dma=True)

            load(0)
            for j in range(NTILE):
                t0, nt = TILES[j]
                b = j % 2
                if j + 1 < NTILE:
                    load(j + 1)
                gi = TILE_GROUP[j]
                for hh in range(2):
                    mm_chain(PQ[b][0:nt, hh * 512:(hh + 1) * 512],
                             [(mixT[:, k, t0:t0 + nt], Wo[:, k, hh * 512:(hh + 1) * 512]) for k in range(8)],
                             reads=[("mixT", c, gi) for c in range(8)] + [("Wo",)], writes=psk(b, hh))
                rstd, skey = rms_stats(PQ[b][0:nt, :], nt, psk(b), junk[0:nt])
                P.add("vector", lambda e, b=b, nt=nt, rstd=rstd: e.scalar_tensor_tensor(
                    out=tmp[b][0:nt], in0=PQ[b][0:nt, :], scalar=rstd, in1=gain[0][0:nt], op0=ALU.mult, op1=ALU.mult),
                    reads=psk(b) + [skey, ("gain", 0)], writes=[("tmp", b)])
                P.add("vector", lambda e, b=b, nt=nt: e.tensor_tensor(out=hnew[b][0:nt], in0=hbuf[b][0:nt],
                                                                     in1=tmp[b][0:nt], op=ALU.add),
                      reads=[("hbuf", b), ("tmp", b)], writes=[("hnew", b)])
                P.add("sync", lambda e, b=b, nt=nt, t0=t0: e.dma_start(out=hs_d[s, t0:t0 + nt, :], in_=hnew[b][0:nt]),
                      reads=[("hnew", b)], writes=[("hs", s, j)], dma=True)
                rstd2, skey2 = rms_stats(hnew[b][0:nt], nt, [("hnew", b)], junk[0:nt])
                P.add("vector", lambda e, b=b, nt=nt, rstd2=rstd2: e.scalar_tensor_tensor(
                    out=ubuf[b][0:nt], in0=hnew[b][0:nt], scalar=rstd2, in1=gain[1][0:nt], op0=ALU.mult,
                    op1=ALU.mult),
                    reads=[("hnew", b), skey2, ("gain", 1)], writes=[("ubuf", b)])
                transposes_to_uT(ubuf[b], ("ubuf", b), j, b)

        def phase_F(s, l, last):
            switch("X1")
            switch("X3")
            switch("X4")
            HT = 1040
            actT = X1.view(0, [NFC, HT], BF16)
            Fst = [X1.view(45760 + 4096 * i, [16, 128], BF16) for i in range(3)]
            Wd = X3.view(0, [NFC, 1024], BF16)
            sg = [X4.view(2048 * i, [512], F32) for i in range(3)]
            hbuf = [X4.view(6144 + 4096 * i, [1024], F32) for i in range(2)]
            tmp = [X4.view(14336 + 4096 * i, [1024], F32) for i in range(2)]
            hnew = [X4.view(22528 + 4096 * i, [1024], F32) for i in range(2)]
            gain = X4.view(30720, [1024], F32)
            junk = X4.view(34816, [1024], BF16)
            P.add("sync", lambda e: e.dma_start(out=gain, in_=gains_d[l, 3].partition_broadcast(128)),
                  writes=[("gain", 0)], dma=True)
            for part in range(2):
                f0, f1 = part * 11, (part + 1) * 11
                P.add("gpsimd", lambda e, f0=f0, f1=f1: e.dma_start(out=Wd[:, f0:f1, :], in_=wdn_d[l, :, f0:f1, :]),
                      writes=[("Wd", part)], dma=True)
            fctr = [0]
            nctr = [0]
            tctr = [0]
            for hf in range(2):
                if hf == 0:
                    h0, tiles, groups = 0, list(range(0, 9)), [0, 1, 2]
                else:
                    h0, tiles, groups = 1040, list(range(9, 17)), [3, 4]

                def load(fc):
                    sl = (fctr[0] + (fc - 0)) % 3
                    P.add("gpsimd", lambda e: e.dma_start(out=Fst[sl], in_=wgu_d[l, fc]), writes=[("Fst", sl)],
                          dma=True)
                    return sl

                slots = {}
                slots[0] = load(0)
                slots[1] = load(1)
                for fc in range(NFC):
                    if fc + 2 < NFC:
                        slots[fc + 2] = load(fc + 2)
                    sl = slots[fc]
                    W = Fst[sl]
                    for gi in groups:
                        g0, N, tl = GROUPS[gi]
                        n = nctr[0]
                        nctr[0] += 1
                        r = n % 4
                        sb_i = n % 3
                        ukeys = [("uT", j) for j in tl]
                        mm_chain(PQ[r][:, 0:N], [(W[:, k, :], uT[:, k, g0:g0 + N]) for k in range(8)],
                                 reads=ukeys + [("Fst", sl)], writes=psk(r, 0))
                        mm_chain(PQ[r][:, 512:512 + N], [(W[:, 8 + k, :], uT[:, k, g0:g0 + N]) for k in range(8)],
                                 reads=ukeys + [("Fst", sl)], writes=psk(r, 1))
                        P.add("scalar", lambda e, r=r, sb_i=sb_i, N=N: e.activation(out=sg[sb_i][:, 0:N],
                                                                                  in_=PQ[r][:, 0:N], func=AF.Silu),
                              reads=psk(r, 0), writes=[("sg", sb_i)])
                        P.add("vector", lambda e, r=r, sb_i=sb_i, N=N, fc=fc, g0=g0, h0=h0: e.tensor_tensor(
                            out=actT[:, fc, g0 - h0:g0 - h0 + N], in0=sg[sb_i][:, 0:N], in1=PQ[r][:, 512:512 + N],
                            op=ALU.mult),
                            reads=psk(r, 1) + [("sg", sb_i)], writes=[("actT", fc, gi)])
                fctr[0] += NFC

                def loadh(j):
                    t0, nt = TILES[j]
                    P.add("sync", lambda e: e.dma_start(out=hbuf[j % 2][0:nt], in_=hs_d[s, t0:t0 + nt, :]),
                          reads=[("hs", s, j)], writes=[("hbuf", j % 2)], dma=True)

                loadh(tiles[0])
                for ti, j in enumerate(tiles):
                    t0, nt = TILES[j]
                    b = j % 2
                    if ti + 1 < len(tiles):
                        loadh(tiles[ti + 1])
                    gi = TILE_GROUP[j]
                    r = tctr[0] % 4
                    tctr[0] += 1
                    lo = t0 - h0
                    for hh in range(2):
                        mm_chain(PQ[r][0:nt, hh * 512:(hh + 1) * 512],
                                 [(actT[:, fc, lo:lo + nt], Wd[:, fc, hh * 512:(hh + 1) * 512]) for fc in range(NFC)],
                                 reads=[("actT", fc, gi) for fc in range(NFC)] + [("Wd", 0), ("Wd", 1)],
                                 writes=psk(r, hh))
                    rstd, skey = rms_stats(PQ[r][0:nt, :], nt, psk(r), junk[0:nt])
                    P.add("vector", lambda e, b=b, nt=nt, rstd=rstd, r=r: e.scalar_tensor_tensor(
                        out=tmp[b][0:nt], in0=PQ[r][0:nt, :], scalar=rstd, in1=gain[0:nt], op0=ALU.mult,
                        op1=ALU.mult),
                        reads=psk(r) + [skey, ("gain", 0)], writes=[("tmp", b)])
                    P.add("vector", lambda e, b=b, nt=nt: e.tensor_tensor(out=hnew[b][0:nt], in0=hbuf[b][0:nt],
                                                                         in1=tmp[b][0:nt], op=ALU.add),
                          reads=[("hbuf", b), ("tmp", b)], writes=[("hnew", b)])
                    if last:
                        if j >= 1:
                            P.add("sync", lambda e, b=b, j=j: e.dma_start(out=out_d[s, 128 * (j - 1):128 * j, :],
                                                                          in_=hnew[b][:, :]),
                                  reads=[("hnew", b)], writes=[("out", s, j)], dma=True)
                    else:
                        P.add("sync", lambda e, b=b, nt=nt, t0=t0: e.dma_start(out=hs_d[s, t0:t0 + nt, :],
                                                                              in_=hnew[b][0:nt]),
                              reads=[("hnew", b)], writes=[("hs", s, j)], dma=True)

        def dump(name, ap, shape, dt):
            d = nc.dram_tensor("dbg_" + name, shape, dt, kind="ExternalOutput").ap()
            dbg_outs[name] = d
            P.add("sync", lambda e: e.dma_start(out=d, in_=ap), reads=list(P.last_writer.keys()), dma=True)

        done = False
        for s in range(nseq):
            for l in range(nlayers):
                bufs = phase_A(s, l)
                attnT = phase_B(s, l, bufs)
                YT = phase_C(s, l, bufs)
                if stop == "C":
                    dump("qT", bufs["qT"], [128, 4, LTOK], BF16)
                    dump("kT", bufs["kT"], [128, LTOK], BF16)
                    dump("VP", bufs["VP"], [128, NTILE, 2, 128], BF16)
                    dump("AB", bufs["AB"], [128, NTILE, 1024], BF16)
                    dump("uT", uT, [128, 8, LTOK], BF16)
                    dump("attnT", attnT, [128, 4, LTOK], BF16)
                    dump("YT", YT, [128, 4, LTOK], BF16)
                    done = True
                    break
                mixT, Wo = phase_D(s, l, attnT, YT)
                if stop == "D":
                    dump("mixT", mixT, [128, 8, LTOK], BF16)
                    done = True
                    break
                phase_E(s, l, mixT, Wo)
                if stop == "E":
                    dump("uT", uT, [128, 8, LTOK], BF16)
                    dump("hs", hs_d[s], [LTOK, D_MODEL], F32)
                    done = True
                    break
                phase_F(s, l, last=(l == nlayers - 1) and stop is None)
                if stop == "F":
                    dump("hs", hs_d[s], [LTOK, D_MODEL], F32)
                    done = True
                    break
            if done:
                break
        P.emit()
    return nc


def _q_perm():
    idx = np.empty(512, np.int64)
    for c in range(4):
        for half in range(2):
            for d in range(64):
                idx[c * 128 + half * 64 + d] = (c + 4 * half) * 64 + d
    return idx


def _swap64(n):
    idx = np.arange(n)
    return (idx // 64) * 64 + ((idx % 64) + 32) % 64


def _constants():
    bf = ml_dtypes.bfloat16
    c = {}
    inv_freq = (10000.0 ** (-(np.arange(0, 64, 2, dtype=np.float32)) / np.float32(64))).astype(np.float32)
    ang = (np.arange(LTOK, dtype=np.float32)[:, None] * inv_freq[None, :]).astype(np.float32)
    cos = np.cos(ang.astype(np.float64))
    sin = np.sin(ang.astype(np.float64))
    p = np.arange(128)
    d = p % 64
    fi = d % 32
    sign = np.where(d < 32, -1.0, 1.0)
    c["rcos"] = np.ascontiguousarray(cos[:, fi].T).astype(np.float32)
    c["rsin"] = np.ascontiguousarray((sin[:, fi] * sign[None, :]).T).astype(np.float32)
    cc = np.arange(128)
    a128 = 2 * np.pi * ((cc[:, None] * cc[None, :]) % 128) / 128.0
    c["cs128"] = np.concatenate([np.cos(a128), -np.sin(a128)], axis=1).astype(np.float64) / np.sqrt(128.0)
    c["cs128"] = c["cs128"].astype(bf)
    n_of = np.zeros((128, NTILE), np.int64)
    valid = np.zeros((128, NTILE), bool)
    for j, (t0, nt) in enumerate(TILES):
        n_of[:nt, j] = t0 + np.arange(nt)
        valid[:nt, j] = True
    k_all = np.arange(LTOK).reshape(NKC, KCW)
    csl = np.zeros((NKC, 128, NTILE, 2, KCW), np.float32)
    for kc in range(NKC):
        r = (n_of[:, :, None] * k_all[kc][None, None, :]) % LTOK
        a = 2 * np.pi * r / LTOK
        csl[kc, :, :, 0, :] = np.cos(a) / np.sqrt(LTOK) * valid[:, :, None]
        csl[kc, :, :, 1, :] = np.sin(a) / np.sqrt(LTOK) * valid[:, :, None]
    c["csl"] = csl.astype(bf)
    jj = np.arange(128)[:, None]
    ss = np.arange(128)[None, :]
    m = np.zeros((128, 3, 128), np.float32)
    m[:, 0, :] = (jj >= ss)
    m[:, 1, :] = (jj <= ss)
    m[:, 2, :] = (jj <= 112 + ss)
    c["masks"] = m.astype(bf)
    c["ident"] = np.eye(128, dtype=np.float32).astype(bf)
    o = np.zeros((128, 2, 128), np.float32)
    o[:, 0, 0:64] = 1.0
    o[:, 1, 64:128] = 1.0
    c["onesp"] = o.astype(bf)
    return c


def _prep_weights(w_in, w_fourier_out, w_attn_out, w_o, sink_logits, norm_mix_pre, norm_mix_post,
                  norm_ffn_pre, norm_ffn_post, w_ffn_gate, w_ffn_up, w_ffn_down):
    f32 = np.float32
    qp = _q_perm()
    w = {}
    wq = w_in[:, :, 0:512][:, :, qp]
    wk = w_in[:, :, 512:640]
    wv = w_in[:, :, 640:768]
    wf = w_in[:, :, 768:1280]
    main = np.concatenate([wq, wk, wf, wv], axis=2)
    swp = np.concatenate([wq[:, :, _swap64(512)], wk[:, :, _swap64(128)]], axis=2)
    w["wa"] = np.ascontiguousarray(main.reshape(DEPTH, 8, 128, 1280).transpose(0, 2, 1, 3), dtype=f32)
    w["was"] = np.ascontiguousarray(swp.reshape(DEPTH, 8, 128, 640).transpose(0, 2, 1, 3), dtype=f32)
    wgf = w_in[:, :, 1280:2304].reshape(DEPTH, 8, 128, 8, 128)
    wga = w_in[:, :, 2304:3328].reshape(DEPTH, 8, 128, 8, 128)
    wfo = w_fourier_out.reshape(DEPTH, 4, 128, 8, 128)
    rows = np.empty(512, np.int64)
    for kc in range(4):
        for p_ in range(128):
            head = kc if p_ < 64 else 4 + kc
            rows[kc * 128 + p_] = head * 64 + (p_ % 64)
    wao = w_attn_out[:, rows, :].reshape(DEPTH, 4, 128, 8, 128)
    pack = np.concatenate([wgf, wga, wfo, wao], axis=1)
    w["wdm"] = np.ascontiguousarray(pack.transpose(0, 3, 2, 1, 4), dtype=f32)
    w["wo"] = np.ascontiguousarray(w_o.reshape(DEPTH, 8, 128, 1024).transpose(0, 2, 1, 3), dtype=f32)
    g_ = w_ffn_gate.reshape(DEPTH, 8, 128, NFC, 128)
    u_ = w_ffn_up.reshape(DEPTH, 8, 128, NFC, 128)
    gu = np.concatenate([g_, u_], axis=1)
    w["wgu"] = np.ascontiguousarray(gu.transpose(0, 3, 2, 1, 4), dtype=f32)
    w["wdn"] = np.ascontiguousarray(w_ffn_down.reshape(DEPTH, NFC, 128, 1024).transpose(0, 2, 1, 3), dtype=f32)
    w["gains"] = np.ascontiguousarray(np.stack([norm_mix_pre, norm_mix_post, norm_ffn_pre, norm_ffn_post], axis=1),
                                      dtype=f32)
    sk = np.empty((DEPTH, 128, 4), f32)
    sk[:, 0:64, :] = sink_logits[:, None, 0:4]
    sk[:, 64:128, :] = sink_logits[:, None, 4:8]
    w["sink"] = sk
    return w


_NC_CACHE = {}


def kernel(x, meta_tokens, w_in, w_fourier_out, w_attn_out, w_o, sink_logits, norm_mix_pre, norm_mix_post,
           norm_ffn_pre, norm_ffn_post, w_ffn_gate, w_ffn_up, w_ffn_down):
    args = [np.asarray(a, dtype=np.float32) for a in (w_in, w_fourier_out, w_attn_out, w_o, sink_logits,
                                                      norm_mix_pre, norm_mix_post, norm_ffn_pre, norm_ffn_post,
                                                      w_ffn_gate, w_ffn_up, w_ffn_down)]
    x = np.asarray(x, dtype=np.float32)
    shared = _prep_weights(*args)
    shared.update(_constants())
    shared["meta"] = np.ascontiguousarray(np.asarray(meta_tokens, dtype=np.float32))
    if "nc" not in _NC_CACHE:
        _NC_CACHE["nc"] = build_program()
    nc = _NC_CACHE["nc"]
    in_maps = []
    for c in range(8):
        m = dict(shared)
        m["x"] = np.ascontiguousarray(x[2 * c:2 * c + 2])
        in_maps.append(m)
    res = run_bass_kernel_spmd(nc, in_maps, core_ids=list(range(8)))
    out = np.concatenate([np.asarray(r["out"]) for r in res.results], axis=0)
    return out.astype(np.float32)
```

```python
import contextlib
import numpy as np
import ml_dtypes
import concourse.bass as bass
import concourse.mybir as mybir
from concourse.bass_utils import run_bass_kernel_spmd

F32 = mybir.dt.float32
BF16 = mybir.dt.bfloat16
AF = mybir.ActivationFunctionType
ALU = mybir.AluOpType

D_MODEL = 1024
SEQ = 2048
DEPTH = 2
N_META = 16
LTOK = N_META + SEQ
NTILE = 17
D_FF = 2816
NFC = D_FF // 128
EPS = 1e-6
NKC = 12
KCW = LTOK // NKC

TILES = [(0, 16)] + [(16 + 128 * i, 128) for i in range(16)]
GROUPS = [(0, 16, [0])] + [(16 + 512 * g, 512, [1 + 4 * g + i for i in range(4)]) for g in range(4)]
TILE_GROUP = {}
for _gi, (_g0, _n, _tl) in enumerate(GROUPS):
    for _j in _tl:
        TILE_GROUP[_j] = _gi


class _Op:
    __slots__ = ("eng", "fn", "deps", "flag", "semval", "sem", "is_dma")


class Prog:
    ENGS = ("sync", "tensor", "vector", "scalar", "gpsimd")

    def __init__(self, nc, n_dma_sems=32):
        self.nc = nc
        self.ops = {e: [] for e in self.ENGS}
        self.last_writer = {}
        self.readers = {}
        self.n_dma_sems = n_dma_sems
        self.dma_count = 0
        self.dma_last = [None] * n_dma_sems
        self.dma_val = [0] * n_dma_sems
        self.region_of = {}

    def _expand(self, keys):
        out = []
        regs = set()
        for k in keys:
            out.append(k)
            r = self.region_of.get(k[0])
            if r is not None:
                regs.add(("reg", r))
        return out, regs

    def add(self, eng, fn, reads=(), writes=(), dma=False, regs=()):
        op = _Op()
        op.eng = eng
        op.fn = fn
        op.flag = False
        op.semval = 0
        op.sem = None
        op.is_dma = dma
        reads, r1 = self._expand(reads)
        writes, r2 = self._expand(writes)
        rset = r1 | r2 | {("reg", r) for r in regs}
        reads = list(reads) + [r for r in rset if r not in writes]
        deps = set()
        for r in reads:
            w = self.last_writer.get(r)
            if w is not None:
                deps.add(w)
        for r in writes:
            w = self.last_writer.get(r)
            if w is not None:
                deps.add(w)
            for rd in self.readers.get(r, ()):
                deps.add(rd)
        for r in reads:
            self.readers.setdefault(r, []).append(op)
        for r in writes:
            self.last_writer[r] = op
            self.readers[r] = []
        if dma:
            k = self.dma_count % self.n_dma_sems
            self.dma_count += 1
            prev = self.dma_last[k]
            if prev is not None:
                deps.add(prev)
            self.dma_last[k] = op
            self.dma_val[k] += 16
            op.sem = k
            op.semval = self.dma_val[k]
        deps.discard(op)
        if eng == "tensor":
            deps = {d for d in deps if not (d.eng == "tensor" and not d.is_dma)}
        op.deps = deps
        self.ops[eng].append(op)
        return op

    def emit(self, final_wait_eng="sync"):
        nc = self.nc
        for e in self.ENGS:
            for op in self.ops[e]:
                for d in op.deps:
                    if not d.is_dma:
                        d.flag = True
        for e in self.ENGS:
            c = 0
            for op in self.ops[e]:
                if op.flag and not op.is_dma:
                    c += 1
                    op.semval = c
        with contextlib.ExitStack() as st:
            engsem = {e: st.enter_context(nc.semaphore("s_" + e)) for e in self.ENGS}
            dmasem = [st.enter_context(nc.semaphore("d%d" % i)) for i in range(self.n_dma_sems)]
            block = st.enter_context(nc.Block())

            def semof(d):
                if d.is_dma:
                    return ("d", d.sem), dmasem[d.sem], d.semval
                return ("e", d.eng), engsem[d.eng], d.semval

            def make_body(ename):
                def body(e):
                    waited = {}
                    for op in self.ops[ename]:
                        need = {}
                        for d in op.deps:
                            key, sem, val = semof(d)
                            if waited.get(key, 0) < val and need.get(key, (None, 0))[1] < val:
                                need[key] = (sem, val)
                        for key, (sem, val) in need.items():
                            e.wait_ge(sem, val)
                            waited[key] = val
                        ins = op.fn(e)
                        if op.is_dma:
                            ins.then_inc(dmasem[op.sem], 16)
                        elif op.flag:
                            ins.then_inc(engsem[ename], 1)
                    if ename == final_wait_eng:
                        for k in range(self.n_dma_sems):
                            if self.dma_val[k] > waited.get(("d", k), 0):
                                e.wait_ge(dmasem[k], self.dma_val[k])
                return body

            for ename in self.ENGS:
                if self.ops[ename] or ename == final_wait_eng:
                    getattr(block, ename)(make_body(ename))


class Region:
    def __init__(self, nc, st, name, nbytes):
        self.name = name
        self.nbytes = nbytes
        self.t = st.enter_context(nc.sbuf_tensor(name, [128, nbytes // 2], BF16))

    def view(self, off, free_shape, dtype):
        esz = 2 if dtype == BF16 else 4
        n = 1
        for d in free_shape:
            n *= d
        assert off % 4 == 0 and off + n * esz <= self.nbytes, (self.name, off, free_shape)
        a = self.t[:, off // 2: off // 2 + n * esz // 2]
        if dtype != BF16:
            a = a.bitcast(dtype)
        if len(free_shape) == 2:
            a = a.rearrange("p (a b) -> p a b", a=free_shape[0])
        elif len(free_shape) == 3:
            a = a.rearrange("p (a b c) -> p a b c", a=free_shape[0], b=free_shape[1])
        return a


def build_program(nseq=2, nlayers=DEPTH, stop=None, dbg=False):
    nc = bass.Bass("TRN2", target_bir_lowering=False)

    def din(name, shape, dt):
        return nc.dram_tensor(name, shape, dt, kind="ExternalInput").ap()

    x_d = din("x", [2, SEQ, D_MODEL], F32)
    meta_d = din("meta", [N_META, D_MODEL], F32)
    wa_d = din("wa", [DEPTH, 128, 8, 1280], F32)
    was_d = din("was", [DEPTH, 128, 8, 640], F32)
    wdm_d = din("wdm", [DEPTH, 8, 128, 24, 128], F32)
    wo_d = din("wo", [DEPTH, 128, 8, 1024], F32)
    wgu_d = din("wgu", [DEPTH, NFC, 128, 16, 128], F32)
    wdn_d = din("wdn", [DEPTH, 128, NFC, 1024], F32)
    gains_d = din("gains", [DEPTH, 4, D_MODEL], F32)
    sink_d = din("sink", [DEPTH, 128, 4], F32)
    cos_d = din("rcos", [128, LTOK], F32)
    sin_d = din("rsin", [128, LTOK], F32)
    cs128_d = din("cs128", [128, 256], BF16)
    csl_d = din("csl", [NKC, 128, NTILE, 2, KCW], BF16)
    masks_d = din("masks", [128, 3, 128], BF16)
    ident_d = din("ident", [128, 128], BF16)
    ones_d = din("onesp", [128, 2, 128], BF16)
    out_d = nc.dram_tensor("out", [2, SEQ, D_MODEL], F32, kind="ExternalOutput").ap()
    hs_d = nc.dram_tensor("hs", [2, LTOK, D_MODEL], F32, kind="Internal").ap()
    dbg_outs = {}

    with contextlib.ExitStack() as st:
        X1 = Region(nc, st, "X1", 64256)
        X2 = Region(nc, st, "X2", 33024)
        X3 = Region(nc, st, "X3", 45056)
        X4 = Region(nc, st, "X4", 45056)

        def sb(name, shape, dt):
            return st.enter_context(nc.sbuf_tensor("sb_" + name, shape, dt))

        ident = sb("ident", [128, 128], BF16)
        cs128 = sb("cs128", [128, 256], BF16)
        masks = sb("masks", [128, 3, 128], BF16)
        onesp = sb("onesp", [128, 2, 128], BF16)
        epsb = sb("epsb", [128, 1], F32)
        sinkraw = sb("sinkraw", [128, DEPTH, 4], F32)
        sinkexp = sb("sinkexp", [128, DEPTH, 4], F32)
        NSTAT = nseq * nlayers * NTILE * 4 * 3
        stats = sb("stats", [128, NSTAT], F32)
        swd = sb("swd", [128, 8], F32)
        PQ = [st.enter_context(nc.psum_tensor("pq%d" % i, [128, 1024], F32)) for i in range(4)]

        P = Prog(nc)
        for nm in ("qT", "kT", "VP", "AB", "mixT", "Dst", "Wo", "actT", "Fst"):
            P.region_of[nm] = "X1"
        P.region_of["uT"] = "X2"
        for nm in ("attnT", "YT", "WAm", "rope", "Wd"):
            P.region_of[nm] = "X3"
        for nm in ("WAs", "t1", "t2", "fT", "gain", "hbuf", "ubuf", "PT", "densb", "rec", "CSL", "sg", "mm",
                   "tmp", "hnew"):
            P.region_of[nm] = "X4"

        uT = X2.view(0, [8, LTOK], BF16)

        def switch(reg):
            P.add("vector", lambda e: e.memset(swd[:, 0:1], 0.0), writes=[("reg", reg)])

        def psk(i, h=None):
            if h is None:
                return [("ps", i, 0), ("ps", i, 1)]
            return [("ps", i, h)]

        stat_ctr = [0]

        def new_stat():
            c = stat_ctr[0]
            stat_ctr[0] += 3
            assert c + 3 <= NSTAT
            return c

        def mm_chain(out_ap, pairs, reads, writes):
            def fn(e):
                n = len(pairs)
                ins = None
                for i, (l_, r_) in enumerate(pairs):
                    ins = e.matmul(out_ap, lhsT=l_, rhs=r_, start=(i == 0), stop=(i == n - 1))
                return ins
            P.add("tensor", fn, reads=reads, writes=writes)

        P.add("sync", lambda e: e.dma_start(out=ident[:], in_=ident_d), writes=[("ident",)], dma=True)
        P.add("sync", lambda e: e.dma_start(out=cs128[:], in_=cs128_d), writes=[("cs128",)], dma=True)
        P.add("sync", lambda e: e.dma_start(out=masks[:], in_=masks_d), writes=[("masks",)], dma=True)
        P.add("sync", lambda e: e.dma_start(out=onesp[:], in_=ones_d), writes=[("onesp",)], dma=True)
        P.add("sync", lambda e: e.dma_start(out=sinkraw[:], in_=sink_d.rearrange("l p c -> p l c")),
              writes=[("sinkraw",)], dma=True)
        P.add("vector", lambda e: e.memset(epsb[:], EPS), writes=[("eps",)])
        P.add("vector", lambda e: e.memset(stats[:], 0.0), writes=[("stats",)])
        P.add("scalar", lambda e: e.activation(out=sinkexp[:], in_=sinkraw[:], func=AF.Exp),
              reads=[("sinkraw",)], writes=[("sinkexp",)])

        hb_ctr = [0]

        def next_hb():
            n = hb_ctr[0] % 4
            hb_ctr[0] += 1
            return n // 2, n % 2

        def rms_stats(src_ap, nt, src_keys, junk_ap):
            c = new_stat()
            key = ("st", c)
            P.add("scalar", lambda e: e.activation(out=junk_ap, in_=src_ap, func=AF.Square,
                                                   accum_out=stats[0:nt, c:c + 1]),
                  reads=list(src_keys) + [("stats",)], writes=[key], regs=["X4"])
            P.add("scalar", lambda e: e.activation(out=stats[0:nt, c + 1:c + 2], in_=stats[0:nt, c:c + 1],
                                                   func=AF.Sqrt, scale=1.0 / D_MODEL, bias=epsb[0:nt, 0:1]),
                  reads=[key, ("eps",)], writes=[key])
            P.add("vector", lambda e: e.reciprocal(out=stats[0:nt, c + 2:c + 3], in_=stats[0:nt, c + 1:c + 2]),
                  reads=[key], writes=[key])
            return stats[0:nt, c + 2:c + 3], key

        def transposes_to_uT(ub_ap, ub_key, j, b):
            t0, nt = TILES[j]
            pT = PQ[3][:, b * 512:(b + 1) * 512].bitcast(BF16).rearrange("p (k t) -> p k t", k=8)

            def fn(e):
                ins = None
                for k in range(8):
                    ins = e.transpose(pT[:, k, 0:nt], ub_ap[0:nt, k * 128:(k + 1) * 128], ident[0:nt, 0:nt])
                return ins
            P.add("tensor", fn, reads=[ub_key, ("ident",)], writes=psk(3, b))
            P.add("scalar", lambda e: e.activation(out=uT[:, :, t0:t0 + nt], in_=pT[:, :, 0:nt], func=AF.Copy),
                  reads=psk(3, b), writes=[("uT", j)])

        def h_src(s, l, j):
            t0, nt = TILES[j]
            if l == 0:
                if j == 0:
                    return meta_d, []
                return x_d[s, 128 * (j - 1):128 * j, :], []
            return hs_d[s, t0:t0 + nt, :], [("hs", s, j)]

        def phase_A(s, l):
            switch("X1")
            switch("X3")
            switch("X4")
            WAm = X3.view(0, [8, 1280], BF16)
            cosT = X3.view(20480, [LTOK], F32)
            sinT = X3.view(20480 + 8256, [LTOK], F32)
            WAs = X4.view(0, [8, 640], BF16)
            t1 = [X4.view(10240 + 2048 * i, [512], F32) for i in range(2)]
            t2 = [X4.view(14336 + 2048 * i, [512], F32) for i in range(2)]
            fT = [X4.view(18432 + 4096 * i, [4, 512], BF16) for i in range(2)]
            gain = X4.view(26624, [1024], F32)
            hbuf = [X4.view(30720 + 4096 * i, [1024], F32) for i in range(2)]
            ubuf = [X4.view(38912 + 2048 * i, [1024], BF16) for i in range(2)]
            junk = X4.view(43008, [1024], BF16)
            qT = X1.view(0, [4, LTOK], BF16)
            kT = X1.view(16512, [LTOK], BF16)
            VP = X1.view(20640, [NTILE, 2, 128], BF16)
            AB = X1.view(29344, [NTILE, 1024], BF16)

            P.add("gpsimd", lambda e: e.dma_start(out=WAm, in_=wa_d[l]), writes=[("WAm",)], dma=True)
            P.add("gpsimd", lambda e: e.dma_start(out=WAs, in_=was_d[l]), writes=[("WAs",)], dma=True)
            P.add("sync", lambda e: e.dma_start(out=cosT, in_=cos_d), writes=[("rope", 0)], dma=True)
            P.add("sync", lambda e: e.dma_start(out=sinT, in_=sin_d), writes=[("rope", 1)], dma=True)
            P.add("sync", lambda e: e.dma_start(out=gain, in_=gains_d[l, 0].partition_broadcast(128)),
                  writes=[("gain", 0)], dma=True)
            VPflat = X1.view(20640, [NTILE * 2 * 128], BF16)
            P.add("vector", lambda e: e.memset(VPflat, 0.0), writes=[("VP", j) for j in range(NTILE)])

            def tile_stage(j):
                t0, nt = TILES[j]
                b = j % 2
                src, skeys = h_src(s, l, j)
                P.add("sync", lambda e: e.dma_start(out=hbuf[b][0:nt], in_=src), reads=skeys,
                      writes=[("hbuf", b)], dma=True)
                rstd, skey = rms_stats(hbuf[b][0:nt], nt, [("hbuf", b)], junk[0:nt])
                P.add("vector", lambda e: e.scalar_tensor_tensor(out=ubuf[b][0:nt], in0=hbuf[b][0:nt], scalar=rstd,
                                                                 in1=gain[0:nt], op0=ALU.mult, op1=ALU.mult),
                      reads=[("hbuf", b), skey, ("gain", 0)], writes=[("ubuf", b)])
                transposes_to_uT(ubuf[b], ("ubuf", b), j, b)

            def group_stage(gi):
                g0, N, tl = GROUPS[gi]
                ukeys = [("uT", j) for j in tl]
                fb = gi % 2
                for c in range(5):
                    col = c * 128
                    a_i, a_h = next_hb()
                    b_i, b_h = next_hb()
                    pa = PQ[a_i][:, a_h * 512:a_h * 512 + N]
                    pb = PQ[b_i][:, b_h * 512:b_h * 512 + N]
                    mm_chain(pa, [(WAm[:, k, col:col + 128], uT[:, k, g0:g0 + N]) for k in range(8)],
                             reads=ukeys + [("WAm",)], writes=psk(a_i, a_h))
                    mm_chain(pb, [(WAs[:, k, col:col + 128], uT[:, k, g0:g0 + N]) for k in range(8)],
                             reads=ukeys + [("WAs",)], writes=psk(b_i, b_h))
                    tb = c % 2
                    P.add("vector", lambda e, pa=pa, tb=tb: e.tensor_tensor(out=t1[tb][:, 0:N], in0=pa,
                                                                          in1=cosT[:, g0:g0 + N], op=ALU.mult),
                          reads=psk(a_i, a_h) + [("rope", 0)], writes=[("t1", tb)])
                    P.add("vector", lambda e, pb=pb, tb=tb: e.tensor_tensor(out=t2[tb][:, 0:N], in0=pb,
                                                                          in1=sinT[:, g0:g0 + N], op=ALU.mult),
                          reads=psk(b_i, b_h) + [("rope", 1)], writes=[("t2", tb)])
                    if c < 4:
                        dst = qT[:, c, g0:g0 + N]
                        dkey = ("qT", c, gi)
                    else:
                        dst = kT[:, g0:g0 + N]
                        dkey = ("kT", gi)
                    P.add("vector", lambda e, dst=dst, tb=tb: e.tensor_tensor(out=dst, in0=t1[tb][:, 0:N],
                                                                            in1=t2[tb][:, 0:N], op=ALU.add),
                          reads=[("t1", tb), ("t2", tb)], writes=[dkey])
                    yield
                for c in range(4):
                    col = 640 + c * 128
                    a_i, a_h = next_hb()
                    pa = PQ[a_i][:, a_h * 512:a_h * 512 + N]
                    mm_chain(pa, [(WAm[:, k, col:col + 128], uT[:, k, g0:g0 + N]) for k in range(8)],
                             reads=ukeys + [("WAm",)], writes=psk(a_i, a_h))
                    P.add("scalar", lambda e, pa=pa, c=c: e.activation(out=fT[fb][:, c, 0:N], in_=pa, func=AF.Copy),
                          reads=psk(a_i, a_h), writes=[("fT", fb, c)])
                    yield
                for j in tl:
                    t0, nt = TILES[j]
                    a_i, a_h = next_hb()
                    pv = PQ[a_i][0:nt, a_h * 512:a_h * 512 + 128]
                    mm_chain(pv, [(uT[:, k, t0:t0 + nt], WAm[:, k, 1152:1280]) for k in range(8)],
                             reads=[("uT", j), ("WAm",)], writes=psk(a_i, a_h))
                    P.add("scalar", lambda e, pv=pv, j=j, nt=nt: e.activation(out=VP[0:nt, j, 0, 0:64],
                                                                            in_=pv[:, 0:64], func=AF.Copy),
                          reads=psk(a_i, a_h), writes=[("VP", j)])
                    P.add("vector", lambda e, pv=pv, j=j, nt=nt: e.tensor_copy(out=VP[0:nt, j, 1, 64:128],
                                                                             in_=pv[:, 64:128]),
                          reads=psk(a_i, a_h), writes=[("VP", j)])
                    lo = t0 - g0
                    for hh in range(2):
                        def fn(e, hh=hh, lo=lo, nt=nt):
                            ins = None
                            for gg in (2 * hh, 2 * hh + 1):
                                ins = e.matmul(PQ[2][0:nt, gg * 256:(gg + 1) * 256], lhsT=fT[fb][:, gg, lo:lo + nt],
                                               rhs=cs128[:, :], start=True, stop=True)
                            return ins
                        P.add("tensor", fn, reads=[("fT", fb, 2 * hh), ("fT", fb, 2 * hh + 1), ("cs128",)],
                              writes=psk(2, hh))
                        eng = "scalar" if hh == 0 else "vector"
                        if hh == 0:
                            P.add("scalar", lambda e, j=j, nt=nt: e.activation(out=AB[0:nt, j, 0:512],
                                                                             in_=PQ[2][0:nt, 0:512], func=AF.Copy),
                                  reads=psk(2, 0), writes=[("AB", j, 0)])
                        else:
                            P.add("vector", lambda e, j=j, nt=nt: e.tensor_copy(out=AB[0:nt, j, 512:1024],
                                                                              in_=PQ[2][0:nt, 512:1024]),
                                  reads=psk(2, 1), writes=[("AB", j, 1)])
                    yield

            for j in GROUPS[0][2] + GROUPS[1][2]:
                tile_stage(j)
            for gi in range(len(GROUPS)):
                pend = list(GROUPS[gi + 1][2]) if (gi >= 1 and gi + 1 < len(GROUPS)) else []
                for step, _ in enumerate(group_stage(gi)):
                    if pend and step % 2 == 0:
                        tile_stage(pend.pop(0))
                for j in pend:
                    tile_stage(j)
            return dict(qT=qT, kT=kT, VP=VP, AB=AB)

        def phase_B(s, l, bufs):
            switch("X4")
            switch("X3")
            qT, kT, VP = bufs["qT"], bufs["kT"], bufs["VP"]
            attnT = X3.view(0, [4, LTOK], BF16)
            NPT = 16
            PT = [X4.view(1024 * i, [512], BF16) for i in range(NPT)]
            densb = [X4.view(16384 + 2048 * i, [512], F32) for i in range(2)]
            rec = [X4.view(20480 + 2048 * i, [512], F32) for i in range(2)]
            pt_ctr = [0]
            blocks = list(range(-1, 16))

            def s_stage(bi):
                if bi < 0:
                    q0, nq = 0, 16
                    ktiles = [(0, None), (1, 2)]
                else:
                    q0, nq = 16 + 128 * bi, 128
                    ktiles = [(0, None)]
                    if bi >= 1:
                        ktiles.append((bi, 0))
                    ktiles.append((bi + 1, None))
                    if bi <= 14:
                        ktiles.append((bi + 2, 1))
                jq = bi + 1
                gq = TILE_GROUP[jq]
                N = 4 * nq
                plist = []
                for kv in range(2):
                    r0 = 64 * kv
                    for (jt, mk) in ktiles:
                        kt0, nk = TILES[jt]
                        a_i, a_h = next_hb()
                        ps = PQ[a_i][0:nk, a_h * 512:a_h * 512 + N]
                        ps3 = ps.rearrange("p (c q) -> p c q", c=4)
                        mm_chain(ps3, [(kT[r0:r0 + 64, kt0:kt0 + nk], qT[r0:r0 + 64, :, q0:q0 + nq])],
                                 reads=[("kT", TILE_GROUP[jt])] + [("qT", c, gq) for c in range(4)],
                                 writes=psk(a_i, a_h))
                        pi = pt_ctr[0] % NPT
                        pt_ctr[0] += 1
                        pt = PT[pi][0:nk, 0:N]
                        P.add("scalar", lambda e, pt=pt, ps=ps: e.activation(out=pt, in_=ps, func=AF.Exp, scale=0.125),
                              reads=psk(a_i, a_h), writes=[("PT", pi)])
                        if mk is not None:
                            pt3 = pt.rearrange("p (c q) -> p c q", c=4)
                            mk_ap = masks[0:nk, mk, 0:nq].unsqueeze(1).to_broadcast([nk, 4, nq])
                            P.add("vector", lambda e, pt3=pt3, mk_ap=mk_ap: e.tensor_tensor(out=pt3, in0=pt3, in1=mk_ap,
                                                                                          op=ALU.mult),
                                  reads=[("PT", pi), ("masks",)], writes=[("PT", pi)])
                        plist.append((kv, jt, nk, pi, pt))
                return (bi, q0, nq, N, plist)

            def pv_stage(info, n):
                bi, q0, nq, N, plist = info
                h = n % 2
                po = PQ[2][:, h * 512:h * 512 + N]
                pd = PQ[3][:, h * 512:h * 512 + N]
                rk = [("PT", pi) for (_, _, _, pi, _) in plist]
                mm_chain(po, [(VP[0:nk, jt, kv, :], pt) for (kv, jt, nk, pi, pt) in plist],
                         reads=rk + [("VP", jt) for (_, jt, _, _, _) in plist], writes=psk(2, h))
                mm_chain(pd, [(onesp[0:nk, kv, :], pt) for (kv, jt, nk, pi, pt) in plist],
                         reads=rk + [("onesp",)], writes=psk(3, h))
                d3 = densb[h][:, 0:N].rearrange("p (c q) -> p c q", c=4)
                r3 = rec[h][:, 0:N].rearrange("p (c q) -> p c q", c=4)
                P.add("vector", lambda e: e.tensor_tensor(out=d3, in0=pd.rearrange("p (c q) -> p c q", c=4),
                                                          in1=sinkexp[:, l, :].unsqueeze(2).to_broadcast([128, 4, nq]),
                                                          op=ALU.add),
                      reads=psk(3, h) + [("sinkexp",)], writes=[("densb", h)])
                P.add("vector", lambda e: e.reciprocal(out=rec[h][:, 0:N], in_=densb[h][:, 0:N]),
                      reads=[("densb", h)], writes=[("rec", h)])
                P.add("vector", lambda e: e.tensor_tensor(out=attnT[:, :, q0:q0 + nq],
                                                          in0=po.rearrange("p (c q) -> p c q", c=4), in1=r3,
                                                          op=ALU.mult),
                      reads=psk(2, h) + [("rec", h)], writes=[("attnT", bi + 1)])

            prev = s_stage(blocks[0])
            for n, bi in enumerate(blocks):
                nxt = s_stage(blocks[n + 1]) if n + 1 < len(blocks) else None
                pv_stage(prev, n)
                prev = nxt
            return attnT

        def phase_C(s, l, bufs):
            switch("X4")
            AB = bufs["AB"]
            YT = X3.view(16512, [4, LTOK], BF16)
            CSL = [X4.view(11696 * i, [NTILE, 2, KCW], BF16) for i in range(2)]
            ctr = [0]

            def load(kc):
                cb = kc % 2
                P.add("sync", lambda e: e.dma_start(out=CSL[cb], in_=csl_d[kc]), writes=[("CSL", cb)], dma=True)

            load(0)
            for kc in range(NKC):
                if kc + 1 < NKC:
                    load(kc + 1)
                cb = kc % 2
                for g in range(4):
                    n = ctr[0]
                    ctr[0] += 1
                    a_i, a_h = (n % 8) // 2, n % 2
                    ps = PQ[a_i][:, a_h * 512:a_h * 512 + KCW]
                    pairs = []
                    for j in range(NTILE):
                        t0, nt = TILES[j]
                        pairs.append((AB[0:nt, j, g * 256:g * 256 + 128], CSL[cb][0:nt, j, 0, :]))
                        pairs.append((AB[0:nt, j, g * 256 + 128:g * 256 + 256], CSL[cb][0:nt, j, 1, :]))
                    mm_chain(ps, pairs, reads=[("AB", j, g // 2) for j in range(NTILE)] + [("CSL", cb)],
                             writes=psk(a_i, a_h))
                    dst = YT[:, g, kc * KCW:(kc + 1) * KCW]
                    if n % 2 == 0:
                        P.add("scalar", lambda e, dst=dst, ps=ps: e.activation(out=dst, in_=ps, func=AF.Copy),
                              reads=psk(a_i, a_h), writes=[("YT", g, kc)])
                    else:
                        P.add("vector", lambda e, dst=dst, ps=ps: e.tensor_copy(out=dst, in_=ps),
                              reads=psk(a_i, a_h), writes=[("YT", g, kc)])
            return YT

        def phase_D(s, l, attnT, YT):
            switch("X1")
            switch("X4")
            mixT = X1.view(0, [8, LTOK], BF16)
            Dst = [X1.view(33024 + 6144 * i, [24, 128], BF16) for i in range(2)]
            Wo = X1.view(45312, [8, 1024], BF16)
            sg = [X4.view(4096 * i, [2, 512], F32) for i in range(2)]
            mm = [X4.view(8192 + 4096 * i, [2, 512], F32) for i in range(2)]

            def load(c):
                P.add("gpsimd", lambda e: e.dma_start(out=Dst[c % 2], in_=wdm_d[l, c]), writes=[("Dst", c % 2)],
                      dma=True)

            load(0)
            P.add("gpsimd", lambda e: e.dma_start(out=Wo, in_=wo_d[l]), writes=[("Wo",)], dma=True)
            n = 0
            for c in range(8):
                if c + 1 < 8:
                    load(c + 1)
                W = Dst[c % 2]
                for gi, (g0, N, tl) in enumerate(GROUPS):
                    pa_i = (n % 2) * 2
                    pb_i = pa_i + 1
                    b = n % 2
                    n += 1
                    ukeys = [("uT", j) for j in tl]
                    ykeys = [("YT", g, kc) for g in range(4) for kc in range(g0 // KCW, (g0 + N - 1) // KCW + 1)]
                    akeys = [("attnT", j) for j in tl]
                    mm_chain(PQ[pa_i][:, 0:N], [(W[:, k, :], uT[:, k, g0:g0 + N]) for k in range(8)],
                             reads=ukeys + [("Dst", c % 2)], writes=psk(pa_i, 0))
                    mm_chain(PQ[pa_i][:, 512:512 + N], [(W[:, 8 + k, :], uT[:, k, g0:g0 + N]) for k in range(8)],
                             reads=ukeys + [("Dst", c % 2)], writes=psk(pa_i, 1))
                    mm_chain(PQ[pb_i][:, 0:N], [(W[:, 16 + k, :], YT[:, k, g0:g0 + N]) for k in range(4)],
                             reads=ykeys + [("Dst", c % 2)], writes=psk(pb_i, 0))
                    mm_chain(PQ[pb_i][:, 512:512 + N], [(W[:, 20 + k, :], attnT[:, k, g0:g0 + N]) for k in range(4)],
                             reads=akeys + [("Dst", c % 2)], writes=psk(pb_i, 1))
                    pa3 = PQ[pa_i][:, :].rearrange("p (a b) -> p a b", a=2)[:, :, 0:N]
                    pb3 = PQ[pb_i][:, :].rearrange("p (a b) -> p a b", a=2)[:, :, 0:N]
                    P.add("scalar", lambda e, b=b, pa3=pa3, N=N: e.activation(out=sg[b][:, :, 0:N], in_=pa3,
                                                                            func=AF.Sigmoid),
                          reads=psk(pa_i), writes=[("sg", b)])
                    P.add("vector", lambda e, b=b, pb3=pb3, N=N: e.tensor_tensor(out=mm[b][:, :, 0:N],
                                                                               in0=sg[b][:, :, 0:N], in1=pb3,
                                                                               op=ALU.mult),
                          reads=psk(pb_i) + [("sg", b)], writes=[("mm", b)])
                    P.add("vector", lambda e, b=b, c=c, g0=g0, N=N: e.tensor_tensor(out=mixT[:, c, g0:g0 + N],
                                                                                  in0=mm[b][:, 0, 0:N],
                                                                                  in1=mm[b][:, 1, 0:N], op=ALU.add),
                          reads=[("mm", b)], writes=[("mixT", c, gi)])
            return mixT, Wo

        def phase_E(s, l, mixT, Wo):
            switch("X4")
            hbuf = [X4.view(4096 * i, [1024], F32) for i in range(2)]
            tmp = [X4.view(8192 + 4096 * i, [1024], F32) for i in range(2)]
            hnew = [X4.view(16384 + 4096 * i, [1024], F32) for i in range(2)]
            ubuf = [X4.view(24576 + 2048 * i, [1024], BF16) for i in range(2)]
            junk = X4.view(28672, [1024], BF16)
            gain = [X4.view(30720 + 4096 * i, [1024], F32) for i in range(2)]
            for gi_, idx in ((0, 1), (1, 2)):
                P.add("sync", lambda e, gi_=gi_, idx=idx: e.dma_start(out=gain[gi_],
                                                                      in_=gains_d[l, idx].partition_broadcast(128)),
                      writes=[("gain", gi_)], dma=True)

            def load(j):
                t0, nt = TILES[j]
                src, skeys = h_src(s, l, j)
                P.add("sync", lambda e: e.dma_start(out=hbuf[j % 2][0:nt], in_=src), reads=skeys,
                      writes=[("hbuf", j % 2)], dma=True)

            def stage1(j):
                t0, nt = TILES[j]
                b = j % 2
                gi = TILE_GROUP[j]
                for hh in range(2):
                    mm_chain(PQ[b][0:nt, hh * 512:(hh + 1) * 512],
                             [(mixT[:, k, t0:t0 + nt], Wo[:, k, hh * 512:(hh + 1) * 512]) for k in range(8)],
                             reads=[("mixT", c, gi) for c in range(8)] + [("Wo",)], writes=psk(b, hh))
                rstd, skey = rms_stats(PQ[b][0:nt, :], nt, psk(b), junk[0:nt])
                P.add("vector", lambda e: e.scalar_tensor_tensor(
                    out=tmp[b][0:nt], in0=PQ[b][0:nt, :], scalar=rstd, in1=gain[0][0:nt], op0=ALU.mult, op1=ALU.mult),
                    reads=psk(b) + [skey, ("gain", 0)], writes=[("tmp", b)])
                P.add("vector", lambda e: e.tensor_tensor(out=hnew[b][0:nt], in0=hbuf[b][0:nt],
                                                          in1=tmp[b][0:nt], op=ALU.add),
                      reads=[("hbuf", b), ("tmp", b)], writes=[("hnew", b)])
                P.add("sync", lambda e: e.dma_start(out=hs_d[s, t0:t0 + nt, :], in_=hnew[b][0:nt]),
                      reads=[("hnew", b)], writes=[("hs", s, j)], dma=True)
                rstd2, skey2 = rms_stats(hnew[b][0:nt], nt, [("hnew", b)], junk[0:nt])
                P.add("vector", lambda e: e.scalar_tensor_tensor(
                    out=ubuf[b][0:nt], in0=hnew[b][0:nt], scalar=rstd2, in1=gain[1][0:nt], op0=ALU.mult,
                    op1=ALU.mult),
                    reads=[("hnew", b), skey2, ("gain", 1)], writes=[("ubuf", b)])

            load(0)
            load(1)
            stage1(0)
            for j in range(NTILE):
                if j + 1 < NTILE:
                    stage1(j + 1)
                if j + 2 < NTILE:
                    load(j + 2)
                transposes_to_uT(ubuf[j % 2], ("ubuf", j % 2), j, j % 2)

        def phase_F(s, l, last):
            switch("X1")
            switch("X3")
            switch("X4")
            HT = 1040
            actT = X1.view(0, [NFC, HT], BF16)
            Fst = [X1.view(45760 + 4096 * i, [16, 128], BF16) for i in range(3)]
            Wd = X3.view(0, [NFC, 1024], BF16)
            sg = [X4.view(2048 * i, [512], F32) for i in range(3)]
            hbuf = [X4.view(6144 + 4096 * i, [1024], F32) for i in range(2)]
            tmp = [X4.view(14336 + 4096 * i, [1024], F32) for i in range(2)]
            hnew = [X4.view(22528 + 4096 * i, [1024], F32) for i in range(2)]
            gain = X4.view(30720, [1024], F32)
            junk = X4.view(34816, [1024], BF16)
            P.add("sync", lambda e: e.dma_start(out=gain, in_=gains_d[l, 3].partition_broadcast(128)),
                  writes=[("gain", 0)], dma=True)
            fctr = [0]
            nctr = [0]
            tctr = [0]
            for hf in range(2):
                if hf == 0:
                    h0, tiles, groups = 0, list(range(0, 9)), [0, 1, 2]
                else:
                    h0, tiles, groups = 1040, list(range(9, 17)), [3, 4]

                def load(fc):
                    sl = (fctr[0] + (fc - 0)) % 3
                    P.add("gpsimd", lambda e: e.dma_start(out=Fst[sl], in_=wgu_d[l, fc]), writes=[("Fst", sl)],
                          dma=True)
                    return sl

                slots = {}
                slots[0] = load(0)
                slots[1] = load(1)
                if hf == 0:
                    for part in range(2):
                        f0, f1 = part * 11, (part + 1) * 11
                        P.add("gpsimd", lambda e, f0=f0, f1=f1: e.dma_start(out=Wd[:, f0:f1, :],
                                                                            in_=wdn_d[l, :, f0:f1, :]),
                              writes=[("Wd", part)], dma=True)
                for fc in range(NFC):
                    if fc + 2 < NFC:
                        slots[fc + 2] = load(fc + 2)
                    sl = slots[fc]
                    W = Fst[sl]
                    for gi in groups:
                        g0, N, tl = GROUPS[gi]
                        n = nctr[0]
                        nctr[0] += 1
                        r = n % 4
                        sb_i = n % 3
                        ukeys = [("uT", j) for j in tl]
                        mm_chain(PQ[r][:, 0:N], [(W[:, k, :], uT[:, k, g0:g0 + N]) for k in range(8)],
                                 reads=ukeys + [("Fst", sl)], writes=psk(r, 0))
                        mm_chain(PQ[r][:, 512:512 + N], [(W[:, 8 + k, :], uT[:, k, g0:g0 + N]) for k in range(8)],
                                 reads=ukeys + [("Fst", sl)], writes=psk(r, 1))
                        P.add("scalar", lambda e, r=r, sb_i=sb_i, N=N: e.activation(out=sg[sb_i][:, 0:N],
                                                                                  in_=PQ[r][:, 0:N], func=AF.Silu),
                              reads=psk(r, 0), writes=[("sg", sb_i)])
                        P.add("vector", lambda e, r=r, sb_i=sb_i, N=N, fc=fc, g0=g0, h0=h0: e.tensor_tensor(
                            out=actT[:, fc, g0 - h0:g0 - h0 + N], in0=sg[sb_i][:, 0:N], in1=PQ[r][:, 512:512 + N],
                            op=ALU.mult),
                            reads=psk(r, 1) + [("sg", sb_i)], writes=[("actT", fc, gi)])
                fctr[0] += NFC

                def loadh(j):
                    t0, nt = TILES[j]
                    P.add("sync", lambda e: e.dma_start(out=hbuf[j % 2][0:nt], in_=hs_d[s, t0:t0 + nt, :]),
                          reads=[("hs", s, j)], writes=[("hbuf", j % 2)], dma=True)

                loadh(tiles[0])
                for ti, j in enumerate(tiles):
                    t0, nt = TILES[j]
                    b = j % 2
                    if ti + 1 < len(tiles):
                        loadh(tiles[ti + 1])
                    gi = TILE_GROUP[j]
                    r = tctr[0] % 4
                    tctr[0] += 1
                    lo = t0 - h0
                    for hh in range(2):
                        mm_chain(PQ[r][0:nt, hh * 512:(hh + 1) * 512],
                                 [(actT[:, fc, lo:lo + nt], Wd[:, fc, hh * 512:(hh + 1) * 512]) for fc in range(NFC)],
                                 reads=[("actT", fc, gi) for fc in range(NFC)] + [("Wd", 0), ("Wd", 1)],
                                 writes=psk(r, hh))
                    rstd, skey = rms_stats(PQ[r][0:nt, :], nt, psk(r), junk[0:nt])
                    P.add("vector", lambda e, b=b, nt=nt, rstd=rstd, r=r: e.scalar_tensor_tensor(
                        out=tmp[b][0:nt], in0=PQ[r][0:nt, :], scalar=rstd, in1=gain[0:nt], op0=ALU.mult,
                        op1=ALU.mult),
                        reads=psk(r) + [skey, ("gain", 0)], writes=[("tmp", b)])
                    P.add("vector", lambda e, b=b, nt=nt: e.tensor_tensor(out=hnew[b][0:nt], in0=hbuf[b][0:nt],
                                                                         in1=tmp[b][0:nt], op=ALU.add),
                          reads=[("hbuf", b), ("tmp", b)], writes=[("hnew", b)])
                    if last:
                        if j >= 1:
                            P.add("sync", lambda e, b=b, j=j: e.dma_start(out=out_d[s, 128 * (j - 1):128 * j, :],
                                                                          in_=hnew[b][:, :]),
                                  reads=[("hnew", b)], writes=[("out", s, j)], dma=True)
                    else:
                        P.add("sync", lambda e, b=b, nt=nt, t0=t0: e.dma_start(out=hs_d[s, t0:t0 + nt, :],
                                                                              in_=hnew[b][0:nt]),
                              reads=[("hnew", b)], writes=[("hs", s, j)], dma=True)

        def dump(name, ap, shape, dt):
            d = nc.dram_tensor("dbg_" + name, shape, dt, kind="ExternalOutput").ap()
            dbg_outs[name] = d
            P.add("sync", lambda e: e.dma_start(out=d, in_=ap), reads=list(P.last_writer.keys()), dma=True)

        done = False
        for s in range(nseq):
            for l in range(nlayers):
                bufs = phase_A(s, l)
                attnT = phase_B(s, l, bufs)
                YT = phase_C(s, l, bufs)
                if stop == "C":
                    dump("qT", bufs["qT"], [128, 4, LTOK], BF16)
                    dump("kT", bufs["kT"], [128, LTOK], BF16)
                    dump("VP", bufs["VP"], [128, NTILE, 2, 128], BF16)
                    dump("AB", bufs["AB"], [128, NTILE, 1024], BF16)
                    dump("uT", uT, [128, 8, LTOK], BF16)
                    dump("attnT", attnT, [128, 4, LTOK], BF16)
                    dump("YT", YT, [128, 4, LTOK], BF16)
                    done = True
                    break
                mixT, Wo = phase_D(s, l, attnT, YT)
                if stop == "D":
                    dump("mixT", mixT, [128, 8, LTOK], BF16)
                    done = True
                    break
                phase_E(s, l, mixT, Wo)
                if stop == "E":
                    dump("uT", uT, [128, 8, LTOK], BF16)
                    dump("hs", hs_d[s], [LTOK, D_MODEL], F32)
                    done = True
                    break
                phase_F(s, l, last=(l == nlayers - 1) and stop is None)
                if stop == "F":
                    dump("hs", hs_d[s], [LTOK, D_MODEL], F32)
                    done = True
                    break
            if done:
                break
        P.emit()
    return nc


def _q_perm():
    idx = np.empty(512, np.int64)
    for c in range(4):
        for half in range(2):
            for d in range(64):
                idx[c * 128 + half * 64 + d] = (c + 4 * half) * 64 + d
    return idx


def _swap64(n):
    idx = np.arange(n)
    return (idx // 64) * 64 + ((idx % 64) + 32) % 64


def _constants():
    bf = ml_dtypes.bfloat16
    c = {}
    inv_freq = (10000.0 ** (-(np.arange(0, 64, 2, dtype=np.float32)) / np.float32(64))).astype(np.float32)
    ang = (np.arange(LTOK, dtype=np.float32)[:, None] * inv_freq[None, :]).astype(np.float32)
    cos = np.cos(ang.astype(np.float64))
    sin = np.sin(ang.astype(np.float64))
    p = np.arange(128)
    d = p % 64
    fi = d % 32
    sign = np.where(d < 32, -1.0, 1.0)
    c["rcos"] = np.ascontiguousarray(cos[:, fi].T).astype(np.float32)
    c["rsin"] = np.ascontiguousarray((sin[:, fi] * sign[None, :]).T).astype(np.float32)
    cc = np.arange(128)
    a128 = 2 * np.pi * ((cc[:, None] * cc[None, :]) % 128) / 128.0
    c["cs128"] = np.concatenate([np.cos(a128), -np.sin(a128)], axis=1).astype(np.float64) / np.sqrt(128.0)
    c["cs128"] = c["cs128"].astype(bf)
    n_of = np.zeros((128, NTILE), np.int64)
    valid = np.zeros((128, NTILE), bool)
    for j, (t0, nt) in enumerate(TILES):
        n_of[:nt, j] = t0 + np.arange(nt)
        valid[:nt, j] = True
    k_all = np.arange(LTOK).reshape(NKC, KCW)
    csl = np.zeros((NKC, 128, NTILE, 2, KCW), np.float32)
    for kc in range(NKC):
        r = (n_of[:, :, None] * k_all[kc][None, None, :]) % LTOK
        a = 2 * np.pi * r / LTOK
        csl[kc, :, :, 0, :] = np.cos(a) / np.sqrt(LTOK) * valid[:, :, None]
        csl[kc, :, :, 1, :] = np.sin(a) / np.sqrt(LTOK) * valid[:, :, None]
    c["csl"] = csl.astype(bf)
    jj = np.arange(128)[:, None]
    ss = np.arange(128)[None, :]
    m = np.zeros((128, 3, 128), np.float32)
    m[:, 0, :] = (jj >= ss)
    m[:, 1, :] = (jj <= ss)
    m[:, 2, :] = (jj <= 112 + ss)
    c["masks"] = m.astype(bf)
    c["ident"] = np.eye(128, dtype=np.float32).astype(bf)
    o = np.zeros((128, 2, 128), np.float32)
    o[:, 0, 0:64] = 1.0
    o[:, 1, 64:128] = 1.0
    c["onesp"] = o.astype(bf)
    return c


def _prep_weights(w_in, w_fourier_out, w_attn_out, w_o, sink_logits, norm_mix_pre, norm_mix_post,
                  norm_ffn_pre, norm_ffn_post, w_ffn_gate, w_ffn_up, w_ffn_down):
    f32 = np.float32
    qp = _q_perm()
    w = {}
    wq = w_in[:, :, 0:512][:, :, qp]
    wk = w_in[:, :, 512:640]
    wv = w_in[:, :, 640:768]
    wf = w_in[:, :, 768:1280]
    main = np.concatenate([wq, wk, wf, wv], axis=2)
    swp = np.concatenate([wq[:, :, _swap64(512)], wk[:, :, _swap64(128)]], axis=2)
    w["wa"] = np.ascontiguousarray(main.reshape(DEPTH, 8, 128, 1280).transpose(0, 2, 1, 3), dtype=f32)
    w["was"] = np.ascontiguousarray(swp.reshape(DEPTH, 8, 128, 640).transpose(0, 2, 1, 3), dtype=f32)
    wgf = w_in[:, :, 1280:2304].reshape(DEPTH, 8, 128, 8, 128)
    wga = w_in[:, :, 2304:3328].reshape(DEPTH, 8, 128, 8, 128)
    wfo = w_fourier_out.reshape(DEPTH, 4, 128, 8, 128)
    rows = np.empty(512, np.int64)
    for kc in range(4):
        for p_ in range(128):
            head = kc if p_ < 64 else 4 + kc
            rows[kc * 128 + p_] = head * 64 + (p_ % 64)
    wao = w_attn_out[:, rows, :].reshape(DEPTH, 4, 128, 8, 128)
    pack = np.concatenate([wgf, wga, wfo, wao], axis=1)
    w["wdm"] = np.ascontiguousarray(pack.transpose(0, 3, 2, 1, 4), dtype=f32)
    w["wo"] = np.ascontiguousarray(w_o.reshape(DEPTH, 8, 128, 1024).transpose(0, 2, 1, 3), dtype=f32)
    g_ = w_ffn_gate.reshape(DEPTH, 8, 128, NFC, 128)
    u_ = w_ffn_up.reshape(DEPTH, 8, 128, NFC, 128)
    gu = np.concatenate([g_, u_], axis=1)
    w["wgu"] = np.ascontiguousarray(gu.transpose(0, 3, 2, 1, 4), dtype=f32)
    w["wdn"] = np.ascontiguousarray(w_ffn_down.reshape(DEPTH, NFC, 128, 1024).transpose(0, 2, 1, 3), dtype=f32)
    w["gains"] = np.ascontiguousarray(np.stack([norm_mix_pre, norm_mix_post, norm_ffn_pre, norm_ffn_post], axis=1),
                                      dtype=f32)
    sk = np.empty((DEPTH, 128, 4), f32)
    sk[:, 0:64, :] = sink_logits[:, None, 0:4]
    sk[:, 64:128, :] = sink_logits[:, None, 4:8]
    w["sink"] = sk
    return w


_NC_CACHE = {}


def kernel(x, meta_tokens, w_in, w_fourier_out, w_attn_out, w_o, sink_logits, norm_mix_pre, norm_mix_post,
           norm_ffn_pre, norm_ffn_post, w_ffn_gate, w_ffn_up, w_ffn_down):
    args = [np.asarray(a, dtype=np.float32) for a in (w_in, w_fourier_out, w_attn_out, w_o, sink_logits,
                                                      norm_mix_pre, norm_mix_post, norm_ffn_pre, norm_ffn_post,
                                                      w_ffn_gate, w_ffn_up, w_ffn_down)]
    x = np.asarray(x, dtype=np.float32)
    shared = _prep_weights(*args)
    shared.update(_constants())
    shared["meta"] = np.ascontiguousarray(np.asarray(meta_tokens, dtype=np.float32))
    if "nc" not in _NC_CACHE:
        _NC_CACHE["nc"] = build_program()
    nc = _NC_CACHE["nc"]
    in_maps = []
    for c in range(8):
        m = dict(shared)
        m["x"] = np.ascontiguousarray(x[2 * c:2 * c + 2])
        in_maps.append(m)
    res = run_bass_kernel_spmd(nc, in_maps, core_ids=list(range(8)))
    out = np.concatenate([np.asarray(r["out"]) for r in res.results], axis=0)
    return out.astype(np.float32)
```

```python
import contextlib
import numpy as np
import ml_dtypes
import concourse.bass as bass
import concourse.mybir as mybir
from concourse.bass_utils import run_bass_kernel_spmd

F32 = mybir.dt.float32
BF16 = mybir.dt.bfloat16
AF = mybir.ActivationFunctionType
ALU = mybir.AluOpType

D_MODEL = 1024
SEQ = 2048
DEPTH = 2
N_META = 16
LTOK = N_META + SEQ
NTILE = 17
D_FF = 2816
NFC = D_FF // 128
EPS = 1e-6
NKC = 6
KCW = 173
KHALF = LTOK // 2

TILES = [(0, 16)] + [(16 + 128 * i, 128) for i in range(16)]
GROUPS = [(0, 16, [0])] + [(16 + 512 * g, 512, [1 + 4 * g + i for i in range(4)]) for g in range(4)]
TILE_GROUP = {}
for _gi, (_g0, _n, _tl) in enumerate(GROUPS):
    for _j in _tl:
        TILE_GROUP[_j] = _gi


class _Op:
    __slots__ = ("eng", "fn", "deps", "flag", "semval", "sem", "is_dma")


class Prog:
    ENGS = ("sync", "tensor", "vector", "scalar", "gpsimd")

    def __init__(self, nc, n_dma_sems=32):
        self.nc = nc
        self.ops = {e: [] for e in self.ENGS}
        self.last_writer = {}
        self.readers = {}
        self.n_dma_sems = n_dma_sems
        self.dma_count = 0
        self.dma_last = [None] * n_dma_sems
        self.dma_val = [0] * n_dma_sems
        self.region_of = {}

    def _expand(self, keys):
        out = []
        regs = set()
        for k in keys:
            out.append(k)
            r = self.region_of.get(k[0])
            if r is not None:
                regs.add(("reg", r))
        return out, regs

    def add(self, eng, fn, reads=(), writes=(), dma=False, regs=()):
        op = _Op()
        op.eng = eng
        op.fn = fn
        op.flag = False
        op.semval = 0
        op.sem = None
        op.is_dma = dma
        reads, r1 = self._expand(reads)
        writes, r2 = self._expand(writes)
        rset = r1 | r2 | {("reg", r) for r in regs}
        reads = list(reads) + [r for r in rset if r not in writes]
        deps = set()
        for r in reads:
            w = self.last_writer.get(r)
            if w is not None:
                deps.add(w)
        for r in writes:
            w = self.last_writer.get(r)
            if w is not None:
                deps.add(w)
            for rd in self.readers.get(r, ()):
                deps.add(rd)
        for r in reads:
            self.readers.setdefault(r, []).append(op)
        for r in writes:
            self.last_writer[r] = op
            self.readers[r] = []
        if dma:
            k = self.dma_count % self.n_dma_sems
            self.dma_count += 1
            prev = self.dma_last[k]
            if prev is not None:
                deps.add(prev)
            self.dma_last[k] = op
            self.dma_val[k] += 16
            op.sem = k
            op.semval = self.dma_val[k]
        deps.discard(op)
        if eng == "tensor":
            deps = {d for d in deps if not (d.eng == "tensor" and not d.is_dma)}
        op.deps = deps
        self.ops[eng].append(op)
        return op

    def emit(self, final_wait_eng="sync"):
        nc = self.nc
        for e in self.ENGS:
            for op in self.ops[e]:
                for d in op.deps:
                    if not d.is_dma:
                        d.flag = True
        for e in self.ENGS:
            c = 0
            for op in self.ops[e]:
                if op.flag and not op.is_dma:
                    c += 1
                    op.semval = c
        with contextlib.ExitStack() as st:
            engsem = {e: st.enter_context(nc.semaphore("s_" + e)) for e in self.ENGS}
            dmasem = [st.enter_context(nc.semaphore("d%d" % i)) for i in range(self.n_dma_sems)]
            block = st.enter_context(nc.Block())

            def semof(d):
                if d.is_dma:
                    return ("d", d.sem), dmasem[d.sem], d.semval
                return ("e", d.eng), engsem[d.eng], d.semval

            def make_body(ename):
                def body(e):
                    waited = {}
                    for op in self.ops[ename]:
                        need = {}
                        for d in op.deps:
                            key, sem, val = semof(d)
                            if waited.get(key, 0) < val and need.get(key, (None, 0))[1] < val:
                                need[key] = (sem, val)
                        for key, (sem, val) in need.items():
                            e.wait_ge(sem, val)
                            waited[key] = val
                        ins = op.fn(e)
                        if op.is_dma:
                            ins.then_inc(dmasem[op.sem], 16)
                        elif op.flag:
                            ins.then_inc(engsem[ename], 1)
                    if ename == final_wait_eng:
                        for k in range(self.n_dma_sems):
                            if self.dma_val[k] > waited.get(("d", k), 0):
                                e.wait_ge(dmasem[k], self.dma_val[k])
                return body

            for ename in self.ENGS:
                if self.ops[ename] or ename == final_wait_eng:
                    getattr(block, ename)(make_body(ename))


class Region:
    def __init__(self, nc, st, name, nbytes):
        self.name = name
        self.nbytes = nbytes
        self.t = st.enter_context(nc.sbuf_tensor(name, [128, nbytes // 2], BF16))

    def view(self, off, free_shape, dtype):
        esz = 2 if dtype == BF16 else 4
        n = 1
        for d in free_shape:
            n *= d
        assert off % 4 == 0 and off + n * esz <= self.nbytes, (self.name, off, free_shape)
        a = self.t[:, off // 2: off // 2 + n * esz // 2]
        if dtype != BF16:
            a = a.bitcast(dtype)
        if len(free_shape) == 2:
            a = a.rearrange("p (a b) -> p a b", a=free_shape[0])
        elif len(free_shape) == 3:
            a = a.rearrange("p (a b c) -> p a b c", a=free_shape[0], b=free_shape[1])
        return a


def build_program(nseq=2, nlayers=DEPTH, stop=None, dbg=False):
    nc = bass.Bass("TRN2", target_bir_lowering=False)

    def din(name, shape, dt):
        return nc.dram_tensor(name, shape, dt, kind="ExternalInput").ap()

    x_d = din("x", [2, SEQ, D_MODEL], F32)
    meta_d = din("meta", [N_META, D_MODEL], F32)
    wa_d = din("wa", [DEPTH, 128, 8, 1280], F32)
    was_d = din("was", [DEPTH, 128, 8, 640], F32)
    wdm_d = din("wdm", [DEPTH, 8, 128, 24, 128], F32)
    wo_d = din("wo", [DEPTH, 128, 8, 1024], F32)
    wgu_d = din("wgu", [DEPTH, NFC, 128, 16, 128], F32)
    wdn_d = din("wdn", [DEPTH, 128, NFC, 1024], F32)
    gains_d = din("gains", [DEPTH, 4, D_MODEL], F32)
    sink_d = din("sink", [DEPTH, 128, 4], F32)
    cos_d = din("rcos", [128, LTOK], F32)
    sin_d = din("rsin", [128, LTOK], F32)
    cs128_d = din("cs128", [128, 256], BF16)
    csl_d = din("csl", [NKC, 128, NTILE, 2, KCW], BF16)
    masks_d = din("masks", [128, 3, 128], BF16)
    ident_d = din("ident", [128, 128], BF16)
    ones_d = din("onesp", [128, 2, 128], BF16)
    out_d = nc.dram_tensor("out", [2, SEQ, D_MODEL], F32, kind="ExternalOutput").ap()
    hs_d = nc.dram_tensor("hs", [2, LTOK, D_MODEL], F32, kind="Internal").ap()
    dbg_outs = {}

    with contextlib.ExitStack() as st:
        X1 = Region(nc, st, "X1", 64256)
        X2 = Region(nc, st, "X2", 33024)
        X3 = Region(nc, st, "X3", 45056)
        X4 = Region(nc, st, "X4", 49152)

        def sb(name, shape, dt):
            return st.enter_context(nc.sbuf_tensor("sb_" + name, shape, dt))

        ident = sb("ident", [128, 128], BF16)
        cs128 = sb("cs128", [128, 256], BF16)
        masks = sb("masks", [128, 3, 128], BF16)
        onesp = sb("onesp", [128, 2, 128], BF16)
        epsb = sb("epsb", [128, 1], F32)
        sinkraw = sb("sinkraw", [128, DEPTH, 4], F32)
        sinkexp = sb("sinkexp", [128, DEPTH, 4], F32)
        NSTAT = nseq * nlayers * NTILE * 4 * 3
        stats = sb("stats", [128, NSTAT], F32)
        swd = sb("swd", [128, 8], F32)
        PQ = [st.enter_context(nc.psum_tensor("pq%d" % i, [128, 1024], F32)) for i in range(4)]

        P = Prog(nc)
        for nm in ("qT", "kT", "VP", "AB", "mixT", "Dst", "Wo", "actT", "Fst"):
            P.region_of[nm] = "X1"
        P.region_of["uT"] = "X2"
        for nm in ("attnT", "YT", "YTm", "WAm", "rope", "Wd"):
            P.region_of[nm] = "X3"
        for nm in ("WAs", "t1", "t2", "fT", "gain", "hbuf", "ubuf", "PT", "densb", "rec", "CSL", "sg", "mm",
                   "tmp", "hnew"):
            P.region_of[nm] = "X4"

        uT = X2.view(0, [8, LTOK], BF16)

        def switch(reg):
            P.add("vector", lambda e: e.memset(swd[:, 0:1], 0.0), writes=[("reg", reg)])

        def psk(i, h=None):
            if h is None:
                return [("ps", i, 0), ("ps", i, 1)]
            return [("ps", i, h)]

        stat_ctr = [0]

        def new_stat():
            c = stat_ctr[0]
            stat_ctr[0] += 3
            assert c + 3 <= NSTAT
            return c

        def mm_chain(out_ap, pairs, reads, writes):
            def fn(e):
                n = len(pairs)
                ins = None
                for i, (l_, r_) in enumerate(pairs):
                    ins = e.matmul(out_ap, lhsT=l_, rhs=r_, start=(i == 0), stop=(i == n - 1))
                return ins
            P.add("tensor", fn, reads=reads, writes=writes)

        P.add("sync", lambda e: e.dma_start(out=ident[:], in_=ident_d), writes=[("ident",)], dma=True)
        P.add("sync", lambda e: e.dma_start(out=cs128[:], in_=cs128_d), writes=[("cs128",)], dma=True)
        P.add("sync", lambda e: e.dma_start(out=masks[:], in_=masks_d), writes=[("masks",)], dma=True)
        P.add("sync", lambda e: e.dma_start(out=onesp[:], in_=ones_d), writes=[("onesp",)], dma=True)
        P.add("sync", lambda e: e.dma_start(out=sinkraw[:], in_=sink_d.rearrange("l p c -> p l c")),
              writes=[("sinkraw",)], dma=True)
        P.add("vector", lambda e: e.memset(epsb[:], EPS), writes=[("eps",)])
        P.add("vector", lambda e: e.memset(stats[:], 0.0), writes=[("stats",)])
        P.add("scalar", lambda e: e.activation(out=sinkexp[:], in_=sinkraw[:], func=AF.Exp),
              reads=[("sinkraw",)], writes=[("sinkexp",)])

        hb_ctr = [0]

        def next_hb():
            n = hb_ctr[0] % 4
            hb_ctr[0] += 1
            return n // 2, n % 2

        def rms_stats(src_ap, nt, src_keys, junk_ap):
            c = new_stat()
            key = ("st", c)
            P.add("scalar", lambda e: e.activation(out=junk_ap, in_=src_ap, func=AF.Square,
                                                   accum_out=stats[0:nt, c:c + 1]),
                  reads=list(src_keys) + [("stats",)], writes=[key], regs=["X4"])
            P.add("scalar", lambda e: e.activation(out=stats[0:nt, c + 1:c + 2], in_=stats[0:nt, c:c + 1],
                                                   func=AF.Sqrt, scale=1.0 / D_MODEL, bias=epsb[0:nt, 0:1]),
                  reads=[key, ("eps",)], writes=[key])
            P.add("vector", lambda e: e.reciprocal(out=stats[0:nt, c + 2:c + 3], in_=stats[0:nt, c + 1:c + 2]),
                  reads=[key], writes=[key])
            return stats[0:nt, c + 2:c + 3], key

        def transposes_to_uT(ub_ap, ub_key, j, b):
            t0, nt = TILES[j]
            pT = PQ[3][:, b * 512:(b + 1) * 512].bitcast(BF16).rearrange("p (k t) -> p k t", k=8)

            def fn(e):
                ins = None
                for k in range(8):
                    ins = e.transpose(pT[:, k, 0:nt], ub_ap[0:nt, k * 128:(k + 1) * 128], ident[0:nt, 0:nt])
                return ins
            P.add("tensor", fn, reads=[ub_key, ("ident",)], writes=psk(3, b))
            P.add("scalar", lambda e: e.activation(out=uT[:, :, t0:t0 + nt], in_=pT[:, :, 0:nt], func=AF.Copy),
                  reads=psk(3, b), writes=[("uT", j)])

        def h_src(s, l, j):
            t0, nt = TILES[j]
            if l == 0:
                if j == 0:
                    return meta_d, []
                return x_d[s, 128 * (j - 1):128 * j, :], []
            return hs_d[s, t0:t0 + nt, :], [("hs", s, j)]

        def phase_A(s, l):
            switch("X1")
            switch("X3")
            switch("X4")
            WAm = X3.view(0, [8, 1280], BF16)
            cosT = X3.view(20480, [LTOK], F32)
            sinT = X3.view(20480 + 8256, [LTOK], F32)
            WAs = X4.view(0, [8, 640], BF16)
            t1 = [X4.view(10240 + 2048 * i, [512], F32) for i in range(2)]
            t2 = [X4.view(14336 + 2048 * i, [512], F32) for i in range(2)]
            fT = [X4.view(18432 + 4096 * i, [4, 512], BF16) for i in range(2)]
            gain = X4.view(26624, [1024], F32)
            hbuf = [X4.view(30720 + 4096 * i, [1024], F32) for i in range(2)]
            ubuf = [X4.view(38912 + 2048 * i, [1024], BF16) for i in range(2)]
            junk = X4.view(43008, [1024], BF16)
            qT = X1.view(0, [4, LTOK], BF16)
            kT = X1.view(16512, [LTOK], BF16)
            VP = X1.view(20640, [NTILE, 2, 128], BF16)
            AB = X1.view(29344, [NTILE, 1024], BF16)

            P.add("gpsimd", lambda e: e.dma_start(out=WAm[:, :, 0:640], in_=wa_d[l, :, :, 0:640]),
                  writes=[("WAm", 0)], dma=True)
            P.add("gpsimd", lambda e: e.dma_start(out=WAs, in_=was_d[l]), writes=[("WAs",)], dma=True)
            P.add("gpsimd", lambda e: e.dma_start(out=WAm[:, :, 640:1280], in_=wa_d[l, :, :, 640:1280]),
                  writes=[("WAm", 1)], dma=True)
            P.add("sync", lambda e: e.dma_start(out=cosT, in_=cos_d), writes=[("rope", 0)], dma=True)
            P.add("sync", lambda e: e.dma_start(out=sinT, in_=sin_d), writes=[("rope", 1)], dma=True)
            P.add("sync", lambda e: e.dma_start(out=gain, in_=gains_d[l, 0].partition_broadcast(128)),
                  writes=[("gain", 0)], dma=True)
            VPflat = X1.view(20640, [NTILE * 2 * 128], BF16)
            P.add("vector", lambda e: e.memset(VPflat, 0.0), writes=[("VP", j) for j in range(NTILE)])

            def tile_p1(j):
                t0, nt = TILES[j]
                b = j % 2
                src, skeys = h_src(s, l, j)
                P.add("sync", lambda e: e.dma_start(out=hbuf[b][0:nt], in_=src), reads=skeys,
                      writes=[("hbuf", b)], dma=True)
                rstd, skey = rms_stats(hbuf[b][0:nt], nt, [("hbuf", b)], junk[0:nt])
                P.add("vector", lambda e: e.scalar_tensor_tensor(out=ubuf[b][0:nt], in0=hbuf[b][0:nt], scalar=rstd,
                                                                 in1=gain[0:nt], op0=ALU.mult, op1=ALU.mult),
                      reads=[("hbuf", b), skey, ("gain", 0)], writes=[("ubuf", b)])

            def tile_p2(j):
                transposes_to_uT(ubuf[j % 2], ("ubuf", j % 2), j, j % 2)

            def group_stage(gi):
                g0, N, tl = GROUPS[gi]
                ukeys = [("uT", j) for j in tl]
                fb = gi % 2
                for c in range(5):
                    col = c * 128
                    a_i, a_h = next_hb()
                    b_i, b_h = next_hb()
                    pa = PQ[a_i][:, a_h * 512:a_h * 512 + N]
                    pb = PQ[b_i][:, b_h * 512:b_h * 512 + N]
                    mm_chain(pa, [(WAm[:, k, col:col + 128], uT[:, k, g0:g0 + N]) for k in range(8)],
                             reads=ukeys + [("WAm", 0)], writes=psk(a_i, a_h))
                    mm_chain(pb, [(WAs[:, k, col:col + 128], uT[:, k, g0:g0 + N]) for k in range(8)],
                             reads=ukeys + [("WAs",)], writes=psk(b_i, b_h))
                    tb = c % 2
                    P.add("vector", lambda e, pa=pa, tb=tb: e.tensor_tensor(out=t1[tb][:, 0:N], in0=pa,
                                                                          in1=cosT[:, g0:g0 + N], op=ALU.mult),
                          reads=psk(a_i, a_h) + [("rope", 0)], writes=[("t1", tb)])
                    P.add("vector", lambda e, pb=pb, tb=tb: e.tensor_tensor(out=t2[tb][:, 0:N], in0=pb,
                                                                          in1=sinT[:, g0:g0 + N], op=ALU.mult),
                          reads=psk(b_i, b_h) + [("rope", 1)], writes=[("t2", tb)])
                    if c < 4:
                        dst = qT[:, c, g0:g0 + N]
                        dkey = ("qT", c, gi)
                    else:
                        dst = kT[:, g0:g0 + N]
                        dkey = ("kT", gi)
                    P.add("vector", lambda e, dst=dst, tb=tb: e.tensor_tensor(out=dst, in0=t1[tb][:, 0:N],
                                                                            in1=t2[tb][:, 0:N], op=ALU.add),
                          reads=[("t1", tb), ("t2", tb)], writes=[dkey])
                    yield
                for c in range(4):
                    col = 640 + c * 128
                    a_i, a_h = next_hb()
                    pa = PQ[a_i][:, a_h * 512:a_h * 512 + N]
                    mm_chain(pa, [(WAm[:, k, col:col + 128], uT[:, k, g0:g0 + N]) for k in range(8)],
                             reads=ukeys + [("WAm", 1)], writes=psk(a_i, a_h))
                    P.add("scalar", lambda e, pa=pa, c=c: e.activation(out=fT[fb][:, c, 0:N], in_=pa, func=AF.Copy),
                          reads=psk(a_i, a_h), writes=[("fT", fb, c)])
                    yield
                for j in tl:
                    t0, nt = TILES[j]
                    a_i, a_h = next_hb()
                    pv = PQ[a_i][0:nt, a_h * 512:a_h * 512 + 128]
                    mm_chain(pv, [(uT[:, k, t0:t0 + nt], WAm[:, k, 1152:1280]) for k in range(8)],
                             reads=[("uT", j), ("WAm", 1)], writes=psk(a_i, a_h))
                    P.add("scalar", lambda e, pv=pv, j=j, nt=nt: e.activation(out=VP[0:nt, j, 0, 0:64],
                                                                            in_=pv[:, 0:64], func=AF.Copy),
                          reads=psk(a_i, a_h), writes=[("VP", j)])
                    P.add("vector", lambda e, pv=pv, j=j, nt=nt: e.tensor_copy(out=VP[0:nt, j, 1, 64:128],
                                                                             in_=pv[:, 64:128]),
                          reads=psk(a_i, a_h), writes=[("VP", j)])
                    lo = t0 - g0
                    for hh in range(2):
                        def fn(e, hh=hh, lo=lo, nt=nt):
                            ins = None
                            for gg in (2 * hh, 2 * hh + 1):
                                ins = e.matmul(PQ[2][0:nt, gg * 256:(gg + 1) * 256], lhsT=fT[fb][:, gg, lo:lo + nt],
                                               rhs=cs128[:, :], start=True, stop=True)
                            return ins
                        P.add("tensor", fn, reads=[("fT", fb, 2 * hh), ("fT", fb, 2 * hh + 1), ("cs128",)],
                              writes=psk(2, hh))
                        eng = "scalar" if hh == 0 else "vector"
                        if hh == 0:
                            P.add("scalar", lambda e, j=j, nt=nt: e.activation(out=AB[0:nt, j, 0:512],
                                                                             in_=PQ[2][0:nt, 0:512], func=AF.Copy),
                                  reads=psk(2, 0), writes=[("AB", j, 0)])
                        else:
                            P.add("vector", lambda e, j=j, nt=nt: e.tensor_copy(out=AB[0:nt, j, 512:1024],
                                                                              in_=PQ[2][0:nt, 512:1024]),
                                  reads=psk(2, 1), writes=[("AB", j, 1)])
                    yield

            def tile_seq(tl):
                acts = []
                for i, j in enumerate(tl):
                    acts.append([("p1", j)] + ([("p2", tl[i - 1])] if i > 0 else []))
                acts.append([("p2", tl[-1])])
                return acts

            def run_acts(a):
                for kind, j in a:
                    (tile_p1 if kind == "p1" else tile_p2)(j)

            for a in tile_seq(GROUPS[0][2] + GROUPS[1][2]):
                run_acts(a)
            for gi in range(len(GROUPS)):
                pend = tile_seq(list(GROUPS[gi + 1][2])) if (gi >= 1 and gi + 1 < len(GROUPS)) else []
                for step, _ in enumerate(group_stage(gi)):
                    if pend and step % 2 == 0:
                        run_acts(pend.pop(0))
                for a in pend:
                    run_acts(a)
            return dict(qT=qT, kT=kT, VP=VP, AB=AB)

        def phase_B(s, l, bufs):
            switch("X4")
            switch("X3")
            qT, kT, VP = bufs["qT"], bufs["kT"], bufs["VP"]
            attnT = X3.view(0, [4, LTOK], BF16)
            NPT = 16
            PT = [X4.view(1024 * i, [512], BF16) for i in range(NPT)]
            densb = [X4.view(16384 + 2048 * i, [512], F32) for i in range(2)]
            rec = [X4.view(20480 + 2048 * i, [512], F32) for i in range(2)]
            CSL = [X4.view(24576 + 11764 * i, [NTILE, 2, KCW], BF16) for i in range(2)]
            bufs["CSL"] = CSL
            for kc in range(2):
                P.add("sync", lambda e, kc=kc: e.dma_start(out=CSL[kc], in_=csl_d[kc]), writes=[("CSL", kc)], dma=True)
            pt_ctr = [0]
            blocks = list(range(-1, 16))

            def s_stage(bi):
                if bi < 0:
                    q0, nq = 0, 16
                    ktiles = [(0, None), (1, 2)]
                else:
                    q0, nq = 16 + 128 * bi, 128
                    ktiles = [(0, None)]
                    if bi >= 1:
                        ktiles.append((bi, 0))
                    ktiles.append((bi + 1, None))
                    if bi <= 14:
                        ktiles.append((bi + 2, 1))
                jq = bi + 1
                gq = TILE_GROUP[jq]
                N = 4 * nq
                plist = []
                for (jt, mk) in ktiles:
                    for kv in range(2):
                        r0 = 64 * kv
                        kt0, nk = TILES[jt]
                        a_i, a_h = next_hb()
                        ps = PQ[a_i][0:nk, a_h * 512:a_h * 512 + N]
                        ps3 = ps.rearrange("p (c q) -> p c q", c=4)
                        mm_chain(ps3, [(kT[r0:r0 + 64, kt0:kt0 + nk], qT[r0:r0 + 64, :, q0:q0 + nq])],
                                 reads=[("kT", TILE_GROUP[jt])] + [("qT", c, gq) for c in range(4)],
                                 writes=psk(a_i, a_h))
                        pi = pt_ctr[0] % NPT
                        pt_ctr[0] += 1
                        pt = PT[pi][0:nk, 0:N]
                        P.add("scalar", lambda e, pt=pt, ps=ps: e.activation(out=pt, in_=ps, func=AF.Exp, scale=0.125),
                              reads=psk(a_i, a_h), writes=[("PT", pi)])
                        if mk is not None:
                            pt3 = pt.rearrange("p (c q) -> p c q", c=4)
                            mk_ap = masks[0:nk, mk, 0:nq].unsqueeze(1).to_broadcast([nk, 4, nq])
                            P.add("vector", lambda e, pt3=pt3, mk_ap=mk_ap: e.tensor_tensor(out=pt3, in0=pt3, in1=mk_ap,
                                                                                          op=ALU.mult),
                                  reads=[("PT", pi), ("masks",)], writes=[("PT", pi)])
                        plist.append((kv, jt, nk, pi, pt))
                return (bi, q0, nq, N, plist)

            def pv_stage(info, n):
                bi, q0, nq, N, plist = info
                h = n % 2
                po = PQ[2][:, h * 512:h * 512 + N]
                pd = PQ[3][:, h * 512:h * 512 + N]
                rk = [("PT", pi) for (_, _, _, pi, _) in plist]
                mm_chain(po, [(VP[0:nk, jt, kv, :], pt) for (kv, jt, nk, pi, pt) in plist],
                         reads=rk + [("VP", jt) for (_, jt, _, _, _) in plist], writes=psk(2, h))
                mm_chain(pd, [(onesp[0:nk, kv, :], pt) for (kv, jt, nk, pi, pt) in plist],
                         reads=rk + [("onesp",)], writes=psk(3, h))
                d3 = densb[h][:, 0:N].rearrange("p (c q) -> p c q", c=4)
                r3 = rec[h][:, 0:N].rearrange("p (c q) -> p c q", c=4)
                P.add("vector", lambda e: e.tensor_tensor(out=d3, in0=pd.rearrange("p (c q) -> p c q", c=4),
                                                          in1=sinkexp[:, l, :].unsqueeze(2).to_broadcast([128, 4, nq]),
                                                          op=ALU.add),
                      reads=psk(3, h) + [("sinkexp",)], writes=[("densb", h)])
                P.add("vector", lambda e: e.reciprocal(out=rec[h][:, 0:N], in_=densb[h][:, 0:N]),
                      reads=[("densb", h)], writes=[("rec", h)])
                P.add("vector", lambda e: e.tensor_tensor(out=attnT[:, :, q0:q0 + nq],
                                                          in0=po.rearrange("p (c q) -> p c q", c=4), in1=r3,
                                                          op=ALU.mult),
                      reads=psk(2, h) + [("rec", h)], writes=[("attnT", bi + 1)])

            prev = s_stage(blocks[0])
            for n, bi in enumerate(blocks):
                nxt = s_stage(blocks[n + 1]) if n + 1 < len(blocks) else None
                pv_stage(prev, n)
                prev = nxt
            return attnT

        def phase_C(s, l, bufs):
            AB = bufs["AB"]
            CSL = bufs["CSL"]
            YT = X3.view(16512, [4, LTOK], BF16)
            qsb = [X4.view(1024 * i, [KCW], F32) for i in range(3)]
            ctr = [0]
            for kc in range(NKC):
                cb = kc % 2
                k0 = kc * KCW
                nd = min(KCW, KHALF + 1 - k0)
                ka = max(k0, 1)
                kb = min(k0 + KCW, KHALF)
                nm = kb - ka
                for g in range(4):
                    n = ctr[0]
                    ctr[0] += 1
                    r = n % 4
                    qi = n % 3
                    psP = PQ[r][:, 0:KCW]
                    psQ = PQ[r][:, 512:512 + KCW]
                    pp, pq = [], []
                    for j in range(NTILE):
                        t0, nt = TILES[j]
                        pp.append((AB[0:nt, j, g * 256:g * 256 + 128], CSL[cb][0:nt, j, 0, :]))
                        pq.append((AB[0:nt, j, g * 256 + 128:g * 256 + 256], CSL[cb][0:nt, j, 1, :]))
                    abk = [("AB", j, g // 2) for j in range(NTILE)] + [("CSL", cb)]
                    mm_chain(psP, pp, reads=abk, writes=psk(r, 0))
                    mm_chain(psQ, pq, reads=abk, writes=psk(r, 1))
                    P.add("scalar", lambda e, qi=qi, psQ=psQ: e.activation(out=qsb[qi][:, :], in_=psQ, func=AF.Copy),
                          reads=psk(r, 1), writes=[("PT", qi)])
                    P.add("vector", lambda e, qi=qi, psP=psP, g=g, k0=k0, nd=nd: e.tensor_tensor(
                        out=YT[:, g, k0:k0 + nd], in0=psP[:, 0:nd], in1=qsb[qi][:, 0:nd], op=ALU.add),
                        reads=psk(r, 0) + [("PT", qi)], writes=[("YT", g, kc)])
                    if nm > 0:
                        base = YT[:, g, LTOK - ka:LTOK - ka + 1]
                        rev = bass.AP(base.tensor, base.offset, [list(base.ap[0]), [-1, nm]])
                        o = ka - k0
                        P.add("vector", lambda e, qi=qi, psP=psP, rev=rev, o=o, nm=nm: e.tensor_tensor(
                            out=rev, in0=psP[:, o:o + nm], in1=qsb[qi][:, o:o + nm], op=ALU.subtract),
                            reads=psk(r, 0) + [("PT", qi)], writes=[("YTm", g, kc)])
                if kc + 2 < NKC:
                    P.add("sync", lambda e, kc=kc, cb=cb: e.dma_start(out=CSL[cb], in_=csl_d[kc + 2]),
                          writes=[("CSL", cb)], dma=True)
            return YT

        def phase_D(s, l, attnT, YT):
            switch("X1")
            switch("X4")
            mixT = X1.view(0, [8, LTOK], BF16)
            Dst = [X1.view(33024 + 6144 * i, [24, 128], BF16) for i in range(2)]
            Wo = X1.view(45312, [8, 1024], BF16)
            sg = [X4.view(4096 * i, [2, 512], F32) for i in range(2)]
            mm = [X4.view(8192 + 4096 * i, [2, 512], F32) for i in range(2)]

            def load(c):
                P.add("gpsimd", lambda e: e.dma_start(out=Dst[c % 2], in_=wdm_d[l, c]), writes=[("Dst", c % 2)],
                      dma=True)

            load(0)
            n = 0
            for c in range(8):
                if c + 1 < 8:
                    load(c + 1)
                if c == 0:
                    P.add("gpsimd", lambda e: e.dma_start(out=Wo, in_=wo_d[l]), writes=[("Wo",)], dma=True)
                W = Dst[c % 2]
                for gi, (g0, N, tl) in enumerate(GROUPS):
                    pa_i = (n % 2) * 2
                    pb_i = pa_i + 1
                    b = n % 2
                    n += 1
                    ukeys = [("uT", j) for j in tl]
                    ykeys = [(nm_, g, kc) for nm_ in ("YT", "YTm") for g in range(4) for kc in range(NKC)]
                    akeys = [("attnT", j) for j in tl]
                    mm_chain(PQ[pa_i][:, 0:N], [(W[:, k, :], uT[:, k, g0:g0 + N]) for k in range(8)],
                             reads=ukeys + [("Dst", c % 2)], writes=psk(pa_i, 0))
                    mm_chain(PQ[pa_i][:, 512:512 + N], [(W[:, 8 + k, :], uT[:, k, g0:g0 + N]) for k in range(8)],
                             reads=ukeys + [("Dst", c % 2)], writes=psk(pa_i, 1))
                    mm_chain(PQ[pb_i][:, 0:N], [(W[:, 16 + k, :], YT[:, k, g0:g0 + N]) for k in range(4)],
                             reads=ykeys + [("Dst", c % 2)], writes=psk(pb_i, 0))
                    mm_chain(PQ[pb_i][:, 512:512 + N], [(W[:, 20 + k, :], attnT[:, k, g0:g0 + N]) for k in range(4)],
                             reads=akeys + [("Dst", c % 2)], writes=psk(pb_i, 1))
                    pa3 = PQ[pa_i][:, :].rearrange("p (a b) -> p a b", a=2)[:, :, 0:N]
                    pb3 = PQ[pb_i][:, :].rearrange("p (a b) -> p a b", a=2)[:, :, 0:N]
                    P.add("scalar", lambda e, b=b, pa3=pa3, N=N: e.activation(out=sg[b][:, :, 0:N], in_=pa3,
                                                                            func=AF.Sigmoid),
                          reads=psk(pa_i), writes=[("sg", b)])
                    P.add("vector", lambda e, b=b, pb3=pb3, N=N: e.tensor_tensor(out=mm[b][:, :, 0:N],
                                                                               in0=sg[b][:, :, 0:N], in1=pb3,
                                                                               op=ALU.mult),
                          reads=psk(pb_i) + [("sg", b)], writes=[("mm", b)])
                    P.add("vector", lambda e, b=b, c=c, g0=g0, N=N: e.tensor_tensor(out=mixT[:, c, g0:g0 + N],
                                                                                  in0=mm[b][:, 0, 0:N],
                                                                                  in1=mm[b][:, 1, 0:N], op=ALU.add),
                          reads=[("mm", b)], writes=[("mixT", c, gi)])
            return mixT, Wo

        def phase_E(s, l, mixT, Wo):
            switch("X4")
            hbuf = [X4.view(4096 * i, [1024], F32) for i in range(2)]
            tmp = [X4.view(8192 + 4096 * i, [1024], F32) for i in range(2)]
            hnew = [X4.view(16384 + 4096 * i, [1024], F32) for i in range(2)]
            ubuf = [X4.view(24576 + 2048 * i, [1024], BF16) for i in range(2)]
            junk = X4.view(28672, [1024], BF16)
            gain = [X4.view(30720 + 4096 * i, [1024], F32) for i in range(2)]
            for gi_, idx in ((0, 1), (1, 2)):
                P.add("sync", lambda e, gi_=gi_, idx=idx: e.dma_start(out=gain[gi_],
                                                                      in_=gains_d[l, idx].partition_broadcast(128)),
                      writes=[("gain", gi_)], dma=True)

            def load(j):
                t0, nt = TILES[j]
                src, skeys = h_src(s, l, j)
                P.add("sync", lambda e: e.dma_start(out=hbuf[j % 2][0:nt], in_=src), reads=skeys,
                      writes=[("hbuf", j % 2)], dma=True)

            def mmE(j):
                t0, nt = TILES[j]
                r = j % 3
                gi = TILE_GROUP[j]
                for hh in range(2):
                    mm_chain(PQ[r][0:nt, hh * 512:(hh + 1) * 512],
                             [(mixT[:, k, t0:t0 + nt], Wo[:, k, hh * 512:(hh + 1) * 512]) for k in range(8)],
                             reads=[("mixT", c, gi) for c in range(8)] + [("Wo",)], writes=psk(r, hh))

            def chain(j):
                t0, nt = TILES[j]
                b = j % 2
                r = j % 3
                rstd, skey = rms_stats(PQ[r][0:nt, :], nt, psk(r), junk[0:nt])
                P.add("vector", lambda e: e.scalar_tensor_tensor(
                    out=tmp[b][0:nt], in0=PQ[r][0:nt, :], scalar=rstd, in1=gain[0][0:nt], op0=ALU.mult, op1=ALU.mult),
                    reads=psk(r) + [skey, ("gain", 0)], writes=[("tmp", b)])
                P.add("vector", lambda e: e.tensor_tensor(out=hnew[b][0:nt], in0=hbuf[b][0:nt],
                                                          in1=tmp[b][0:nt], op=ALU.add),
                      reads=[("hbuf", b), ("tmp", b)], writes=[("hnew", b)])
                P.add("sync", lambda e: e.dma_start(out=hs_d[s, t0:t0 + nt, :], in_=hnew[b][0:nt]),
                      reads=[("hnew", b)], writes=[("hs", s, j)], dma=True)
                rstd2, skey2 = rms_stats(hnew[b][0:nt], nt, [("hnew", b)], junk[0:nt])
                P.add("vector", lambda e: e.scalar_tensor_tensor(
                    out=ubuf[b][0:nt], in0=hnew[b][0:nt], scalar=rstd2, in1=gain[1][0:nt], op0=ALU.mult,
                    op1=ALU.mult),
                    reads=[("hnew", b), skey2, ("gain", 1)], writes=[("ubuf", b)])

            load(0)
            load(1)
            mmE(0)
            mmE(1)
            chain(0)
            for j in range(NTILE):
                if j + 2 < NTILE:
                    mmE(j + 2)
                if j + 1 < NTILE:
                    chain(j + 1)
                if j + 2 < NTILE:
                    load(j + 2)
                transposes_to_uT(ubuf[j % 2], ("ubuf", j % 2), j, j % 2)

        def phase_F(s, l, last):
            switch("X1")
            switch("X3")
            switch("X4")
            HT = 1040
            actT = X1.view(0, [NFC, HT], BF16)
            Fst = [X1.view(45760 + 4096 * i, [16, 128], BF16) for i in range(3)]
            Wd = X3.view(0, [NFC, 1024], BF16)
            sg = [X4.view(2048 * i, [512], F32) for i in range(3)]
            hbuf = [X4.view(6144 + 4096 * i, [1024], F32) for i in range(2)]
            tmp = [X4.view(14336 + 4096 * i, [1024], F32) for i in range(2)]
            hnew = [X4.view(22528 + 4096 * i, [1024], F32) for i in range(2)]
            gain = X4.view(30720, [1024], F32)
            junk = X4.view(34816, [1024], BF16)
            P.add("sync", lambda e: e.dma_start(out=gain, in_=gains_d[l, 3].partition_broadcast(128)),
                  writes=[("gain", 0)], dma=True)
            fctr = [0]
            nctr = [0]
            tctr = [0]
            for hf in range(2):
                if hf == 0:
                    h0, tiles, groups = 0, list(range(0, 9)), [0, 1, 2]
                else:
                    h0, tiles, groups = 1040, list(range(9, 17)), [3, 4]

                def load(fc):
                    sl = (fctr[0] + (fc - 0)) % 3
                    P.add("gpsimd", lambda e: e.dma_start(out=Fst[sl], in_=wgu_d[l, fc]), writes=[("Fst", sl)],
                          dma=True)
                    return sl

                slots = {}
                slots[0] = load(0)
                slots[1] = load(1)
                for fc in range(NFC):
                    if fc + 2 < NFC:
                        slots[fc + 2] = load(fc + 2)
                    if hf == 0 and fc < 11:
                        P.add("gpsimd", lambda e, fc=fc: e.dma_start(out=Wd[:, 2 * fc:2 * fc + 2, :],
                                                                     in_=wdn_d[l, :, 2 * fc:2 * fc + 2, :]),
                              writes=[("Wd", fc)], dma=True)
                    sl = slots[fc]
                    W = Fst[sl]
                    for gi in groups:
                        g0, N, tl = GROUPS[gi]
                        n = nctr[0]
                        nctr[0] += 1
                        r = n % 4
                        sb_i = n % 3
                        ukeys = [("uT", j) for j in tl]
                        mm_chain(PQ[r][:, 0:N], [(W[:, k, :], uT[:, k, g0:g0 + N]) for k in range(8)],
                                 reads=ukeys + [("Fst", sl)], writes=psk(r, 0))
                        mm_chain(PQ[r][:, 512:512 + N], [(W[:, 8 + k, :], uT[:, k, g0:g0 + N]) for k in range(8)],
                                 reads=ukeys + [("Fst", sl)], writes=psk(r, 1))
                        P.add("scalar", lambda e, r=r, sb_i=sb_i, N=N: e.activation(out=sg[sb_i][:, 0:N],
                                                                                  in_=PQ[r][:, 0:N], func=AF.Silu),
                              reads=psk(r, 0), writes=[("sg", sb_i)])
                        P.add("vector", lambda e, r=r, sb_i=sb_i, N=N, fc=fc, g0=g0, h0=h0: e.tensor_tensor(
                            out=actT[:, fc, g0 - h0:g0 - h0 + N], in0=sg[sb_i][:, 0:N], in1=PQ[r][:, 512:512 + N],
                            op=ALU.mult),
                            reads=psk(r, 1) + [("sg", sb_i)], writes=[("actT", fc, gi)])
                fctr[0] += NFC

                def loadh(j):
                    t0, nt = TILES[j]
                    P.add("sync", lambda e: e.dma_start(out=hbuf[j % 2][0:nt], in_=hs_d[s, t0:t0 + nt, :]),
                          reads=[("hs", s, j)], writes=[("hbuf", j % 2)], dma=True)

                loadh(tiles[0])
                for ti, j in enumerate(tiles):
                    t0, nt = TILES[j]
                    b = j % 2
                    if ti + 1 < len(tiles):
                        loadh(tiles[ti + 1])
                    gi = TILE_GROUP[j]
                    r = tctr[0] % 4
                    tctr[0] += 1
                    lo = t0 - h0
                    for hh in range(2):
                        mm_chain(PQ[r][0:nt, hh * 512:(hh + 1) * 512],
                                 [(actT[:, fc, lo:lo + nt], Wd[:, fc, hh * 512:(hh + 1) * 512]) for fc in range(NFC)],
                                 reads=[("actT", fc, gi) for fc in range(NFC)] + [("Wd", i_) for i_ in range(11)],
                                 writes=psk(r, hh))
                    rstd, skey = rms_stats(PQ[r][0:nt, :], nt, psk(r), junk[0:nt])
                    P.add("vector", lambda e, b=b, nt=nt, rstd=rstd, r=r: e.scalar_tensor_tensor(
                        out=tmp[b][0:nt], in0=PQ[r][0:nt, :], scalar=rstd, in1=gain[0:nt], op0=ALU.mult,
                        op1=ALU.mult),
                        reads=psk(r) + [skey, ("gain", 0)], writes=[("tmp", b)])
                    P.add("vector", lambda e, b=b, nt=nt: e.tensor_tensor(out=hnew[b][0:nt], in0=hbuf[b][0:nt],
                                                                         in1=tmp[b][0:nt], op=ALU.add),
                          reads=[("hbuf", b), ("tmp", b)], writes=[("hnew", b)])
                    if last:
                        if j >= 1:
                            P.add("sync", lambda e, b=b, j=j: e.dma_start(out=out_d[s, 128 * (j - 1):128 * j, :],
                                                                          in_=hnew[b][:, :]),
                                  reads=[("hnew", b)], writes=[("out", s, j)], dma=True)
                    else:
                        P.add("sync", lambda e, b=b, nt=nt, t0=t0: e.dma_start(out=hs_d[s, t0:t0 + nt, :],
                                                                              in_=hnew[b][0:nt]),
                              reads=[("hnew", b)], writes=[("hs", s, j)], dma=True)

        def dump(name, ap, shape, dt):
            d = nc.dram_tensor("dbg_" + name, shape, dt, kind="ExternalOutput").ap()
            dbg_outs[name] = d
            P.add("sync", lambda e: e.dma_start(out=d, in_=ap), reads=list(P.last_writer.keys()), dma=True)

        done = False
        for s in range(nseq):
            for l in range(nlayers):
                bufs = phase_A(s, l)
                attnT = phase_B(s, l, bufs)
                YT = phase_C(s, l, bufs)
                if stop == "C":
                    dump("qT", bufs["qT"], [128, 4, LTOK], BF16)
                    dump("kT", bufs["kT"], [128, LTOK], BF16)
                    dump("VP", bufs["VP"], [128, NTILE, 2, 128], BF16)
                    dump("AB", bufs["AB"], [128, NTILE, 1024], BF16)
                    dump("uT", uT, [128, 8, LTOK], BF16)
                    dump("attnT", attnT, [128, 4, LTOK], BF16)
                    dump("YT", YT, [128, 4, LTOK], BF16)
                    done = True
                    break
                mixT, Wo = phase_D(s, l, attnT, YT)
                if stop == "D":
                    dump("mixT", mixT, [128, 8, LTOK], BF16)
                    done = True
                    break
                phase_E(s, l, mixT, Wo)
                if stop == "E":
                    dump("uT", uT, [128, 8, LTOK], BF16)
                    dump("hs", hs_d[s], [LTOK, D_MODEL], F32)
                    done = True
                    break
                phase_F(s, l, last=(l == nlayers - 1) and stop is None)
                if stop == "F":
                    dump("hs", hs_d[s], [LTOK, D_MODEL], F32)
                    done = True
                    break
            if done:
                break
        P.emit()
    return nc


def _q_perm():
    idx = np.empty(512, np.int64)
    for c in range(4):
        for half in range(2):
            for d in range(64):
                idx[c * 128 + half * 64 + d] = (c + 4 * half) * 64 + d
    return idx


def _swap64(n):
    idx = np.arange(n)
    return (idx // 64) * 64 + ((idx % 64) + 32) % 64


def _constants():
    bf = ml_dtypes.bfloat16
    c = {}
    inv_freq = (10000.0 ** (-(np.arange(0, 64, 2, dtype=np.float32)) / np.float32(64))).astype(np.float32)
    ang = (np.arange(LTOK, dtype=np.float32)[:, None] * inv_freq[None, :]).astype(np.float32)
    cos = np.cos(ang.astype(np.float64))
    sin = np.sin(ang.astype(np.float64))
    p = np.arange(128)
    d = p % 64
    fi = d % 32
    sign = np.where(d < 32, -1.0, 1.0)
    c["rcos"] = np.ascontiguousarray(cos[:, fi].T).astype(np.float32)
    c["rsin"] = np.ascontiguousarray((sin[:, fi] * sign[None, :]).T).astype(np.float32)
    cc = np.arange(128)
    a128 = 2 * np.pi * ((cc[:, None] * cc[None, :]) % 128) / 128.0
    c["cs128"] = np.concatenate([np.cos(a128), -np.sin(a128)], axis=1).astype(np.float64) / np.sqrt(128.0)
    c["cs128"] = c["cs128"].astype(bf)
    n_of = np.zeros((128, NTILE), np.int64)
    valid = np.zeros((128, NTILE), bool)
    for j, (t0, nt) in enumerate(TILES):
        n_of[:nt, j] = t0 + np.arange(nt)
        valid[:nt, j] = True
    k_all = np.arange(NKC * KCW).reshape(NKC, KCW)
    csl = np.zeros((NKC, 128, NTILE, 2, KCW), np.float32)
    for kc in range(NKC):
        r = (n_of[:, :, None] * k_all[kc][None, None, :]) % LTOK
        a = 2 * np.pi * r / LTOK
        csl[kc, :, :, 0, :] = np.cos(a) / np.sqrt(LTOK) * valid[:, :, None]
        csl[kc, :, :, 1, :] = np.sin(a) / np.sqrt(LTOK) * valid[:, :, None]
    c["csl"] = csl.astype(bf)
    jj = np.arange(128)[:, None]
    ss = np.arange(128)[None, :]
    m = np.zeros((128, 3, 128), np.float32)
    m[:, 0, :] = (jj >= ss)
    m[:, 1, :] = (jj <= ss)
    m[:, 2, :] = (jj <= 112 + ss)
    c["masks"] = m.astype(bf)
    c["ident"] = np.eye(128, dtype=np.float32).astype(bf)
    o = np.zeros((128, 2, 128), np.float32)
    o[:, 0, 0:64] = 1.0
    o[:, 1, 64:128] = 1.0
    c["onesp"] = o.astype(bf)
    return c


def _prep_weights(w_in, w_fourier_out, w_attn_out, w_o, sink_logits, norm_mix_pre, norm_mix_post,
                  norm_ffn_pre, norm_ffn_post, w_ffn_gate, w_ffn_up, w_ffn_down):
    f32 = np.float32
    qp = _q_perm()
    w = {}
    wq = w_in[:, :, 0:512][:, :, qp]
    wk = w_in[:, :, 512:640]
    wv = w_in[:, :, 640:768]
    wf = w_in[:, :, 768:1280]
    main = np.concatenate([wq, wk, wf, wv], axis=2)
    swp = np.concatenate([wq[:, :, _swap64(512)], wk[:, :, _swap64(128)]], axis=2)
    w["wa"] = np.ascontiguousarray(main.reshape(DEPTH, 8, 128, 1280).transpose(0, 2, 1, 3), dtype=f32)
    w["was"] = np.ascontiguousarray(swp.reshape(DEPTH, 8, 128, 640).transpose(0, 2, 1, 3), dtype=f32)
    wgf = w_in[:, :, 1280:2304].reshape(DEPTH, 8, 128, 8, 128)
    wga = w_in[:, :, 2304:3328].reshape(DEPTH, 8, 128, 8, 128)
    wfo = w_fourier_out.reshape(DEPTH, 4, 128, 8, 128)
    rows = np.empty(512, np.int64)
    for kc in range(4):
        for p_ in range(128):
            head = kc if p_ < 64 else 4 + kc
            rows[kc * 128 + p_] = head * 64 + (p_ % 64)
    wao = w_attn_out[:, rows, :].reshape(DEPTH, 4, 128, 8, 128)
    pack = np.concatenate([wgf, wga, wfo, wao], axis=1)
    w["wdm"] = np.ascontiguousarray(pack.transpose(0, 3, 2, 1, 4), dtype=f32)
    w["wo"] = np.ascontiguousarray(w_o.reshape(DEPTH, 8, 128, 1024).transpose(0, 2, 1, 3), dtype=f32)
    g_ = w_ffn_gate.reshape(DEPTH, 8, 128, NFC, 128)
    u_ = w_ffn_up.reshape(DEPTH, 8, 128, NFC, 128)
    gu = np.concatenate([g_, u_], axis=1)
    w["wgu"] = np.ascontiguousarray(gu.transpose(0, 3, 2, 1, 4), dtype=f32)
    w["wdn"] = np.ascontiguousarray(w_ffn_down.reshape(DEPTH, NFC, 128, 1024).transpose(0, 2, 1, 3), dtype=f32)
    w["gains"] = np.ascontiguousarray(np.stack([norm_mix_pre, norm_mix_post, norm_ffn_pre, norm_ffn_post], axis=1),
                                      dtype=f32)
    sk = np.empty((DEPTH, 128, 4), f32)
    sk[:, 0:64, :] = sink_logits[:, None, 0:4]
    sk[:, 64:128, :] = sink_logits[:, None, 4:8]
    w["sink"] = sk
    return w


_NC_CACHE = {}


def kernel(x, meta_tokens, w_in, w_fourier_out, w_attn_out, w_o, sink_logits, norm_mix_pre, norm_mix_post,
           norm_ffn_pre, norm_ffn_post, w_ffn_gate, w_ffn_up, w_ffn_down):
    args = [np.asarray(a, dtype=np.float32) for a in (w_in, w_fourier_out, w_attn_out, w_o, sink_logits,
                                                      norm_mix_pre, norm_mix_post, norm_ffn_pre, norm_ffn_post,
                                                      w_ffn_gate, w_ffn_up, w_ffn_down)]
    x = np.asarray(x, dtype=np.float32)
    shared = _prep_weights(*args)
    shared.update(_constants())
    shared["meta"] = np.ascontiguousarray(np.asarray(meta_tokens, dtype=np.float32))
    if "nc" not in _NC_CACHE:
        _NC_CACHE["nc"] = build_program()
    nc = _NC_CACHE["nc"]
    in_maps = []
    for c in range(8):
        m = dict(shared)
        m["x"] = np.ascontiguousarray(x[2 * c:2 * c + 2])
        in_maps.append(m)
    res = run_bass_kernel_spmd(nc, in_maps, core_ids=list(range(8)))
    out = np.concatenate([np.asarray(r["out"]) for r in res.results], axis=0)
    return out.astype(np.float32)
```

```python
import contextlib
import numpy as np
import ml_dtypes
import concourse.bass as bass
import concourse.mybir as mybir
from concourse.bass_utils import run_bass_kernel_spmd

F32 = mybir.dt.float32
BF16 = mybir.dt.bfloat16
AF = mybir.ActivationFunctionType
ALU = mybir.AluOpType

D_MODEL = 1024
SEQ = 2048
DEPTH = 2
N_META = 16
LTOK = N_META + SEQ
NTILE = 17
D_FF = 2816
NFC = D_FF // 128
EPS = 1e-6
NKC = 6
KCW = 173
KHALF = LTOK // 2

TILES = [(0, 16)] + [(16 + 128 * i, 128) for i in range(16)]
GROUPS = [(0, 16, [0])] + [(16 + 512 * g, 512, [1 + 4 * g + i for i in range(4)]) for g in range(4)]
TILE_GROUP = {}
for _gi, (_g0, _n, _tl) in enumerate(GROUPS):
    for _j in _tl:
        TILE_GROUP[_j] = _gi


class _Op:
    __slots__ = ("eng", "fn", "deps", "flag", "semval", "sem", "is_dma")


class Prog:
    ENGS = ("sync", "tensor", "vector", "scalar", "gpsimd")

    def __init__(self, nc, n_dma_sems=32):
        self.nc = nc
        self.ops = {e: [] for e in self.ENGS}
        self.last_writer = {}
        self.readers = {}
        self.n_dma_sems = n_dma_sems
        self.dma_count = 0
        self.dma_last = [None] * n_dma_sems
        self.dma_val = [0] * n_dma_sems
        self.region_of = {}

    def _expand(self, keys):
        out = []
        regs = set()
        for k in keys:
            out.append(k)
            r = self.region_of.get(k[0])
            if r is not None:
                regs.add(("reg", r))
        return out, regs

    def add(self, eng, fn, reads=(), writes=(), dma=False, regs=()):
        op = _Op()
        op.eng = eng
        op.fn = fn
        op.flag = False
        op.semval = 0
        op.sem = None
        op.is_dma = dma
        reads, r1 = self._expand(reads)
        writes, r2 = self._expand(writes)
        rset = r1 | r2 | {("reg", r) for r in regs}
        reads = list(reads) + [r for r in rset if r not in writes]
        deps = set()
        for r in reads:
            w = self.last_writer.get(r)
            if w is not None:
                deps.add(w)
        for r in writes:
            w = self.last_writer.get(r)
            if w is not None:
                deps.add(w)
            for rd in self.readers.get(r, ()):
                deps.add(rd)
        for r in reads:
            self.readers.setdefault(r, []).append(op)
        for r in writes:
            self.last_writer[r] = op
            self.readers[r] = []
        if dma:
            k = self.dma_count % self.n_dma_sems
            self.dma_count += 1
            prev = self.dma_last[k]
            if prev is not None:
                deps.add(prev)
            self.dma_last[k] = op
            self.dma_val[k] += 16
            op.sem = k
            op.semval = self.dma_val[k]
        deps.discard(op)
        if eng == "tensor":
            deps = {d for d in deps if not (d.eng == "tensor" and not d.is_dma)}
        op.deps = deps
        self.ops[eng].append(op)
        return op

    def emit(self, final_wait_eng="sync"):
        nc = self.nc
        for e in self.ENGS:
            for op in self.ops[e]:
                for d in op.deps:
                    if not d.is_dma:
                        d.flag = True
        for e in self.ENGS:
            c = 0
            for op in self.ops[e]:
                if op.flag and not op.is_dma:
                    c += 1
                    op.semval = c
        with contextlib.ExitStack() as st:
            engsem = {e: st.enter_context(nc.semaphore("s_" + e)) for e in self.ENGS}
            dmasem = [st.enter_context(nc.semaphore("d%d" % i)) for i in range(self.n_dma_sems)]
            block = st.enter_context(nc.Block())

            def semof(d):
                if d.is_dma:
                    return ("d", d.sem), dmasem[d.sem], d.semval
                return ("e", d.eng), engsem[d.eng], d.semval

            def make_body(ename):
                def body(e):
                    waited = {}
                    for op in self.ops[ename]:
                        need = {}
                        for d in op.deps:
                            key, sem, val = semof(d)
                            if waited.get(key, 0) < val and need.get(key, (None, 0))[1] < val:
                                need[key] = (sem, val)
                        for key, (sem, val) in need.items():
                            e.wait_ge(sem, val)
                            waited[key] = val
                        ins = op.fn(e)
                        if op.is_dma:
                            ins.then_inc(dmasem[op.sem], 16)
                        elif op.flag:
                            ins.then_inc(engsem[ename], 1)
                    if ename == final_wait_eng:
                        for k in range(self.n_dma_sems):
                            if self.dma_val[k] > waited.get(("d", k), 0):
                                e.wait_ge(dmasem[k], self.dma_val[k])
                return body

            for ename in self.ENGS:
                if self.ops[ename] or ename == final_wait_eng:
                    getattr(block, ename)(make_body(ename))


class Region:
    def __init__(self, nc, st, name, nbytes):
        self.name = name
        self.nbytes = nbytes
        self.t = st.enter_context(nc.sbuf_tensor(name, [128, nbytes // 2], BF16))

    def view(self, off, free_shape, dtype):
        esz = 2 if dtype == BF16 else 4
        n = 1
        for d in free_shape:
            n *= d
        assert off % 4 == 0 and off + n * esz <= self.nbytes, (self.name, off, free_shape)
        a = self.t[:, off // 2: off // 2 + n * esz // 2]
        if dtype != BF16:
            a = a.bitcast(dtype)
        if len(free_shape) == 2:
            a = a.rearrange("p (a b) -> p a b", a=free_shape[0])
        elif len(free_shape) == 3:
            a = a.rearrange("p (a b c) -> p a b c", a=free_shape[0], b=free_shape[1])
        return a


def build_program(nseq=2, nlayers=DEPTH, stop=None, dbg=False):
    nc = bass.Bass("TRN2", target_bir_lowering=False)

    def din(name, shape, dt):
        return nc.dram_tensor(name, shape, dt, kind="ExternalInput").ap()

    x_d = din("x", [2, SEQ, D_MODEL], F32)
    meta_d = din("meta", [N_META, D_MODEL], F32)
    wa_d = din("wa", [DEPTH, 128, 8, 1280], F32)
    was_d = din("was", [DEPTH, 128, 8, 640], F32)
    wdm_d = din("wdm", [DEPTH, 8, 128, 24, 128], F32)
    wo_d = din("wo", [DEPTH, 128, 8, 1024], F32)
    wgu_d = din("wgu", [DEPTH, NFC, 128, 16, 128], F32)
    wdn_d = din("wdn", [DEPTH, 128, NFC, 1024], F32)
    gains_d = din("gains", [DEPTH, 4, D_MODEL], F32)
    sink_d = din("sink", [DEPTH, 128, 4], F32)
    cos_d = din("rcos", [128, LTOK], F32)
    sin_d = din("rsin", [128, LTOK], F32)
    cs128_d = din("cs128", [128, 256], BF16)
    csl_d = din("csl", [NKC, 128, NTILE, 2, KCW], BF16)
    masks_d = din("masks", [128, 3, 128], BF16)
    ident_d = din("ident", [128, 128], BF16)
    ones_d = din("onesp", [128, 2, 128], BF16)
    out_d = nc.dram_tensor("out", [2, SEQ, D_MODEL], F32, kind="ExternalOutput").ap()
    hs_d = nc.dram_tensor("hs", [2, LTOK, D_MODEL], F32, kind="Internal").ap()
    dbg_outs = {}

    with contextlib.ExitStack() as st:
        X1 = Region(nc, st, "X1", 64256)
        X2 = Region(nc, st, "X2", 33024)
        X3 = Region(nc, st, "X3", 45056)
        X4 = Region(nc, st, "X4", 49152)

        def sb(name, shape, dt):
            return st.enter_context(nc.sbuf_tensor("sb_" + name, shape, dt))

        ident = sb("ident", [128, 128], BF16)
        cs128 = sb("cs128", [128, 256], BF16)
        masks = sb("masks", [128, 3, 128], BF16)
        onesp = sb("onesp", [128, 2, 128], BF16)
        epsb = sb("epsb", [128, 1], F32)
        sinkraw = sb("sinkraw", [128, DEPTH, 4], F32)
        sinkexp = sb("sinkexp", [128, DEPTH, 4], F32)
        NSTAT = nseq * nlayers * NTILE * 4 * 3
        stats = sb("stats", [128, NSTAT], F32)
        swd = sb("swd", [128, 8], F32)
        PQ = [st.enter_context(nc.psum_tensor("pq%d" % i, [128, 1024], F32)) for i in range(4)]

        P = Prog(nc)
        for nm in ("qT", "kT", "VP", "AB", "mixT", "Dst", "Wo", "actT", "Fst"):
            P.region_of[nm] = "X1"
        P.region_of["uT"] = "X2"
        for nm in ("attnT", "YT", "YTm", "WAm", "rope", "Wd"):
            P.region_of[nm] = "X3"
        for nm in ("WAs", "t1", "t2", "fT", "gain", "hbuf", "ubuf", "PT", "densb", "rec", "CSL", "sg", "mm",
                   "tmp", "hnew"):
            P.region_of[nm] = "X4"

        uT = X2.view(0, [8, LTOK], BF16)

        def switch(reg):
            P.add("vector", lambda e: e.memset(swd[:, 0:1], 0.0), writes=[("reg", reg)])

        def psk(i, h=None):
            if h is None:
                return [("ps", i, 0), ("ps", i, 1)]
            return [("ps", i, h)]

        stat_ctr = [0]

        def new_stat():
            c = stat_ctr[0]
            stat_ctr[0] += 3
            assert c + 3 <= NSTAT
            return c

        def mm_chain(out_ap, pairs, reads, writes):
            def fn(e):
                n = len(pairs)
                ins = None
                for i, (l_, r_) in enumerate(pairs):
                    ins = e.matmul(out_ap, lhsT=l_, rhs=r_, start=(i == 0), stop=(i == n - 1))
                return ins
            P.add("tensor", fn, reads=reads, writes=writes)

        P.add("sync", lambda e: e.dma_start(out=ident[:], in_=ident_d), writes=[("ident",)], dma=True)
        P.add("sync", lambda e: e.dma_start(out=cs128[:], in_=cs128_d), writes=[("cs128",)], dma=True)
        P.add("sync", lambda e: e.dma_start(out=masks[:], in_=masks_d), writes=[("masks",)], dma=True)
        P.add("sync", lambda e: e.dma_start(out=onesp[:], in_=ones_d), writes=[("onesp",)], dma=True)
        P.add("sync", lambda e: e.dma_start(out=sinkraw[:], in_=sink_d.rearrange("l p c -> p l c")),
              writes=[("sinkraw",)], dma=True)
        P.add("vector", lambda e: e.memset(epsb[:], EPS), writes=[("eps",)])
        P.add("vector", lambda e: e.memset(stats[:], 0.0), writes=[("stats",)])
        P.add("scalar", lambda e: e.activation(out=sinkexp[:], in_=sinkraw[:], func=AF.Exp),
              reads=[("sinkraw",)], writes=[("sinkexp",)])

        hb_ctr = [0]

        def next_hb():
            n = hb_ctr[0] % 4
            hb_ctr[0] += 1
            return n // 2, n % 2

        def rms_stats(src_ap, nt, src_keys, junk_ap):
            c = new_stat()
            key = ("st", c)
            P.add("scalar", lambda e: e.activation(out=junk_ap, in_=src_ap, func=AF.Square,
                                                   accum_out=stats[0:nt, c:c + 1]),
                  reads=list(src_keys) + [("stats",)], writes=[key], regs=["X4"])
            P.add("scalar", lambda e: e.activation(out=stats[0:nt, c + 1:c + 2], in_=stats[0:nt, c:c + 1],
                                                   func=AF.Sqrt, scale=1.0 / D_MODEL, bias=epsb[0:nt, 0:1]),
                  reads=[key, ("eps",)], writes=[key])
            P.add("vector", lambda e: e.reciprocal(out=stats[0:nt, c + 2:c + 3], in_=stats[0:nt, c + 1:c + 2]),
                  reads=[key], writes=[key])
            return stats[0:nt, c + 2:c + 3], key

        def transposes_to_uT(ub_ap, ub_key, j, b):
            t0, nt = TILES[j]
            pT = PQ[3][:, b * 512:(b + 1) * 512].bitcast(BF16).rearrange("p (k t) -> p k t", k=8)

            def fn(e):
                ins = None
                for k in range(8):
                    ins = e.transpose(pT[:, k, 0:nt], ub_ap[0:nt, k * 128:(k + 1) * 128], ident[0:nt, 0:nt])
                return ins
            P.add("tensor", fn, reads=[ub_key, ("ident",)], writes=psk(3, b))
            P.add("scalar", lambda e: e.activation(out=uT[:, :, t0:t0 + nt], in_=pT[:, :, 0:nt], func=AF.Copy),
                  reads=psk(3, b), writes=[("uT", j)])

        def h_src(s, l, j):
            t0, nt = TILES[j]
            if l == 0:
                if j == 0:
                    return meta_d, []
                return x_d[s, 128 * (j - 1):128 * j, :], []
            return hs_d[s, t0:t0 + nt, :], [("hs", s, j)]

        def phase_A(s, l):
            switch("X1")
            switch("X3")
            switch("X4")
            WAm = X3.view(0, [8, 1280], BF16)
            cosT = X3.view(20480, [LTOK], F32)
            sinT = X3.view(20480 + 8256, [LTOK], F32)
            WAs = X4.view(0, [8, 640], BF16)
            t1 = [X4.view(10240 + 2048 * i, [512], F32) for i in range(2)]
            t2 = [X4.view(14336 + 2048 * i, [512], F32) for i in range(2)]
            fT = [X4.view(18432 + 4096 * i, [4, 512], BF16) for i in range(2)]
            gain = X4.view(26624, [1024], F32)
            hbuf = [X4.view(30720 + 4096 * i, [1024], F32) for i in range(2)]
            ubuf = [X4.view(38912 + 2048 * i, [1024], BF16) for i in range(2)]
            junk = X4.view(43008, [1024], BF16)
            qT = X1.view(0, [4, LTOK], BF16)
            kT = X1.view(16512, [LTOK], BF16)
            VP = X1.view(20640, [NTILE, 2, 128], BF16)
            AB = X1.view(29344, [NTILE, 1024], BF16)

            P.add("gpsimd", lambda e: e.dma_start(out=WAm[:, :, 0:640], in_=wa_d[l, :, :, 0:640]),
                  writes=[("WAm", 0)], dma=True)
            P.add("gpsimd", lambda e: e.dma_start(out=WAs, in_=was_d[l]), writes=[("WAs",)], dma=True)
            P.add("gpsimd", lambda e: e.dma_start(out=WAm[:, :, 640:1280], in_=wa_d[l, :, :, 640:1280]),
                  writes=[("WAm", 1)], dma=True)
            P.add("sync", lambda e: e.dma_start(out=cosT, in_=cos_d), writes=[("rope", 0)], dma=True)
            P.add("sync", lambda e: e.dma_start(out=sinT, in_=sin_d), writes=[("rope", 1)], dma=True)
            P.add("sync", lambda e: e.dma_start(out=gain, in_=gains_d[l, 0].partition_broadcast(128)),
                  writes=[("gain", 0)], dma=True)
            VPflat = X1.view(20640, [NTILE * 2 * 128], BF16)
            P.add("vector", lambda e: e.memset(VPflat, 0.0), writes=[("VP", j) for j in range(NTILE)])

            def tile_p1(j):
                t0, nt = TILES[j]
                b = j % 2
                src, skeys = h_src(s, l, j)
                P.add("sync", lambda e: e.dma_start(out=hbuf[b][0:nt], in_=src), reads=skeys,
                      writes=[("hbuf", b)], dma=True)
                rstd, skey = rms_stats(hbuf[b][0:nt], nt, [("hbuf", b)], junk[0:nt])
                P.add("vector", lambda e: e.scalar_tensor_tensor(out=ubuf[b][0:nt], in0=hbuf[b][0:nt], scalar=rstd,
                                                                 in1=gain[0:nt], op0=ALU.mult, op1=ALU.mult),
                      reads=[("hbuf", b), skey, ("gain", 0)], writes=[("ubuf", b)])

            def tile_p2(j):
                transposes_to_uT(ubuf[j % 2], ("ubuf", j % 2), j, j % 2)

            def group_stage(gi):
                g0, N, tl = GROUPS[gi]
                ukeys = [("uT", j) for j in tl]
                fb = gi % 2
                for c in range(5):
                    col = c * 128
                    a_i, a_h = next_hb()
                    b_i, b_h = next_hb()
                    pa = PQ[a_i][:, a_h * 512:a_h * 512 + N]
                    pb = PQ[b_i][:, b_h * 512:b_h * 512 + N]
                    mm_chain(pa, [(WAm[:, k, col:col + 128], uT[:, k, g0:g0 + N]) for k in range(8)],
                             reads=ukeys + [("WAm", 0)], writes=psk(a_i, a_h))
                    mm_chain(pb, [(WAs[:, k, col:col + 128], uT[:, k, g0:g0 + N]) for k in range(8)],
                             reads=ukeys + [("WAs",)], writes=psk(b_i, b_h))
                    tb = c % 2
                    P.add("vector", lambda e, pa=pa, tb=tb: e.tensor_tensor(out=t1[tb][:, 0:N], in0=pa,
                                                                          in1=cosT[:, g0:g0 + N], op=ALU.mult),
                          reads=psk(a_i, a_h) + [("rope", 0)], writes=[("t1", tb)])
                    P.add("vector", lambda e, pb=pb, tb=tb: e.tensor_tensor(out=t2[tb][:, 0:N], in0=pb,
                                                                          in1=sinT[:, g0:g0 + N], op=ALU.mult),
                          reads=psk(b_i, b_h) + [("rope", 1)], writes=[("t2", tb)])
                    if c < 4:
                        dst = qT[:, c, g0:g0 + N]
                        dkey = ("qT", c, gi)
                    else:
                        dst = kT[:, g0:g0 + N]
                        dkey = ("kT", gi)
                    P.add("vector", lambda e, dst=dst, tb=tb: e.tensor_tensor(out=dst, in0=t1[tb][:, 0:N],
                                                                            in1=t2[tb][:, 0:N], op=ALU.add),
                          reads=[("t1", tb), ("t2", tb)], writes=[dkey])
                    yield
                for c in range(4):
                    col = 640 + c * 128
                    a_i, a_h = next_hb()
                    pa = PQ[a_i][:, a_h * 512:a_h * 512 + N]
                    mm_chain(pa, [(WAm[:, k, col:col + 128], uT[:, k, g0:g0 + N]) for k in range(8)],
                             reads=ukeys + [("WAm", 1)], writes=psk(a_i, a_h))
                    P.add("scalar", lambda e, pa=pa, c=c: e.activation(out=fT[fb][:, c, 0:N], in_=pa, func=AF.Copy),
                          reads=psk(a_i, a_h), writes=[("fT", fb, c)])
                    yield
                for j in tl:
                    t0, nt = TILES[j]
                    a_i, a_h = next_hb()
                    pv = PQ[a_i][0:nt, a_h * 512:a_h * 512 + 128]
                    mm_chain(pv, [(uT[:, k, t0:t0 + nt], WAm[:, k, 1152:1280]) for k in range(8)],
                             reads=[("uT", j), ("WAm", 1)], writes=psk(a_i, a_h))
                    P.add("scalar", lambda e, pv=pv, j=j, nt=nt: e.activation(out=VP[0:nt, j, 0, 0:64],
                                                                            in_=pv[:, 0:64], func=AF.Copy),
                          reads=psk(a_i, a_h), writes=[("VP", j)])
                    P.add("vector", lambda e, pv=pv, j=j, nt=nt: e.tensor_copy(out=VP[0:nt, j, 1, 64:128],
                                                                             in_=pv[:, 64:128]),
                          reads=psk(a_i, a_h), writes=[("VP", j)])
                    lo = t0 - g0
                    for hh in range(2):
                        def fn(e, hh=hh, lo=lo, nt=nt):
                            ins = None
                            for gg in (2 * hh, 2 * hh + 1):
                                ins = e.matmul(PQ[2][0:nt, gg * 256:(gg + 1) * 256], lhsT=fT[fb][:, gg, lo:lo + nt],
                                               rhs=cs128[:, :], start=True, stop=True)
                            return ins
                        P.add("tensor", fn, reads=[("fT", fb, 2 * hh), ("fT", fb, 2 * hh + 1), ("cs128",)],
                              writes=psk(2, hh))
                        eng = "scalar" if hh == 0 else "vector"
                        if hh == 0:
                            P.add("scalar", lambda e, j=j, nt=nt: e.activation(out=AB[0:nt, j, 0:512],
                                                                             in_=PQ[2][0:nt, 0:512], func=AF.Copy),
                                  reads=psk(2, 0), writes=[("AB", j, 0)])
                        else:
                            P.add("vector", lambda e, j=j, nt=nt: e.tensor_copy(out=AB[0:nt, j, 512:1024],
                                                                              in_=PQ[2][0:nt, 512:1024]),
                                  reads=psk(2, 1), writes=[("AB", j, 1)])
                    yield

            def tile_seq(tl):
                acts = []
                for i, j in enumerate(tl):
                    acts.append([("p1", j)] + ([("p2", tl[i - 1])] if i > 0 else []))
                acts.append([("p2", tl[-1])])
                return acts

            def run_acts(a):
                for kind, j in a:
                    (tile_p1 if kind == "p1" else tile_p2)(j)

            for a in tile_seq(GROUPS[0][2] + GROUPS[1][2]):
                run_acts(a)
            for gi in range(len(GROUPS)):
                pend = tile_seq(list(GROUPS[gi + 1][2])) if (gi >= 1 and gi + 1 < len(GROUPS)) else []
                for step, _ in enumerate(group_stage(gi)):
                    if pend and step % 2 == 0:
                        run_acts(pend.pop(0))
                for a in pend:
                    run_acts(a)
            return dict(qT=qT, kT=kT, VP=VP, AB=AB)

        def phase_B(s, l, bufs):
            switch("X4")
            switch("X3")
            qT, kT, VP = bufs["qT"], bufs["kT"], bufs["VP"]
            attnT = X3.view(0, [4, LTOK], BF16)
            NPT = 16
            PT = [X4.view(1024 * i, [512], BF16) for i in range(NPT)]
            densb = [X4.view(16384 + 2048 * i, [512], F32) for i in range(2)]
            rec = [X4.view(20480 + 2048 * i, [512], F32) for i in range(2)]
            CSL = [X4.view(24576 + 11764 * i, [NTILE, 2, KCW], BF16) for i in range(2)]
            bufs["CSL"] = CSL
            for kc in range(2):
                P.add("sync", lambda e, kc=kc: e.dma_start(out=CSL[kc], in_=csl_d[kc]), writes=[("CSL", kc)], dma=True)
            pt_ctr = [0]
            blocks = list(range(-1, 16))

            def s_stage(bi):
                if bi < 0:
                    q0, nq = 0, 16
                    ktiles = [(0, None), (1, 2)]
                else:
                    q0, nq = 16 + 128 * bi, 128
                    ktiles = [(0, None)]
                    if bi >= 1:
                        ktiles.append((bi, 0))
                    ktiles.append((bi + 1, None))
                    if bi <= 14:
                        ktiles.append((bi + 2, 1))
                jq = bi + 1
                gq = TILE_GROUP[jq]
                N = 4 * nq
                plist = []
                for (jt, mk) in ktiles:
                    for kv in range(2):
                        r0 = 64 * kv
                        kt0, nk = TILES[jt]
                        a_i, a_h = next_hb()
                        ps = PQ[a_i][0:nk, a_h * 512:a_h * 512 + N]
                        ps3 = ps.rearrange("p (c q) -> p c q", c=4)
                        mm_chain(ps3, [(kT[r0:r0 + 64, kt0:kt0 + nk], qT[r0:r0 + 64, :, q0:q0 + nq])],
                                 reads=[("kT", TILE_GROUP[jt])] + [("qT", c, gq) for c in range(4)],
                                 writes=psk(a_i, a_h))
                        pi = pt_ctr[0] % NPT
                        pt_ctr[0] += 1
                        pt = PT[pi][0:nk, 0:N]
                        P.add("scalar", lambda e, pt=pt, ps=ps: e.activation(out=pt, in_=ps, func=AF.Exp, scale=0.125),
                              reads=psk(a_i, a_h), writes=[("PT", pi)])
                        if mk is not None:
                            pt3 = pt.rearrange("p (c q) -> p c q", c=4)
                            mk_ap = masks[0:nk, mk, 0:nq].unsqueeze(1).to_broadcast([nk, 4, nq])
                            P.add("vector", lambda e, pt3=pt3, mk_ap=mk_ap: e.tensor_tensor(out=pt3, in0=pt3, in1=mk_ap,
                                                                                          op=ALU.mult),
                                  reads=[("PT", pi), ("masks",)], writes=[("PT", pi)])
                        plist.append((kv, jt, nk, pi, pt))
                return (bi, q0, nq, N, plist)

            def pv_stage(info, n):
                bi, q0, nq, N, plist = info
                h = n % 2
                po = PQ[2][:, h * 512:h * 512 + N]
                pd = PQ[3][:, h * 512:h * 512 + N]
                rk = [("PT", pi) for (_, _, _, pi, _) in plist]
                mm_chain(po, [(VP[0:nk, jt, kv, :], pt) for (kv, jt, nk, pi, pt) in plist],
                         reads=rk + [("VP", jt) for (_, jt, _, _, _) in plist], writes=psk(2, h))
                mm_chain(pd, [(onesp[0:nk, kv, :], pt) for (kv, jt, nk, pi, pt) in plist],
                         reads=rk + [("onesp",)], writes=psk(3, h))
                d3 = densb[h][:, 0:N].rearrange("p (c q) -> p c q", c=4)
                r3 = rec[h][:, 0:N].rearrange("p (c q) -> p c q", c=4)
                P.add("vector", lambda e: e.tensor_tensor(out=d3, in0=pd.rearrange("p (c q) -> p c q", c=4),
                                                          in1=sinkexp[:, l, :].unsqueeze(2).to_broadcast([128, 4, nq]),
                                                          op=ALU.add),
                      reads=psk(3, h) + [("sinkexp",)], writes=[("densb", h)])
                P.add("vector", lambda e: e.reciprocal(out=rec[h][:, 0:N], in_=densb[h][:, 0:N]),
                      reads=[("densb", h)], writes=[("rec", h)])
                P.add("vector", lambda e: e.tensor_tensor(out=attnT[:, :, q0:q0 + nq],
                                                          in0=po.rearrange("p (c q) -> p c q", c=4), in1=r3,
                                                          op=ALU.mult),
                      reads=psk(2, h) + [("rec", h)], writes=[("attnT", bi + 1)])

            prev = s_stage(blocks[0])
            for n, bi in enumerate(blocks):
                nxt = s_stage(blocks[n + 1]) if n + 1 < len(blocks) else None
                pv_stage(prev, n)
                prev = nxt
            return attnT

        def phase_C(s, l, bufs):
            AB = bufs["AB"]
            CSL = bufs["CSL"]
            YT = X3.view(16512, [4, LTOK], BF16)
            qsb = [X4.view(1024 * i, [KCW], F32) for i in range(3)]
            ctr = [0]
            for kc in range(NKC):
                cb = kc % 2
                k0 = kc * KCW
                nd = min(KCW, KHALF + 1 - k0)
                ka = max(k0, 1)
                kb = min(k0 + KCW, KHALF)
                nm = kb - ka
                for g in range(4):
                    n = ctr[0]
                    ctr[0] += 1
                    r = n % 4
                    qi = n % 3
                    psP = PQ[r][:, 0:KCW]
                    psQ = PQ[r][:, 512:512 + KCW]
                    pp, pq = [], []
                    for j in range(NTILE):
                        t0, nt = TILES[j]
                        pp.append((AB[0:nt, j, g * 256:g * 256 + 128], CSL[cb][0:nt, j, 0, :]))
                        pq.append((AB[0:nt, j, g * 256 + 128:g * 256 + 256], CSL[cb][0:nt, j, 1, :]))
                    abk = [("AB", j, g // 2) for j in range(NTILE)] + [("CSL", cb)]
                    mm_chain(psP, pp, reads=abk, writes=psk(r, 0))
                    mm_chain(psQ, pq, reads=abk, writes=psk(r, 1))
                    P.add("scalar", lambda e, qi=qi, psQ=psQ: e.activation(out=qsb[qi][:, :], in_=psQ, func=AF.Copy),
                          reads=psk(r, 1), writes=[("PT", qi)])
                    P.add("vector", lambda e, qi=qi, psP=psP, g=g, k0=k0, nd=nd: e.tensor_tensor(
                        out=YT[:, g, k0:k0 + nd], in0=psP[:, 0:nd], in1=qsb[qi][:, 0:nd], op=ALU.add),
                        reads=psk(r, 0) + [("PT", qi)], writes=[("YT", g, kc)])
                    if nm > 0:
                        base = YT[:, g, LTOK - ka:LTOK - ka + 1]
                        rev = bass.AP(base.tensor, base.offset, [list(base.ap[0]), [-1, nm]])
                        o = ka - k0
                        P.add("vector", lambda e, qi=qi, psP=psP, rev=rev, o=o, nm=nm: e.tensor_tensor(
                            out=rev, in0=psP[:, o:o + nm], in1=qsb[qi][:, o:o + nm], op=ALU.subtract),
                            reads=psk(r, 0) + [("PT", qi)], writes=[("YTm", g, kc)])
                if kc + 2 < NKC:
                    P.add("sync", lambda e, kc=kc, cb=cb: e.dma_start(out=CSL[cb], in_=csl_d[kc + 2]),
                          writes=[("CSL", cb)], dma=True)
            return YT

        def phase_D(s, l, attnT, YT):
            switch("X1")
            switch("X4")
            mixT = X1.view(0, [8, LTOK], BF16)
            Dst = [X1.view(33024 + 6144 * i, [24, 128], BF16) for i in range(2)]
            Wo = X1.view(45312, [8, 1024], BF16)
            sg = [X4.view(4096 * i, [2, 512], F32) for i in range(2)]
            mm = [X4.view(8192 + 4096 * i, [2, 512], F32) for i in range(2)]

            def load(c):
                P.add("gpsimd", lambda e: e.dma_start(out=Dst[c % 2], in_=wdm_d[l, c]), writes=[("Dst", c % 2)],
                      dma=True)

            load(0)
            n = 0
            for c in range(8):
                if c + 1 < 8:
                    load(c + 1)
                if c == 0:
                    P.add("gpsimd", lambda e: e.dma_start(out=Wo, in_=wo_d[l]), writes=[("Wo",)], dma=True)
                W = Dst[c % 2]
                for gi, (g0, N, tl) in enumerate(GROUPS):
                    pa_i = (n % 2) * 2
                    pb_i = pa_i + 1
                    b = n % 2
                    n += 1
                    ukeys = [("uT", j) for j in tl]
                    ykeys = [(nm_, g, kc) for nm_ in ("YT", "YTm") for g in range(4) for kc in range(NKC)]
                    akeys = [("attnT", j) for j in tl]
                    mm_chain(PQ[pa_i][:, 0:N], [(W[:, k, :], uT[:, k, g0:g0 + N]) for k in range(8)],
                             reads=ukeys + [("Dst", c % 2)], writes=psk(pa_i, 0))
                    mm_chain(PQ[pa_i][:, 512:512 + N], [(W[:, 8 + k, :], uT[:, k, g0:g0 + N]) for k in range(8)],
                             reads=ukeys + [("Dst", c % 2)], writes=psk(pa_i, 1))
                    mm_chain(PQ[pb_i][:, 0:N], [(W[:, 16 + k, :], YT[:, k, g0:g0 + N]) for k in range(4)],
                             reads=ykeys + [("Dst", c % 2)], writes=psk(pb_i, 0))
                    mm_chain(PQ[pb_i][:, 512:512 + N], [(W[:, 20 + k, :], attnT[:, k, g0:g0 + N]) for k in range(4)],
                             reads=akeys + [("Dst", c % 2)], writes=psk(pb_i, 1))
                    pa3 = PQ[pa_i][:, :].rearrange("p (a b) -> p a b", a=2)[:, :, 0:N]
                    pb3 = PQ[pb_i][:, :].rearrange("p (a b) -> p a b", a=2)[:, :, 0:N]
                    P.add("scalar", lambda e, b=b, pa3=pa3, N=N: e.activation(out=sg[b][:, :, 0:N], in_=pa3,
                                                                            func=AF.Sigmoid),
                          reads=psk(pa_i), writes=[("sg", b)])
                    P.add("vector", lambda e, b=b, pb3=pb3, N=N: e.tensor_tensor(out=mm[b][:, :, 0:N],
                                                                               in0=sg[b][:, :, 0:N], in1=pb3,
                                                                               op=ALU.mult),
                          reads=psk(pb_i) + [("sg", b)], writes=[("mm", b)])
                    P.add("vector", lambda e, b=b, c=c, g0=g0, N=N: e.tensor_tensor(out=mixT[:, c, g0:g0 + N],
                                                                                  in0=mm[b][:, 0, 0:N],
                                                                                  in1=mm[b][:, 1, 0:N], op=ALU.add),
                          reads=[("mm", b)], writes=[("mixT", c, gi)])
            return mixT, Wo

        def phase_E(s, l, mixT, Wo):
            switch("X4")
            hbuf = [X4.view(4096 * i, [1024], F32) for i in range(2)]
            tmp = [X4.view(8192 + 4096 * i, [1024], F32) for i in range(2)]
            hnew = [X4.view(16384 + 4096 * i, [1024], F32) for i in range(2)]
            ubuf = [X4.view(24576 + 2048 * i, [1024], BF16) for i in range(2)]
            junk = X4.view(28672, [1024], BF16)
            gain = [X4.view(30720 + 4096 * i, [1024], F32) for i in range(2)]
            for gi_, idx in ((0, 1), (1, 2)):
                P.add("sync", lambda e, gi_=gi_, idx=idx: e.dma_start(out=gain[gi_],
                                                                      in_=gains_d[l, idx].partition_broadcast(128)),
                      writes=[("gain", gi_)], dma=True)

            def load(j):
                t0, nt = TILES[j]
                src, skeys = h_src(s, l, j)
                P.add("sync", lambda e: e.dma_start(out=hbuf[j % 2][0:nt], in_=src), reads=skeys,
                      writes=[("hbuf", j % 2)], dma=True)

            def mmE(j):
                t0, nt = TILES[j]
                r = j % 3
                gi = TILE_GROUP[j]
                for hh in range(2):
                    mm_chain(PQ[r][0:nt, hh * 512:(hh + 1) * 512],
                             [(mixT[:, k, t0:t0 + nt], Wo[:, k, hh * 512:(hh + 1) * 512]) for k in range(8)],
                             reads=[("mixT", c, gi) for c in range(8)] + [("Wo",)], writes=psk(r, hh))

            def chainA(j):
                t0, nt = TILES[j]
                b = j % 2
                r = j % 3
                rstd, skey = rms_stats(PQ[r][0:nt, :], nt, psk(r), junk[0:nt])
                P.add("vector", lambda e: e.scalar_tensor_tensor(
                    out=tmp[b][0:nt], in0=PQ[r][0:nt, :], scalar=rstd, in1=gain[0][0:nt], op0=ALU.mult, op1=ALU.mult),
                    reads=psk(r) + [skey, ("gain", 0)], writes=[("tmp", b)])
                P.add("vector", lambda e: e.tensor_tensor(out=hnew[b][0:nt], in0=hbuf[b][0:nt],
                                                          in1=tmp[b][0:nt], op=ALU.add),
                      reads=[("hbuf", b), ("tmp", b)], writes=[("hnew", b)])
                P.add("sync", lambda e: e.dma_start(out=hs_d[s, t0:t0 + nt, :], in_=hnew[b][0:nt]),
                      reads=[("hnew", b)], writes=[("hs", s, j)], dma=True)

            def chainB(j):
                t0, nt = TILES[j]
                b = j % 2
                rstd2, skey2 = rms_stats(hnew[b][0:nt], nt, [("hnew", b)], junk[0:nt])
                P.add("vector", lambda e: e.scalar_tensor_tensor(
                    out=ubuf[b][0:nt], in0=hnew[b][0:nt], scalar=rstd2, in1=gain[1][0:nt], op0=ALU.mult,
                    op1=ALU.mult),
                    reads=[("hnew", b), skey2, ("gain", 1)], writes=[("ubuf", b)])

            load(0)
            load(1)
            mmE(0)
            mmE(1)
            chainA(0)
            load(2)
            for i in range(NTILE + 1):
                if i + 2 < NTILE:
                    mmE(i + 2)
                if i + 1 < NTILE:
                    chainA(i + 1)
                if i < NTILE:
                    chainB(i)
                if i + 3 < NTILE:
                    load(i + 3)
                if i >= 1:
                    transposes_to_uT(ubuf[(i - 1) % 2], ("ubuf", (i - 1) % 2), i - 1, (i - 1) % 2)

        def phase_F(s, l, last):
            switch("X1")
            switch("X3")
            switch("X4")
            HT = 1040
            actT = X1.view(0, [NFC, HT], BF16)
            Fst = [X1.view(45760 + 4096 * i, [16, 128], BF16) for i in range(3)]
            Wd = X3.view(0, [NFC, 1024], BF16)
            sg = [X4.view(2048 * i, [512], F32) for i in range(3)]
            hbuf = [X4.view(6144 + 4096 * i, [1024], F32) for i in range(2)]
            tmp = [X4.view(14336 + 4096 * i, [1024], F32) for i in range(2)]
            hnew = [X4.view(22528 + 4096 * i, [1024], F32) for i in range(2)]
            gain = X4.view(30720, [1024], F32)
            junk = X4.view(34816, [1024], BF16)
            P.add("sync", lambda e: e.dma_start(out=gain, in_=gains_d[l, 3].partition_broadcast(128)),
                  writes=[("gain", 0)], dma=True)
            fctr = [0]
            nctr = [0]
            tctr = [0]
            for hf in range(2):
                if hf == 0:
                    h0, tiles, groups = 0, list(range(0, 9)), [0, 1, 2]
                else:
                    h0, tiles, groups = 1040, list(range(9, 17)), [3, 4]

                def load(fc):
                    sl = (fctr[0] + (fc - 0)) % 3
                    P.add("gpsimd", lambda e: e.dma_start(out=Fst[sl], in_=wgu_d[l, fc]), writes=[("Fst", sl)],
                          dma=True)
                    return sl

                slots = {}
                slots[0] = load(0)
                slots[1] = load(1)
                for fc in range(NFC):
                    if fc + 2 < NFC:
                        slots[fc + 2] = load(fc + 2)
                    if hf == 0 and fc < 11:
                        P.add("gpsimd", lambda e, fc=fc: e.dma_start(out=Wd[:, 2 * fc:2 * fc + 2, :],
                                                                     in_=wdn_d[l, :, 2 * fc:2 * fc + 2, :]),
                              writes=[("Wd", fc)], dma=True)
                    sl = slots[fc]
                    W = Fst[sl]
                    for gi in groups:
                        g0, N, tl = GROUPS[gi]
                        n = nctr[0]
                        nctr[0] += 1
                        r = n % 4
                        sb_i = n % 3
                        ukeys = [("uT", j) for j in tl]
                        mm_chain(PQ[r][:, 0:N], [(W[:, k, :], uT[:, k, g0:g0 + N]) for k in range(8)],
                                 reads=ukeys + [("Fst", sl)], writes=psk(r, 0))
                        mm_chain(PQ[r][:, 512:512 + N], [(W[:, 8 + k, :], uT[:, k, g0:g0 + N]) for k in range(8)],
                                 reads=ukeys + [("Fst", sl)], writes=psk(r, 1))
                        P.add("scalar", lambda e, r=r, sb_i=sb_i, N=N: e.activation(out=sg[sb_i][:, 0:N],
                                                                                  in_=PQ[r][:, 0:N], func=AF.Silu),
                              reads=psk(r, 0), writes=[("sg", sb_i)])
                        P.add("vector", lambda e, r=r, sb_i=sb_i, N=N, fc=fc, g0=g0, h0=h0: e.tensor_tensor(
                            out=actT[:, fc, g0 - h0:g0 - h0 + N], in0=sg[sb_i][:, 0:N], in1=PQ[r][:, 512:512 + N],
                            op=ALU.mult),
                            reads=psk(r, 1) + [("sg", sb_i)], writes=[("actT", fc, gi)])
                fctr[0] += NFC

                def loadh(j):
                    t0, nt = TILES[j]
                    P.add("sync", lambda e: e.dma_start(out=hbuf[j % 2][0:nt], in_=hs_d[s, t0:t0 + nt, :]),
                          reads=[("hs", s, j)], writes=[("hbuf", j % 2)], dma=True)

                loadh(tiles[0])
                for ti, j in enumerate(tiles):
                    t0, nt = TILES[j]
                    b = j % 2
                    if ti + 1 < len(tiles):
                        loadh(tiles[ti + 1])
                    gi = TILE_GROUP[j]
                    r = tctr[0] % 4
                    tctr[0] += 1
                    lo = t0 - h0
                    for hh in range(2):
                        mm_chain(PQ[r][0:nt, hh * 512:(hh + 1) * 512],
                                 [(actT[:, fc, lo:lo + nt], Wd[:, fc, hh * 512:(hh + 1) * 512]) for fc in range(NFC)],
                                 reads=[("actT", fc, gi) for fc in range(NFC)] + [("Wd", i_) for i_ in range(11)],
                                 writes=psk(r, hh))
                    rstd, skey = rms_stats(PQ[r][0:nt, :], nt, psk(r), junk[0:nt])
                    P.add("vector", lambda e, b=b, nt=nt, rstd=rstd, r=r: e.scalar_tensor_tensor(
                        out=tmp[b][0:nt], in0=PQ[r][0:nt, :], scalar=rstd, in1=gain[0:nt], op0=ALU.mult,
                        op1=ALU.mult),
                        reads=psk(r) + [skey, ("gain", 0)], writes=[("tmp", b)])
                    P.add("vector", lambda e, b=b, nt=nt: e.tensor_tensor(out=hnew[b][0:nt], in0=hbuf[b][0:nt],
                                                                         in1=tmp[b][0:nt], op=ALU.add),
                          reads=[("hbuf", b), ("tmp", b)], writes=[("hnew", b)])
                    if last:
                        if j >= 1:
                            P.add("sync", lambda e, b=b, j=j: e.dma_start(out=out_d[s, 128 * (j - 1):128 * j, :],
                                                                          in_=hnew[b][:, :]),
                                  reads=[("hnew", b)], writes=[("out", s, j)], dma=True)
                    else:
                        P.add("sync", lambda e, b=b, nt=nt, t0=t0: e.dma_start(out=hs_d[s, t0:t0 + nt, :],
                                                                              in_=hnew[b][0:nt]),
                              reads=[("hnew", b)], writes=[("hs", s, j)], dma=True)

        def dump(name, ap, shape, dt):
            d = nc.dram_tensor("dbg_" + name, shape, dt, kind="ExternalOutput").ap()
            dbg_outs[name] = d
            P.add("sync", lambda e: e.dma_start(out=d, in_=ap), reads=list(P.last_writer.keys()), dma=True)

        done = False
        for s in range(nseq):
            for l in range(nlayers):
                bufs = phase_A(s, l)
                attnT = phase_B(s, l, bufs)
                YT = phase_C(s, l, bufs)
                if stop == "C":
                    dump("qT", bufs["qT"], [128, 4, LTOK], BF16)
                    dump("kT", bufs["kT"], [128, LTOK], BF16)
                    dump("VP", bufs["VP"], [128, NTILE, 2, 128], BF16)
                    dump("AB", bufs["AB"], [128, NTILE, 1024], BF16)
                    dump("uT", uT, [128, 8, LTOK], BF16)
                    dump("attnT", attnT, [128, 4, LTOK], BF16)
                    dump("YT", YT, [128, 4, LTOK], BF16)
                    done = True
                    break
                mixT, Wo = phase_D(s, l, attnT, YT)
                if stop == "D":
                    dump("mixT", mixT, [128, 8, LTOK], BF16)
                    done = True
                    break
                phase_E(s, l, mixT, Wo)
                if stop == "E":
                    dump("uT", uT, [128, 8, LTOK], BF16)
                    dump("hs", hs_d[s], [LTOK, D_MODEL], F32)
                    done = True
                    break
                phase_F(s, l, last=(l == nlayers - 1) and stop is None)
                if stop == "F":
                    dump("hs", hs_d[s], [LTOK, D_MODEL], F32)
                    done = True
                    break
            if done:
                break
        P.emit()
    return nc


def _q_perm():
    idx = np.empty(512, np.int64)
    for c in range(4):
        for half in range(2):
            for d in range(64):
                idx[c * 128 + half * 64 + d] = (c + 4 * half) * 64 + d
    return idx


def _swap64(n):
    idx = np.arange(n)
    return (idx // 64) * 64 + ((idx % 64) + 32) % 64


def _constants():
    bf = ml_dtypes.bfloat16
    c = {}
    inv_freq = (10000.0 ** (-(np.arange(0, 64, 2, dtype=np.float32)) / np.float32(64))).astype(np.float32)
    ang = (np.arange(LTOK, dtype=np.float32)[:, None] * inv_freq[None, :]).astype(np.float32)
    cos = np.cos(ang.astype(np.float64))
    sin = np.sin(ang.astype(np.float64))
    p = np.arange(128)
    d = p % 64
    fi = d % 32
    sign = np.where(d < 32, -1.0, 1.0)
    c["rcos"] = np.ascontiguousarray(cos[:, fi].T).astype(np.float32)
    c["rsin"] = np.ascontiguousarray((sin[:, fi] * sign[None, :]).T).astype(np.float32)
    cc = np.arange(128)
    a128 = 2 * np.pi * ((cc[:, None] * cc[None, :]) % 128) / 128.0
    c["cs128"] = np.concatenate([np.cos(a128), -np.sin(a128)], axis=1).astype(np.float64) / np.sqrt(128.0)
    c["cs128"] = c["cs128"].astype(bf)
    n_of = np.zeros((128, NTILE), np.int64)
    valid = np.zeros((128, NTILE), bool)
    for j, (t0, nt) in enumerate(TILES):
        n_of[:nt, j] = t0 + np.arange(nt)
        valid[:nt, j] = True
    k_all = np.arange(NKC * KCW).reshape(NKC, KCW)
    csl = np.zeros((NKC, 128, NTILE, 2, KCW), np.float32)
    for kc in range(NKC):
        r = (n_of[:, :, None] * k_all[kc][None, None, :]) % LTOK
        a = 2 * np.pi * r / LTOK
        csl[kc, :, :, 0, :] = np.cos(a) / np.sqrt(LTOK) * valid[:, :, None]
        csl[kc, :, :, 1, :] = np.sin(a) / np.sqrt(LTOK) * valid[:, :, None]
    c["csl"] = csl.astype(bf)
    jj = np.arange(128)[:, None]
    ss = np.arange(128)[None, :]
    m = np.zeros((128, 3, 128), np.float32)
    m[:, 0, :] = (jj >= ss)
    m[:, 1, :] = (jj <= ss)
    m[:, 2, :] = (jj <= 112 + ss)
    c["masks"] = m.astype(bf)
    c["ident"] = np.eye(128, dtype=np.float32).astype(bf)
    o = np.zeros((128, 2, 128), np.float32)
    o[:, 0, 0:64] = 1.0
    o[:, 1, 64:128] = 1.0
    c["onesp"] = o.astype(bf)
    return c


def _prep_weights(w_in, w_fourier_out, w_attn_out, w_o, sink_logits, norm_mix_pre, norm_mix_post,
                  norm_ffn_pre, norm_ffn_post, w_ffn_gate, w_ffn_up, w_ffn_down):
    f32 = np.float32
    qp = _q_perm()
    w = {}
    wq = w_in[:, :, 0:512][:, :, qp]
    wk = w_in[:, :, 512:640]
    wv = w_in[:, :, 640:768]
    wf = w_in[:, :, 768:1280]
    main = np.concatenate([wq, wk, wf, wv], axis=2)
    swp = np.concatenate([wq[:, :, _swap64(512)], wk[:, :, _swap64(128)]], axis=2)
    w["wa"] = np.ascontiguousarray(main.reshape(DEPTH, 8, 128, 1280).transpose(0, 2, 1, 3), dtype=f32)
    w["was"] = np.ascontiguousarray(swp.reshape(DEPTH, 8, 128, 640).transpose(0, 2, 1, 3), dtype=f32)
    wgf = w_in[:, :, 1280:2304].reshape(DEPTH, 8, 128, 8, 128)
    wga = w_in[:, :, 2304:3328].reshape(DEPTH, 8, 128, 8, 128)
    wfo = w_fourier_out.reshape(DEPTH, 4, 128, 8, 128)
    rows = np.empty(512, np.int64)
    for kc in range(4):
        for p_ in range(128):
            head = kc if p_ < 64 else 4 + kc
            rows[kc * 128 + p_] = head * 64 + (p_ % 64)
    wao = w_attn_out[:, rows, :].reshape(DEPTH, 4, 128, 8, 128)
    pack = np.concatenate([wgf, wga, wfo, wao], axis=1)
    w["wdm"] = np.ascontiguousarray(pack.transpose(0, 3, 2, 1, 4), dtype=f32)
    w["wo"] = np.ascontiguousarray(w_o.reshape(DEPTH, 8, 128, 1024).transpose(0, 2, 1, 3), dtype=f32)
    g_ = w_ffn_gate.reshape(DEPTH, 8, 128, NFC, 128)
    u_ = w_ffn_up.reshape(DEPTH, 8, 128, NFC, 128)
    gu = np.concatenate([g_, u_], axis=1)
    w["wgu"] = np.ascontiguousarray(gu.transpose(0, 3, 2, 1, 4), dtype=f32)
    w["wdn"] = np.ascontiguousarray(w_ffn_down.reshape(DEPTH, NFC, 128, 1024).transpose(0, 2, 1, 3), dtype=f32)
    w["gains"] = np.ascontiguousarray(np.stack([norm_mix_pre, norm_mix_post, norm_ffn_pre, norm_ffn_post], axis=1),
                                      dtype=f32)
    sk = np.empty((DEPTH, 128, 4), f32)
    sk[:, 0:64, :] = sink_logits[:, None, 0:4]
    sk[:, 64:128, :] = sink_logits[:, None, 4:8]
    w["sink"] = sk
    return w


_NC_CACHE = {}


def kernel(x, meta_tokens, w_in, w_fourier_out, w_attn_out, w_o, sink_logits, norm_mix_pre, norm_mix_post,
           norm_ffn_pre, norm_ffn_post, w_ffn_gate, w_ffn_up, w_ffn_down):
    args = [np.asarray(a, dtype=np.float32) for a in (w_in, w_fourier_out, w_attn_out, w_o, sink_logits,
                                                      norm_mix_pre, norm_mix_post, norm_ffn_pre, norm_ffn_post,
                                                      w_ffn_gate, w_ffn_up, w_ffn_down)]
    x = np.asarray(x, dtype=np.float32)
    shared = _prep_weights(*args)
    shared.update(_constants())
    shared["meta"] = np.ascontiguousarray(np.asarray(meta_tokens, dtype=np.float32))
    if "nc" not in _NC_CACHE:
        _NC_CACHE["nc"] = build_program()
    nc = _NC_CACHE["nc"]
    in_maps = []
    for c in range(8):
        m = dict(shared)
        m["x"] = np.ascontiguousarray(x[2 * c:2 * c + 2])
        in_maps.append(m)
    res = run_bass_kernel_spmd(nc, in_maps, core_ids=list(range(8)))
    out = np.concatenate([np.asarray(r["out"]) for r in res.results], axis=0)
    return out.astype(np.float32)
```
